# Optimizing a Trainium2 kernel written in Bass

```python
import math
import jax, jax.numpy as jnp
from jax import lax
import numpy as np

D_MODEL = 2048
BATCH = 8
SEQ = 2048
DEPTH = 1
DEC_BATCH = 128
DEC_SEQ = 4
PAST_LEN = 2048
PAGE_SIZE = 128

D_RNN = D_MODEL // 2
RNN_BLOCKS = 8
RNN_BLK = D_RNN // RNN_BLOCKS
RNN_CONV = 4
LRU_C = 8.0
HEAD_DIM = 128
D_ATT = D_MODEL // 2
N_HEADS = D_ATT // HEAD_DIM
N_KV = 2
HPG = N_HEADS // N_KV
KV_W = 2 * N_KV * HEAD_DIM
CMP_BLOCK = 32
CMP_STRIDE = 16
SLC_BLOCK = 64
TOPK = 8
WINDOW = 512
WIN_Q_BLOCK = 128
D_FF = 3 * D_MODEL
FFN_CONV = 3
D_IN = 2 * D_RNN + D_ATT + 3 * KV_W + 3 * N_HEADS + 2 * D_MODEL
RMS_EPS = 1e-6
NEG = -1e30
FORCE = 1e9

kernel_name = 'hawk_nsa_convffn_step'


def rmsnorm(x, g):
    x32 = x.astype(jnp.float32)
    y = x32 * lax.rsqrt(jnp.mean(x32 * x32, axis=-1, keepdims=True) + RMS_EPS)
    return (y * g.astype(jnp.float32)).astype(x.dtype)


def alibi_slopes():
    h = jnp.arange(1, N_HEADS + 1, dtype=jnp.float32)
    return jnp.exp2(-8.0 * h / N_HEADS).reshape(N_KV, HPG)


def masked_softmax(s, mask):
    s = jnp.where(mask, s.astype(jnp.float32), NEG)
    e = jnp.where(mask, jnp.exp(s - jnp.max(s, axis=-1, keepdims=True)), 0.0)
    return e / jnp.maximum(jnp.sum(e, axis=-1, keepdims=True), 1e-30)


def causal_dwconv(x, prev, w, b):
    k = w.shape[0]
    t = x.shape[1]
    xcat = jnp.concatenate([prev.astype(x.dtype), x], axis=1)
    y = b + sum(w[j] * xcat[:, j:j + t] for j in range(k))
    return y, xcat[:, xcat.shape[1] - (k - 1):]


def split_in(z):
    widths = (D_RNN, D_RNN, D_ATT, KV_W, KV_W, KV_W, 3 * N_HEADS, 2 * D_MODEL)
    cuts = [int(c) for c in np.cumsum(widths)[:-1]]
    return jnp.split(z, cuts, axis=-1)


def mixer_inputs(x, norm_g, w_in):
    B, T, _ = x.shape
    h = rmsnorm(x, norm_g)
    xr, gr, q, kvc, kvs, kvw, g_nsa, g_m = split_in(h @ w_in)
    kv_shape = (B, T, 2, N_KV, HEAD_DIM)
    return (xr, gr, q.reshape(B, T, N_KV, HPG, HEAD_DIM), kvc.reshape(kv_shape),
            kvs.reshape(kv_shape), kvw.reshape(kv_shape), g_nsa, g_m)


def rglru(xr, gr, conv_prev, h0, conv_w, conv_b, wa, ba, wx, bx, lam):
    B, T, _ = xr.shape
    xc, new_conv = causal_dwconv(xr, conv_prev, conv_w, conv_b)
    xb = xc.reshape(B, T, RNN_BLOCKS, RNN_BLK)
    r = jax.nn.sigmoid(jnp.einsum('btnc,ncd->btnd', xb, wa).reshape(B, T, D_RNN) + ba)
    i = jax.nn.sigmoid(jnp.einsum('btnc,ncd->btnd', xb, wx).reshape(B, T, D_RNN) + bx)
    log_a = -LRU_C * r.astype(jnp.float32) * jax.nn.softplus(-lam.astype(jnp.float32))
    a = jnp.exp(log_a)
    u = jnp.sqrt(-jnp.expm1(2.0 * log_a)) * (i * xc).astype(jnp.float32)

    def step(h, au):
        h = au[0] * h + au[1]
        return h, h

    h_last, hs = lax.scan(step, h0.astype(jnp.float32), (jnp.swapaxes(a, 0, 1), jnp.swapaxes(u, 0, 1)))
    y = jnp.swapaxes(hs, 0, 1).astype(xr.dtype) * jax.nn.gelu(gr)
    return y, h_last.astype(xr.dtype), new_conv


def chunk_proj(rows, cmp_w):
    B, L = rows.shape[:2]
    ch = rows.reshape(B, L // CMP_STRIDE, CMP_STRIDE, 2, N_KV, HEAD_DIM)
    first = jnp.einsum('bnlcgd,clde->bncge', ch, cmp_w[:, :CMP_STRIDE])
    second = jnp.einsum('bnlcgd,clde->bncge', ch, cmp_w[:, CMP_STRIDE:])
    return first, second


def compress_blocks(first, second, cmp_w, cmp_pos):
    pos_bias = jnp.einsum('cld,clde->ce', cmp_pos, cmp_w)
    return first[:, :-1] + second[:, 1:] + pos_bias[:, None, :]


def compressed_attention(q, qpos, kc, slopes):
    n = kc.shape[1]
    bend = CMP_STRIDE * jnp.arange(n) + (CMP_BLOCK - 1)
    dist = (qpos[:, None] - bend[None, :]).astype(jnp.float32)
    s = jnp.einsum('btgpd,bngd->bgptn', q, kc[:, :, 0]).astype(jnp.float32) / math.sqrt(HEAD_DIM)
    p = masked_softmax(s - slopes[:, :, None, None] * dist, dist >= 0)
    o = jnp.einsum('bgptn,bngd->btgpd', p.astype(q.dtype), kc[:, :, 1])
    return o, p


def cmp_to_slc(n_cmp, n_blk):
    start = CMP_STRIDE * jnp.arange(n_cmp)[:, None]
    b0 = SLC_BLOCK * jnp.arange(n_blk)[None, :]
    return ((start < b0 + SLC_BLOCK) & (start + CMP_BLOCK > b0)).astype(jnp.float32)


def select_blocks(p_cmp, qpos, n_blk):
    imp = jnp.einsum('bgptn,nk->bgtk', p_cmp, cmp_to_slc(p_cmp.shape[-1], n_blk))
    blk = jnp.arange(n_blk)
    cur = (qpos // SLC_BLOCK)[:, None]
    cand = blk[None, :] < cur
    score = jnp.where(cand, jnp.where(blk == 0, FORCE, imp), NEG)
    _, idx = lax.top_k(score, min(TOPK - 1, n_blk))
    return idx, idx < cur


def selected_attention(q, qpos, kv_sel, idx, valid, kv_loc, kpos_loc, slopes):
    B, T = q.shape[:2]
    k1 = idx.shape[-1]
    kpos = idx[..., None] * SLC_BLOCK + jnp.arange(SLC_BLOCK)
    d_sel = (qpos[None, None, :, None, None] - kpos)[:, :, None].astype(jnp.float32)
    s_sel = jnp.einsum('btgpd,bgtksd->bgptks', q, kv_sel[..., 0, :]).astype(jnp.float32) / math.sqrt(HEAD_DIM)
    s_sel = s_sel - slopes[:, :, None, None, None] * d_sel
    m_sel = jnp.broadcast_to(valid[:, :, None, :, :, None], s_sel.shape)
    d_loc = (qpos[:, None] - kpos_loc[None, :]).astype(jnp.float32)
    s_loc = jnp.einsum('btgpd,bsgd->bgpts', q, kv_loc[:, :, 0]).astype(jnp.float32) / math.sqrt(HEAD_DIM)
    s_loc = s_loc - slopes[:, :, None, None] * d_loc
    m_loc = jnp.broadcast_to(d_loc >= 0, s_loc.shape)
    flat = (B, N_KV, HPG, T, k1 * SLC_BLOCK)
    p = masked_softmax(jnp.concatenate([s_sel.reshape(flat), s_loc], -1),
                       jnp.concatenate([m_sel.reshape(flat), m_loc], -1)).astype(q.dtype)
    p_sel = p[..., :k1 * SLC_BLOCK].reshape(s_sel.shape)
    return (jnp.einsum('bgptks,bgtksd->btgpd', p_sel, kv_sel[..., 1, :])
            + jnp.einsum('bgpts,bsgd->btgpd', p[..., k1 * SLC_BLOCK:], kv_loc[:, :, 1]))


def window_attention(q, qpos, kv, kpos, slopes):
    d = qpos[:, None] - kpos[None, :]
    mask = (d >= 0) & (d <= WINDOW) & (kpos[None, :] >= 0)
    s = jnp.einsum('btgpd,bsgd->bgpts', q, kv[:, :, 0]).astype(jnp.float32) / math.sqrt(HEAD_DIM)
    p = masked_softmax(s - slopes[:, :, None, None] * d.astype(jnp.float32), mask)
    return jnp.einsum('bgpts,bsgd->btgpd', p.astype(q.dtype), kv[:, :, 1])


def nsa_prompt(q, kv_cmp, kv_slc, kv_win, cmp_w, cmp_pos, slopes):
    B, T = q.shape[:2]
    qpos = jnp.arange(T)
    first, second = chunk_proj(kv_cmp, cmp_w)
    kc = compress_blocks(first, second, cmp_w, cmp_pos)
    o_cmp, p_cmp = compressed_attention(q, qpos, kc, slopes)
    nb = T // SLC_BLOCK
    idx, valid = select_blocks(p_cmp, qpos, nb)
    k1 = idx.shape[-1]
    kv_blocks = kv_slc.reshape(B, nb, SLC_BLOCK, 2, N_KV, HEAD_DIM)
    kvb = kv_blocks.transpose(0, 4, 1, 2, 3, 5)
    bi = jnp.arange(B)[:, None, None, None]
    gi = jnp.arange(N_KV)[None, :, None, None]

    def one_block(args):
        i, q_i, idx_i, val_i, loc_i = args
        qpos_i = i * SLC_BLOCK + jnp.arange(SLC_BLOCK)
        kv_sel = kvb[bi, gi, idx_i]
        return selected_attention(q_i, qpos_i, kv_sel, idx_i, val_i, loc_i, qpos_i, slopes)

    q_b = jnp.swapaxes(q.reshape(B, nb, SLC_BLOCK, N_KV, HPG, HEAD_DIM), 0, 1)
    idx_b = idx.reshape(B, N_KV, nb, SLC_BLOCK, k1).transpose(2, 0, 1, 3, 4)
    val_b = valid.reshape(B, N_KV, nb, SLC_BLOCK, k1).transpose(2, 0, 1, 3, 4)
    o_slc = lax.map(one_block, (jnp.arange(nb), q_b, idx_b, val_b, jnp.swapaxes(kv_blocks, 0, 1)))
    o_slc = jnp.swapaxes(o_slc, 0, 1).reshape(B, T, N_KV, HPG, HEAD_DIM)
    nq = T // WIN_Q_BLOCK
    kv_pad = jnp.pad(kv_win, ((0, 0), (WINDOW, 0), (0, 0), (0, 0), (0, 0)))
    band = jnp.arange(nq)[:, None] * WIN_Q_BLOCK + jnp.arange(WIN_Q_BLOCK + WINDOW)[None, :]
    o_win = jax.vmap(window_attention, in_axes=(1, 0, 1, 0, None), out_axes=1)(
        q.reshape(B, nq, WIN_Q_BLOCK, N_KV, HPG, HEAD_DIM), qpos.reshape(nq, WIN_Q_BLOCK),
        kv_pad[:, band], band - WINDOW, slopes)
    return o_cmp, o_slc, o_win.reshape(B, T, N_KV, HPG, HEAD_DIM)


def nsa_sample(q, kv_cmp_new, kv_slc_new, kv_win_new, cache_cmp, cache_slc, cache_win, page_table, cmp_w, cmp_pos, slopes):
    DB, Tq = q.shape[:2]
    page = cache_cmp.shape[1]
    past = page_table.shape[1] * page
    qpos = past + jnp.arange(Tq)
    past_cmp = cache_cmp[page_table].reshape(DB, past, 2, N_KV, HEAD_DIM)
    first, second = chunk_proj(past_cmp, cmp_w)
    n_new = Tq // CMP_STRIDE
    if n_new > 0:
        f2, s2 = chunk_proj(kv_cmp_new[:, :n_new * CMP_STRIDE], cmp_w)
        first = jnp.concatenate([first, f2], axis=1)
        second = jnp.concatenate([second, s2], axis=1)
    kc = compress_blocks(first, second, cmp_w, cmp_pos)
    o_cmp, p_cmp = compressed_attention(q, qpos, kc, slopes)
    idx, valid = select_blocks(p_cmp, qpos, past // SLC_BLOCK)
    bpp = page // SLC_BLOCK
    phys = page_table[jnp.arange(DB)[:, None, None, None], idx // bpp]
    rows = (idx % bpp)[..., None] * SLC_BLOCK + jnp.arange(SLC_BLOCK)
    gi = jnp.arange(N_KV)[None, :, None, None, None]
    kv_sel = cache_slc[phys[..., None], rows, :, gi, :]
    o_slc = selected_attention(q, qpos, kv_sel, idx, valid, kv_slc_new, qpos, slopes)
    wb = cache_win.shape[1]
    kv_cat = jnp.concatenate([cache_win, kv_win_new.astype(cache_win.dtype)], axis=1)
    o_win = window_attention(q, qpos, kv_cat, past - wb + jnp.arange(wb + Tq), slopes)
    return o_cmp, o_slc, o_win, kv_cat[:, Tq:]


def merge_out(y_rnn, o_cmp, o_slc, o_win, g_nsa, g_m, w_proj_rnn, w_proj_att, w_out):
    B, T = y_rnn.shape[:2]
    g = jax.nn.sigmoid(g_nsa).reshape(B, T, 3, N_KV, HPG, 1)
    o = (g[:, :, 0] * o_cmp + g[:, :, 1] * o_slc + g[:, :, 2] * o_win).reshape(B, T, D_ATT)
    m = (jax.nn.sigmoid(g_m[..., :D_MODEL]) * (y_rnn @ w_proj_rnn)
         + jax.nn.sigmoid(g_m[..., D_MODEL:]) * (o @ w_proj_att))
    return m @ w_out


def conv_ffn(h, prev, w_gate, w_up, conv_w, conv_b, w_down):
    u, new_conv = causal_dwconv(h @ w_gate, prev, conv_w, conv_b)
    return (jax.nn.gelu(u) * (h @ w_up)) @ w_down, new_conv


def setup_inputs(seed: int = 0) -> dict:
    key = jax.random.key(seed)
    ks = jax.random.split(key, 32)

    def nrm(k, shape, scale):
        return jax.random.normal(k, shape, jnp.float32) * scale

    n_pages = PAST_LEN // PAGE_SIZE
    n_used = DEC_BATCH * n_pages
    n_pool = (5 * n_used) // 4
    page_table = jax.random.permutation(ks[0], n_pool)[:n_used].reshape(DEC_BATCH, n_pages).astype(jnp.int32)
    wbuf = min(WINDOW, PAST_LEN)
    u = jax.random.uniform(ks[1], (DEPTH, D_RNN), jnp.float32, 0.9, 0.999)
    s = u ** (1.0 / LRU_C)
    return {
        'x_prompt': nrm(ks[2], (BATCH, SEQ, D_MODEL), 1.0),
        'x_sample': nrm(ks[3], (DEC_BATCH, DEC_SEQ, D_MODEL), 1.0),
        'cache_kv_cmp': nrm(ks[4], (DEPTH, n_pool, PAGE_SIZE, 2, N_KV, HEAD_DIM), 1.0),
        'cache_kv_slc': nrm(ks[5], (DEPTH, n_pool, PAGE_SIZE, 2, N_KV, HEAD_DIM), 1.0),
        'cache_kv_win': nrm(ks[6], (DEPTH, DEC_BATCH, wbuf, 2, N_KV, HEAD_DIM), 1.0),
        'state_rnn_h': nrm(ks[7], (DEPTH, DEC_BATCH, D_RNN), 0.5),
        'state_rnn_conv': nrm(ks[8], (DEPTH, DEC_BATCH, RNN_CONV - 1, D_RNN), 1.0),
        'state_ffn_conv': nrm(ks[9], (DEPTH, DEC_BATCH, FFN_CONV - 1, D_FF), 1.0),
        'page_table': page_table,
        'norm_mix': 1.0 + nrm(ks[10], (DEPTH, D_MODEL), 0.02),
        'w_in': nrm(ks[11], (DEPTH, D_MODEL, D_IN), D_MODEL ** -0.5),
        'rnn_conv_w': nrm(ks[12], (DEPTH, RNN_CONV, D_RNN), RNN_CONV ** -0.5),
        'rnn_conv_b': nrm(ks[13], (DEPTH, D_RNN), 0.01),
        'rnn_wa': nrm(ks[14], (DEPTH, RNN_BLOCKS, RNN_BLK, RNN_BLK), RNN_BLK ** -0.5),
        'rnn_ba': nrm(ks[15], (DEPTH, D_RNN), 0.01),
        'rnn_wx': nrm(ks[16], (DEPTH, RNN_BLOCKS, RNN_BLK, RNN_BLK), RNN_BLK ** -0.5),
        'rnn_bx': nrm(ks[17], (DEPTH, D_RNN), 0.01),
        'rnn_lambda': jnp.log(s) - jnp.log1p(-s),
        'cmp_w': nrm(ks[18], (DEPTH, 2, CMP_BLOCK, HEAD_DIM, HEAD_DIM), (CMP_BLOCK * HEAD_DIM) ** -0.5),
        'cmp_pos': nrm(ks[19], (DEPTH, 2, CMP_BLOCK, HEAD_DIM), 0.02),
        'w_proj_rnn': nrm(ks[20], (DEPTH, D_RNN, D_MODEL), D_RNN ** -0.5),
        'w_proj_att': nrm(ks[21], (DEPTH, D_ATT, D_MODEL), D_ATT ** -0.5),
        'w_out': nrm(ks[22], (DEPTH, D_MODEL, D_MODEL), D_MODEL ** -0.5),
        'norm_ffn': 1.0 + nrm(ks[23], (DEPTH, D_MODEL), 0.02),
        'ffn_w_gate': nrm(ks[24], (DEPTH, D_MODEL, D_FF), D_MODEL ** -0.5),
        'ffn_w_up': nrm(ks[25], (DEPTH, D_MODEL, D_FF), D_MODEL ** -0.5),
        'ffn_conv_w': nrm(ks[26], (DEPTH, FFN_CONV, D_FF), FFN_CONV ** -0.5),
        'ffn_conv_b': nrm(ks[27], (DEPTH, D_FF), 0.01),
        'ffn_w_down': nrm(ks[28], (DEPTH, D_FF, D_MODEL), D_FF ** -0.5),
        'norm_final': 1.0 + nrm(ks[29], (D_MODEL,), 0.02),
    }


def reference(x_prompt, x_sample, cache_kv_cmp, cache_kv_slc, cache_kv_win, state_rnn_h, state_rnn_conv,
              state_ffn_conv, page_table, norm_mix, w_in, rnn_conv_w, rnn_conv_b, rnn_wa, rnn_ba, rnn_wx, rnn_bx,
              rnn_lambda, cmp_w, cmp_pos, w_proj_rnn, w_proj_att, w_out, norm_ffn, ffn_w_gate, ffn_w_up,
              ffn_conv_w, ffn_conv_b, ffn_w_down, norm_final):
    slopes = alibi_slopes()
    xp, xs = x_prompt, x_sample
    B, T, _ = xp.shape
    DB = xs.shape[0]
    pst = [[] for _ in range(6)]
    sst = [[] for _ in range(6)]
    for l in range(DEPTH):
        rnn_args = (rnn_conv_w[l], rnn_conv_b[l], rnn_wa[l], rnn_ba[l], rnn_wx[l], rnn_bx[l], rnn_lambda[l])
        ffn_args = (ffn_w_gate[l], ffn_w_up[l], ffn_conv_w[l], ffn_conv_b[l], ffn_w_down[l])
        out_args = (w_proj_rnn[l], w_proj_att[l], w_out[l])
        xr, gr, q, kvc, kvs, kvw, g_nsa, g_m = mixer_inputs(xp, norm_mix[l], w_in[l])
        y_rnn, h_last, rconv = rglru(xr, gr, jnp.zeros((B, RNN_CONV - 1, D_RNN), xp.dtype),
                                     jnp.zeros((B, D_RNN), xp.dtype), *rnn_args)
        o_c, o_s, o_w = nsa_prompt(q, kvc, kvs, kvw, cmp_w[l], cmp_pos[l], slopes)
        xp = xp + merge_out(y_rnn, o_c, o_s, o_w, g_nsa, g_m, *out_args)
        f, fconv = conv_ffn(rmsnorm(xp, norm_ffn[l]), jnp.zeros((B, FFN_CONV - 1, D_FF), xp.dtype), *ffn_args)
        xp = xp + f
        for lst, v in zip(pst, (kvc, kvs, kvw[:, T - min(WINDOW, T):], h_last, rconv, fconv)):
            lst.append(v)
        xr, gr, q, kvc, kvs, kvw, g_nsa, g_m = mixer_inputs(xs, norm_mix[l], w_in[l])
        y_rnn, h_last, rconv = rglru(xr, gr, state_rnn_conv[l], state_rnn_h[l], *rnn_args)
        o_c, o_s, o_w, win_buf = nsa_sample(q, kvc, kvs, kvw, cache_kv_cmp[l], cache_kv_slc[l], cache_kv_win[l],
                                            page_table, cmp_w[l], cmp_pos[l], slopes)
        xs = xs + merge_out(y_rnn, o_c, o_s, o_w, g_nsa, g_m, *out_args)
        f, fconv = conv_ffn(rmsnorm(xs, norm_ffn[l]), state_ffn_conv[l], *ffn_args)
        xs = xs + f
        for lst, v in zip(sst, (kvc, kvs, win_buf, h_last, rconv, fconv)):
            lst.append(v)
    y_prompt = rmsnorm(xp, norm_final)
    y_sample = rmsnorm(xs, norm_final)
    p_kv_cmp, p_kv_slc, p_kv_win, p_rnn_h, p_rnn_conv, p_ffn_conv = [jnp.stack(v) for v in pst]
    s_kv_cmp, s_kv_slc, s_kv_win, s_rnn_h, s_rnn_conv, s_ffn_conv = [jnp.stack(v) for v in sst]
    return (y_prompt, y_sample, p_kv_cmp, p_kv_slc, p_kv_win, p_rnn_h, p_rnn_conv, p_ffn_conv,
            s_kv_cmp, s_kv_slc, s_kv_win, s_rnn_h, s_rnn_conv, s_ffn_conv)
```

```python
import contextlib
import numpy as np
import concourse.bass as bass
import concourse.mybir as mybir
from concourse.bass_utils import run_bass_kernel_spmd

F32 = mybir.dt.float32
BF16 = mybir.dt.bfloat16
I32 = mybir.dt.int32
AF = mybir.ActivationFunctionType
ALU = mybir.AluOpType
AX = mybir.AxisListType

NCORES = 8
D = 2048
T = 2048
NS = 64
NT = T + NS
DIN = 8728
DRNN = 1024
DFF = 6144
TG = [(0, 512), (512, 512), (1024, 512), (1536, 512), (2048, 64)]
TT = [(i * 128, 128) for i in range(16)] + [(2048, 64)]
C_XR, C_GR, C_Q, C_KVC, C_KVS, C_KVW, C_GN, C_GM = 0, 1024, 2048, 3072, 3584, 4096, 4608, 4632

ENGS = ("pe", "act", "dve", "pool", "sp")
BLOCKFN = {"pe": "tensor", "act": "scalar", "dve": "vector", "pool": "gpsimd", "sp": "sync"}
DMA_SLOTS = {"sp": 30, "pool": 24, "act": 4}


class Dep:
    __slots__ = ("w", "r", "rd")

    def __init__(self):
        self.w = None
        self.r = {}
        self.rd = []


class Graph:
    def __init__(self, nc, stack):
        self.nc = nc
        self.ops = []
        self.start = 0
        self.esem = {}
        self.ecnt = {e: 0 for e in ENGS}
        self.last_eng_op = {e: None for e in ENGS}
        for e in ENGS:
            self.esem[e] = stack.enter_context(nc.semaphore("es_" + e))
        self.dsem, self.dtot, self.dlast, self.dnext = {}, {}, {}, {}
        for q, n in DMA_SLOTS.items():
            self.dsem[q] = [stack.enter_context(nc.semaphore("ds_%s%d" % (q, i))) for i in range(n)]
            self.dtot[q] = [0] * n
            self.dlast[q] = [None] * n
            self.dnext[q] = 0
        self.waited = {e: {} for e in ENGS}
        self.pe_fence = None

    def op(self, eng, fn, reads=(), writes=(), dma=False):
        deps = set()
        for d in reads:
            if d.w is not None:
                deps.add(d.w)
        for d in writes:
            if d.w is not None:
                deps.add(d.w)
            deps.update(d.r.values())
            deps.update(d.rd)
        i = len(self.ops)
        slot = None
        if dma:
            k = self.dnext[eng]
            self.dnext[eng] = (k + 1) % len(self.dsem[eng])
            if self.dlast[eng][k] is not None:
                deps.add(self.dlast[eng][k])
            self.dlast[eng][k] = i
            slot = k
        self.ops.append([eng, fn, deps, dma, slot, None])
        if not dma and fn is not None:
            self.last_eng_op[eng] = i
        for d in writes:
            d.w = i
            d.r = {}
            d.rd = []
        for d in reads:
            if d.w == i:
                continue
            if dma:
                d.rd.append(i)
            else:
                d.r[eng] = i
        return i

    def dma(self, q, out, in_, reads=(), writes=(), **kw):
        return self.op(q, lambda e: e.dma_start(out=out, in_=in_, **kw), reads, writes, dma=True)

    def barrier(self):
        a = []
        real = [x for x in self.last_eng_op.values() if x is not None]
        for e in ENGS:
            deps = set()
            if self.last_eng_op[e] is not None:
                deps.add(self.last_eng_op[e])
            if e in self.dlast:
                for x in self.dlast[e]:
                    if x is not None:
                        deps.add(x)
            i = len(self.ops)
            self.ops.append([e, (lambda en: en.nop()), deps, False, None, None])
            a.append(i)
        for k, e in enumerate(ENGS):
            self.last_eng_op[e] = a[k]
        for e in ENGS:
            self.ops.append([e, None, set(a) | set(real), False, None, None])

    def flush(self):
        self.barrier()
        nc, ops, start = self.nc, self.ops, self.start
        n = len(ops)
        needed = set()
        for i in range(start, n):
            for d in ops[i][2]:
                if d >= start and not ops[d][3]:
                    needed.add(d)
        by_eng = {e: [] for e in ENGS}
        for i in range(start, n):
            o = ops[i]
            eng = o[0]
            by_eng[eng].append(i)
            if o[3]:
                k = o[4]
                self.dtot[eng][k] += 16
                o[5] = (self.dsem[eng][k], self.dtot[eng][k])
            elif i in needed and o[1] is not None:
                self.ecnt[eng] += 1
                o[5] = (self.esem[eng], self.ecnt[eng])
        g = self
        with nc.Block() as block:
            for eng in ENGS:
                def body(e, eng=eng):
                    waited = g.waited[eng]
                    for i in by_eng[eng]:
                        o = ops[i]
                        waited_now = False
                        for d in sorted(o[2]):
                            if d < start:
                                continue
                            od = ops[d]
                            if od[0] == "pe" and eng == "pe" and not od[3]:
                                continue
                            sem, val = od[5]
                            key = id(sem)
                            if waited.get(key, 0) >= val:
                                continue
                            e.wait_ge(sem, val)
                            waited[key] = val
                            waited_now = True
                        if o[1] is None:
                            continue
                        if eng == "pe" and waited_now and g.pe_fence is not None:
                            g.pe_fence(e)
                            g.pe_fence(e)
                        ins = o[1](e)
                        if o[3]:
                            ins.then_inc(o[5][0], 16)
                        elif o[5] is not None:
                            if eng == "pe" and g.pe_fence is not None:
                                ins = g.pe_fence(e)
                            ins.then_inc(o[5][0], 1)
                getattr(block, BLOCKFN[eng])(body)
        self.start = n


class Stop(Exception):
    pass


class Tl:
    def __init__(self, t, nd=1):
        self.t = t
        self.d = [Dep() for _ in range(nd)]

    def __getitem__(self, k):
        return self.t[k]


class K:
    def __init__(self, nc, stack):
        self.nc = nc
        self.g = Graph(nc, stack)
        self.uid = 0

    def sb(self, st, shape, dt, nd=1, name=None):
        self.uid += 1
        t = st.enter_context(self.nc.sbuf_tensor("%s_%d" % (name or "sb", self.uid), list(shape), dt))
        return Tl(t, nd)

    def ps(self, st, shape, dt, nd=1):
        self.uid += 1
        t = st.enter_context(self.nc.psum_tensor("ps_%d" % self.uid, list(shape), dt))
        return Tl(t, nd)


def build(flags):
    nc = bass.Bass("TRN2", target_bir_lowering=False)
    ins = {}
    outs = {}

    def din(name, shape, dt=F32):
        if flags.get("lean") and name not in ("xp", "xs", "ident", "norm_mix", "w_in"):
            return None
        ins[name] = nc.dram_tensor(name, list(shape), dt, kind="ExternalInput").ap()
        return ins[name]

    def dout(name, shape, dt=F32):
        if flags.get("noout") and name not in ("o_pkv", "o_skv"):
            return None
        outs[name] = nc.dram_tensor(name, list(shape), dt, kind="ExternalOutput").ap()
        return outs[name]

    def dscr(name, shape, dt):
        if flags.get("noscr") or (flags.get("lean") and name not in ("qT_d", "kcT_d", "ksT_d", "kwT_d")):
            return None
        return nc.dram_tensor(name, list(shape), dt, kind="ExternalOutput").ap()

    xp = din("xp", [T, D])
    xs = din("xs", [NS, D])
    ident = din("ident", [128, 128])
    norm_mix = din("norm_mix", [1, D])
    w_in = din("w_in", [D, DIN])
    rnn_rows = din("rnn_rows", [8, DRNN])
    rnn_wa = din("rnn_wa", [8, 128, 128])
    rnn_wx = din("rnn_wx", [8, 128, 128])
    st_rconv = din("st_rconv", [48, DRNN])
    st_rh = din("st_rh", [16, DRNN])
    w_proj_rnn = din("w_proj_rnn", [DRNN, D])
    w_proj_att = din("w_proj_att", [DRNN, D])
    w_out = din("w_out", [D, D])
    norm_ffn = din("norm_ffn", [1, D])
    w_gate = din("ffn_w_gate", [D, DFF])
    w_up = din("ffn_w_up", [D, DFF])
    ffn_rows = din("ffn_rows", [4, DFF])
    st_fconv = din("st_fconv", [32, DFF])
    w_down = din("ffn_w_down", [DFF, D])
    norm_final = din("norm_final", [1, D])
    cwin = din("cwin", [16, 512, 512])
    cmp_w = din("cmp_w", [2, 32, 128, 128])
    cmp_pos = din("cmp_pos", [2, 32, 128])
    m_c2s = din("m_c2s", [127, 32])
    ccmp = din("ccmp", [(16 if flags.get("smallc") else 2560) * 128, 512])
    cslc = din("cslc", [(16 if flags.get("smallc") else 2560) * 128, 512])
    ptab = din("ptab", [1, 256], I32)

    o_pkv = dout("o_pkv", [3, T, 512])
    o_skv = dout("o_skv", [3, NS, 512])
    o_prh = dout("o_prh", [1, DRNN])
    o_prconv = dout("o_prconv", [3, DRNN])
    o_srh = dout("o_srh", [16, DRNN])
    o_srconv = dout("o_srconv", [16, 3, DRNN])
    o_pfconv = dout("o_pfconv", [2, DFF])
    o_sfconv = dout("o_sfconv", [16, 2, DFF])
    y_p = dout("y_p", [T, D])
    y_s = dout("y_s", [NS, D])
    o_swin = dout("o_swin", [16, 512, 512])

    gm_d = dscr("gm_d", [flags.get("gmn", 32), 128, NT], BF16)
    qT_d = dscr("qT_d", [8, 128, NT], BF16)
    kcT_d = dscr("kcT_d", [4, 128, NT], BF16)
    ksT_d = dscr("ksT_d", [2, 128, NT], BF16)
    kwT_d = dscr("kwT_d", [2, 128, NT], BF16)
    yrT_d = dscr("yrT_d", [8, 128, NT], BF16)
    oT_d = dscr("oT_d", [8, 128, NT], BF16)
    gn_d = dscr("gn_d", [NT, 24], F32)
    x2_d = dscr("x2_d", [NT, D], F32)
    x3_d = dscr("x3_d", [NT, D], F32)
    act_d = dscr("act_d", [48, 128, NT], BF16)

    kh = []

    def sst(tag):
        if flags.get('sstop') == tag:
            kh[0].g.flush()
            raise Stop()

    def chk(n):
        if flags.get('upto', 99) < n:
            kh[0].g.flush()
            raise Stop()

    try:
      with contextlib.ExitStack() as top:
        k = K(nc, top)
        kh.append(k)
        g = k.g
        ident_f = k.sb(top, [128, 128], F32, name="identf")
        ident_b = k.sb(top, [128, 128], BF16, name="identb")
        ones_c = k.sb(top, [128, 1], F32, name="onesc")
        epst = k.sb(top, [128, 1], F32, name="epst")
        g.dma("sp", ident_f[:], ident, writes=ident_f.d)
        g.op("dve", lambda e: e.tensor_copy(out=ident_b[:], in_=ident_f[:]), reads=ident_f.d, writes=ident_b.d)
        g.op("dve", lambda e: e.memset(ones_c[:], 1.0), writes=ones_c.d)
        g.op("dve", lambda e: e.memset(epst[:], 1e-6), writes=epst.d)
        fence_ps = k.ps(top, [128, 512], F32)
        g.pe_fence = lambda e: e.matmul(fence_ps[:, 0:128], lhsT=ident_b[:, :], rhs=ident_b[:, :], start=True, stop=True)

        def cp(eng, out, in_, reads, writes):
            if eng == "act":
                g.op("act", lambda e: e.copy(out=out, in_=in_), reads=reads, writes=writes)
            else:
                g.op(eng, lambda e: e.tensor_copy(out=out, in_=in_), reads=reads, writes=writes)

        def row_stats(sa, x_t, n, ss, rstd, junk, ti):
            g.op("act", lambda e: e.activation(
                out=junk[0:n, :], in_=x_t[0:n, :], func=AF.Square, accum_out=ss[0:n, ti:ti + 1]),
                reads=x_t.d, writes=junk.d + ss.d)
            g.op("act", lambda e: e.activation(
                out=rstd[0:n, ti:ti + 1], in_=ss[0:n, ti:ti + 1], func=AF.Sqrt, scale=1.0 / D, bias=epst[0:n, 0:1]),
                reads=ss.d + epst.d, writes=rstd.d)
            g.op("dve", lambda e: e.reciprocal(
                out=rstd[0:n, ti:ti + 1], in_=rstd[0:n, ti:ti + 1]), reads=rstd.d, writes=rstd.d)

        def norm_phase(hT, src_fn, gain):
            with contextlib.ExitStack() as sa:
                gbc = k.sb(sa, [128, D], F32, name="gbc")
                g.dma("sp", gbc[:], gain.partition_broadcast(128), writes=gbc.d)
                xt = [k.sb(sa, [128, D], F32, name="xt") for _ in range(2)]
                hb = [k.sb(sa, [128, D], BF16, name="hb") for _ in range(2)]
                junk = k.sb(sa, [128, D], BF16, name="junk")
                ss = k.sb(sa, [128, 17], F32, name="ss")
                rstd = k.sb(sa, [128, 17], F32, name="rstd")
                pst = [k.ps(sa, [128, 8, 128], BF16) for _ in range(2)]
                for ti, (t0, n) in enumerate(TT):
                    x_t, h_b = xt[ti % 2], hb[ti % 2]
                    g.dma("sp", x_t[0:n, :], src_fn(ti, t0, n), writes=x_t.d)
                    row_stats(sa, x_t, n, ss, rstd, junk, ti)
                    g.op("dve", lambda e, x_t=x_t, h_b=h_b, n=n, ti=ti: e.scalar_tensor_tensor(
                        out=h_b[0:n, :], in0=x_t[0:n, :], scalar=rstd[0:n, ti:ti + 1], in1=gbc[0:n, :],
                        op0=ALU.mult, op1=ALU.mult), reads=x_t.d + rstd.d + gbc.d, writes=h_b.d)
                    for half in range(2):
                        p = pst[half]
                        for j in range(8):
                            kc = half * 8 + j
                            g.op("pe", lambda e, p=p, j=j, kc=kc, h_b=h_b, n=n: e.transpose(
                                out=p[:, j, 0:n], in_=h_b[0:n, kc * 128:(kc + 1) * 128], identity=ident_b[0:n, 0:n]),
                                reads=h_b.d + ident_b.d, writes=p.d)
                        cp("act" if half == 0 else "dve", hT[:, half * 8:half * 8 + 8, t0:t0 + n], p[:, :, 0:n], p.d, hT.d)
                g.flush()

        with contextlib.ExitStack() as s1:
            hT = k.sb(s1, [128, 16, NT], BF16, name="hT")
            norm_phase(hT, lambda ti, t0, n: (xp[t0:t0 + n, :] if ti < 16 else xs[:, :]), norm_mix)

            with contextlib.ExitStack() as sb_:
                wsl = [k.sb(sb_, [128, 16, 128], BF16, name="wsl") for _ in range(4)]
                wctr = [0]
                psf = [k.ps(sb_, [128, 512], F32) for _ in range(6)]
                pss = k.ps(sb_, [128, 512], F32)
                pctr = [0]
                stg_t = [k.sb(sb_, [128, 4, 128], F32, name="stgt") for _ in range(3)]
                stg_f = [k.sb(sb_, [128, NT], BF16, nd=5, name="stgf") for _ in range(2)]
                tmp64 = [k.sb(sb_, [128, NS], F32, name="tmp64") for _ in range(2)]
                t64c = [0]
                sctr = [0]
                fctr = [0]
                ectr = [0]

                def load_w(wd, c0, ncols=128):
                    w = wsl[wctr[0] % len(wsl)]
                    wctr[0] += 1
                    src = wd[:, c0:c0 + ncols].rearrange("(kc p) n -> p kc n", p=128)
                    g.dma("pool", w[:, :, 0:ncols], src, writes=w.d)
                    return w

                def next_ps():
                    p = psf[pctr[0] % len(psf)]
                    pctr[0] += 1
                    return p

                def alt():
                    ectr[0] += 1
                    return "act" if ectr[0] % 2 else "dve"

                def t_layout(hT_, w, ncols, emit):
                    for q0 in range(0, 17, 4):
                        tiles = TT[q0:q0 + 4]
                        p = next_ps()
                        for qi, (t0, n) in enumerate(tiles):
                            for kc in range(16):
                                g.op("pe", lambda e, p=p, qi=qi, kc=kc, t0=t0, n=n: e.matmul(
                                    p[0:n, qi * 128:qi * 128 + ncols], lhsT=hT_[:, kc, t0:t0 + n], rhs=w[:, kc, 0:ncols],
                                    start=(kc == 0), stop=(kc == 15)), reads=hT_.d + w.d, writes=p.d)
                        emit(p, q0, tiles)

                def f_layout(hT_, w, emit):
                    for gi, (t0, n) in enumerate(TG):
                        p = next_ps()
                        for kc in range(16):
                            g.op("pe", lambda e, p=p, kc=kc, t0=t0, n=n: e.matmul(
                                p[:, 0:n], lhsT=w[:, kc, :], rhs=hT_[:, kc, t0:t0 + n],
                                start=(kc == 0), stop=(kc == 15)), reads=hT_.d + w.d, writes=p.d)
                        emit(p, gi, t0, n)

                def f_to_scratch(hT_, w, dst, func=None, scale=None):
                    sf = stg_f[fctr[0] % 2]
                    fctr[0] += 1

                    def emit(p, gi, t0, n):
                        sd = [sf.d[gi]]
                        if func is not None:
                            if n == NS:
                                tq = tmp64[t64c[0] % 2]
                                t64c[0] += 1
                                cp("dve", tq[:, 0:n], p[:, 0:n], p.d, tq.d)
                                g.op("act", lambda e: e.activation(out=sf[:, t0:t0 + n], in_=tq[:, 0:n], func=func),
                                     reads=tq.d, writes=sd)
                            else:
                                g.op("act", lambda e: e.activation(out=sf[:, t0:t0 + n], in_=p[:, 0:n], func=func),
                                     reads=p.d, writes=sd)
                        elif scale is not None:
                            g.op("dve", lambda e: e.tensor_scalar(out=sf[:, t0:t0 + n], in0=p[:, 0:n], scalar1=scale, scalar2=None, op0=ALU.mult),
                                 reads=p.d, writes=sd)
                        else:
                            cp("dve" if n == NS else alt(), sf[:, t0:t0 + n], p[:, 0:n], p.d, sd)
                    f_layout(hT_, w, emit)
                    g.dma("sp", dst, sf[:], reads=sf.d)

                for br in range(3):
                    for cc in range(4):
                        c0 = C_KVC + br * 512 + cc * 128
                        w = load_w(w_in, c0)

                        def emit(p, q0, tiles, br=br, cc=cc):
                            st_ = stg_t[sctr[0] % len(stg_t)]
                            sctr[0] += 1
                            nq = len(tiles)
                            nrow = tiles[0][1]
                            cp(alt(), st_[0:nrow, 0:nq, :], p[0:nrow, 0:nq * 128].rearrange("p (q c) -> p q c", c=128),
                               p.d, st_.d)
                            if tiles[0][0] < T:
                                t0 = tiles[0][0]
                                dst = o_pkv[br, t0:t0 + nq * 128, cc * 128:(cc + 1) * 128].rearrange("(q p) c -> p q c", p=128)
                                g.dma("sp", dst, st_[:, 0:nq, :], reads=st_.d)
                            else:
                                g.dma("sp", o_skv[br, :, cc * 128:(cc + 1) * 128], st_[0:NS, 0, :], reads=st_.d)
                        t_layout(hT, w, 128, emit)
                        if flags.get('nof2s'):
                            pass
                        elif br == 0:
                            f_to_scratch(hT, w, kcT_d[cc])
                        elif cc < 2:
                            f_to_scratch(hT, w, (ksT_d if br == 1 else kwT_d)[cc])
                chk(0.5)
                for h in range(flags.get('nq', 8)):
                    w = load_w(w_in, C_Q + h * 128)
                    f_to_scratch(hT, w, qT_d[h], scale=float(1.0 / np.sqrt(128.0)))
                chk(0.6)
                if not flags.get('nofl'):
                    g.flush()
                for j in range(flags.get('ngm', 32)):
                    if flags.get('gmz'):
                        w = load_w(w_in, C_Q + j * 128)
                        f_to_scratch(hT, w, qT_d[j], scale=float(1.0 / np.sqrt(128.0)))
                        continue
                    w = load_w(w_in, flags.get('gmoff', C_GM) + j * 128)
                    f_to_scratch(hT, w, (qT_d if flags.get('gmq') else gm_d)[j], func=(None if flags.get('gmcopy') else AF.Sigmoid), scale=(1.0 if flags.get('gms') else None))
                chk(0.7)
                w = load_w(w_in, C_GN, 24)
                gn_st = k.sb(sb_, [128, 4, 24], F32, name="gnst")

                def emit_gn(p, q0, tiles):
                    nq = len(tiles)
                    nrow = tiles[0][1]
                    g.op("act", lambda e: e.activation(
                        out=gn_st[0:nrow, 0:nq, :], in_=p[0:nrow, 0:nq * 128].rearrange("p (q c) -> p q c", c=128)[:, :, 0:24],
                        func=AF.Sigmoid), reads=p.d, writes=gn_st.d)
                    t0 = tiles[0][0]
                    if t0 < T:
                        g.dma("sp", gn_d[t0:t0 + nq * 128, :].rearrange("(q p) c -> p q c", p=128), gn_st[:, 0:nq, :], reads=gn_st.d)
                    else:
                        g.dma("sp", gn_d[T:NT, :], gn_st[0:NS, 0, :], reads=gn_st.d)
                t_layout(hT, w, 24, emit_gn)

                chk(0.8)
                prow = k.sb(sb_, [8, DRNN], F32, name="prow")
                g.dma("sp", prow[:], rnn_rows, writes=prow.d)
                csr = k.sb(sb_, [48, DRNN], F32, name="csr")
                g.dma("sp", csr[:], st_rconv, writes=csr.d)
                h0r = k.sb(sb_, [16, DRNN], F32, name="h0r")
                g.dma("sp", h0r[:], st_rh, writes=h0r.d)
                wa_sb = k.sb(sb_, [128, 8, 128], BF16, name="wa")
                wx_sb = k.sb(sb_, [128, 8, 128], BF16, name="wx")
                g.dma("pool", wa_sb[:], rnn_wa.rearrange("n c d -> c n d"), writes=wa_sb.d)
                g.dma("pool", wx_sb[:], rnn_wx.rearrange("n c d -> c n d"), writes=wx_sb.d)
                rp = k.sb(sb_, [128, 8, 8], F32, name="rp")
                for n_ in range(8):
                    g.op("pe", lambda e, n_=n_: e.transpose(out=pss[:, n_ * 8:n_ * 8 + 8], in_=prow[0:8, n_ * 128:(n_ + 1) * 128],
                                                          identity=ident_f[0:8, 0:8]), reads=prow.d + ident_f.d, writes=pss.d)
                cp("dve", rp[:], pss[:, 0:64].rearrange("p (n j) -> p n j", j=8), pss.d, rp.d)
                c1 = k.sb(sb_, [128, 8], F32, name="c1")
                g.op("act", lambda e: e.activation(out=c1[:], in_=rp[:, :, 7], func=AF.Exp, scale=-1.0), reads=rp.d, writes=c1.d)
                g.op("act", lambda e: e.activation(out=c1[:], in_=c1[:], func=AF.Ln, bias=ones_c[:, 0:1]), reads=c1.d + ones_c.d, writes=c1.d)
                g.op("dve", lambda e: e.tensor_scalar(out=c1[:], in0=c1[:], scalar1=-8.0, scalar2=None, op0=ALU.mult), reads=c1.d, writes=c1.d)

                chk(0.85)
                xpad = k.sb(sb_, [128, 3 + T], F32, nd=5, name="xpad")
                xsm = k.sb(sb_, [128, 16, 7], F32, name="xsm")
                h0T = k.sb(sb_, [128, 16], F32, name="h0T")
                xc = k.sb(sb_, [128, NT], F32, name="xc")
                xcb = k.sb(sb_, [128, NT], BF16, name="xcb")
                rr = k.sb(sb_, [128, NT], F32, name="rr")
                ii = k.sb(sb_, [128, NT], F32, name="ii")
                aa = k.sb(sb_, [128, NT], F32, name="aa")
                hh = k.sb(sb_, [128, NT], F32, name="hh")
                gg = k.sb(sb_, [128, NT], BF16, name="gg")
                ysb = k.sb(sb_, [128, NT], BF16, name="ysb")
                cst = k.sb(sb_, [16, 3, 128], F32, name="cst")
                cst3 = k.sb(sb_, [3, 128], F32, name="cst3")
                hst = k.sb(sb_, [32, 128], F32, name="hst")
                hcol = k.sb(sb_, [128, 128], F32, name="hcol")
                g.op("dve", lambda e: e.memset(hcol[:], 0.0), writes=hcol.d)
                g.op("dve", lambda e: e.memset(xpad[:, 0:3], 0.0), writes=[xpad.d[4]])

                def v3(ap, t):
                    return ap.rearrange("p (s t) -> p s t", t=t)

                for n_ in range(8):
                    wxr = load_w(w_in, C_XR + n_ * 128)
                    wgr = load_w(w_in, C_GR + n_ * 128)
                    g.op("pe", lambda e, n_=n_: e.transpose(out=pss[:, 64:112], in_=csr[0:48, n_ * 128:(n_ + 1) * 128],
                                                          identity=ident_f[0:48, 0:48]), reads=csr.d + ident_f.d, writes=pss.d)
                    g.op("pe", lambda e, n_=n_: e.transpose(out=pss[:, 112:128], in_=h0r[0:16, n_ * 128:(n_ + 1) * 128],
                                                          identity=ident_f[0:16, 0:16]), reads=h0r.d + ident_f.d, writes=pss.d)
                    cp("dve", xsm[:, :, 0:3], v3(pss[:, 64:112], 3), pss.d, xsm.d)
                    cp("dve", h0T[:], pss[:, 112:128], pss.d, h0T.d)

                    def emit_x(p, gi, t0, n):
                        if t0 < T:
                            cp(alt(), xpad[:, 3 + t0:3 + t0 + n], p[:, 0:n], p.d, [xpad.d[gi]])
                        else:
                            cp("dve", xsm[:, :, 3:7], v3(p[:, 0:NS], 4), p.d, xsm.d)
                    f_layout(hT, wxr, emit_x)
                    chk(0.86)
                    p = next_ps()
                    for kc in range(16):
                        g.op("pe", lambda e, p=p, kc=kc, wxr=wxr: e.matmul(p[0:3, 0:128], lhsT=hT[:, kc, T - 3:T], rhs=wxr[:, kc, :],
                                                                  start=(kc == 0), stop=(kc == 15)), reads=hT.d + wxr.d, writes=p.d)
                    for j in range(3):
                        for kc in range(16):
                            g.op("pe", lambda e, p=p, kc=kc, j=j, wxr=wxr: e.matmul(
                                p[0:16, 128 + j * 128:256 + j * 128], lhsT=hT[:, kc, T + 1 + j:NT:4], rhs=wxr[:, kc, :],
                                start=(kc == 0), stop=(kc == 15)), reads=hT.d + wxr.d, writes=p.d)
                    cp("dve", cst3[:], p[0:3, 0:128], p.d, cst3.d)
                    cp("dve", cst[:], p[0:16, 128:512].rearrange("p (j c) -> p j c", c=128), p.d, cst.d)
                    g.dma("sp", o_prconv[:, n_ * 128:(n_ + 1) * 128], cst3[:], reads=cst3.d)
                    g.dma("sp", o_srconv[:, :, n_ * 128:(n_ + 1) * 128], cst[:], reads=cst.d)
                    chk(0.87)
                    def rpc(j):
                        return rp[:, n_, j:j + 1]
                    g.op("dve", lambda e, n_=n_: e.tensor_scalar(out=xc[:, 0:T], in0=xpad[:, 3:3 + T], scalar1=rp[:, n_, 3:4], scalar2=rp[:, n_, 4:5],
                                                               op0=ALU.mult, op1=ALU.add), reads=xpad.d + rp.d, writes=xc.d)
                    g.op("dve", lambda e, n_=n_: e.tensor_scalar(out=v3(xc[:, T:NT], 4), in0=xsm[:, :, 3:7], scalar1=rp[:, n_, 3:4], scalar2=rp[:, n_, 4:5],
                                                               op0=ALU.mult, op1=ALU.add), reads=xsm.d + rp.d, writes=xc.d)
                    for j in range(3):
                        g.op("dve", lambda e, n_=n_, j=j: e.scalar_tensor_tensor(
                            out=xc[:, 0:T], in0=xpad[:, j:j + T], scalar=rp[:, n_, j:j + 1], in1=xc[:, 0:T], op0=ALU.mult, op1=ALU.add),
                            reads=xpad.d + rp.d + xc.d, writes=xc.d)
                        g.op("dve", lambda e, n_=n_, j=j: e.scalar_tensor_tensor(
                            out=v3(xc[:, T:NT], 4), in0=xsm[:, :, j:j + 4], scalar=rp[:, n_, j:j + 1], in1=v3(xc[:, T:NT], 4), op0=ALU.mult, op1=ALU.add),
                            reads=xsm.d + rp.d + xc.d, writes=xc.d)
                    chk(0.88)
                    cp("act", xcb[:], xc[:], xc.d, xcb.d)
                    for (t0, n) in TG:
                        p1 = next_ps()
                        g.op("pe", lambda e, p1=p1, t0=t0, n=n, n_=n_: e.matmul(p1[:, 0:n], lhsT=wa_sb[:, n_, :], rhs=xcb[:, t0:t0 + n], start=True, stop=True),
                             reads=wa_sb.d + xcb.d, writes=p1.d)
                        src1 = p1
                        if n == NS:
                            src1 = tmp64[t64c[0] % 2]
                            t64c[0] += 1
                            cp("dve", src1[:, 0:n], p1[:, 0:n], p1.d, src1.d)
                        g.op("act", lambda e, src1=src1, t0=t0, n=n, n_=n_: e.activation(out=rr[:, t0:t0 + n], in_=src1[:, 0:n], func=AF.Sigmoid, bias=rp[:, n_, 5:6]),
                             reads=src1.d + rp.d, writes=rr.d)
                        p2 = next_ps()
                        g.op("pe", lambda e, p2=p2, t0=t0, n=n, n_=n_: e.matmul(p2[:, 0:n], lhsT=wx_sb[:, n_, :], rhs=xcb[:, t0:t0 + n], start=True, stop=True),
                             reads=wx_sb.d + xcb.d, writes=p2.d)
                        src2 = p2
                        if n == NS:
                            src2 = tmp64[t64c[0] % 2]
                            t64c[0] += 1
                            cp("dve", src2[:, 0:n], p2[:, 0:n], p2.d, src2.d)
                        g.op("act", lambda e, src2=src2, t0=t0, n=n, n_=n_: e.activation(out=ii[:, t0:t0 + n], in_=src2[:, 0:n], func=AF.Sigmoid, bias=rp[:, n_, 6:7]),
                             reads=src2.d + rp.d, writes=ii.d)
                    chk(0.89)
                    g.op("act", lambda e, n_=n_: e.activation(out=aa[:], in_=rr[:], func=AF.Exp, scale=c1[:, n_:n_ + 1]), reads=rr.d + c1.d, writes=aa.d)
                    g.op("dve", lambda e: e.tensor_tensor(out=rr[:], in0=aa[:], in1=aa[:], op=ALU.mult), reads=aa.d, writes=rr.d)
                    g.op("act", lambda e: e.activation(out=rr[:], in_=rr[:], func=AF.Sqrt, scale=-1.0, bias=ones_c[:, 0:1]), reads=rr.d + ones_c.d, writes=rr.d)
                    g.op("dve", lambda e: e.tensor_tensor(out=ii[:], in0=ii[:], in1=xc[:], op=ALU.mult), reads=ii.d + xc.d, writes=ii.d)
                    g.op("dve", lambda e: e.tensor_tensor(out=rr[:], in0=rr[:], in1=ii[:], op=ALU.mult), reads=ii.d + rr.d, writes=rr.d)
                    chk(0.9)
                    a0 = v3(aa[:, T:NT], 4)[:, :, 0]
                    u0 = v3(rr[:, T:NT], 4)[:, :, 0]
                    g.op("dve", lambda e, a0=a0: e.tensor_tensor(out=h0T[:], in0=h0T[:], in1=a0, op=ALU.mult), reads=aa.d + h0T.d, writes=h0T.d)
                    g.op("dve", lambda e, u0=u0: e.tensor_tensor(out=u0, in0=u0, in1=h0T[:], op=ALU.add), reads=rr.d + h0T.d, writes=rr.d)
                    g.op("dve", lambda e, a0=a0: e.memset(a0, 0.0), reads=aa.d, writes=aa.d)
                    g.op("dve", lambda e: e.tensor_tensor_scan(out=hh[:, 0:T], data0=aa[:, 0:T], data1=rr[:, 0:T], initial=0.0, op0=ALU.mult, op1=ALU.add),
                         reads=aa.d + rr.d, writes=hh.d)
                    g.op("dve", lambda e: e.tensor_tensor_scan(out=hh[:, T:NT], data0=aa[:, T:NT], data1=rr[:, T:NT], initial=0.0, op0=ALU.mult, op1=ALU.add),
                         reads=aa.d + rr.d, writes=hh.d)
                    chk(0.92)
                    g.op("dve", lambda e: e.tensor_copy(out=hcol[:, 0:16], in_=hh[:, T + 3:NT:4]), reads=hh.d, writes=hcol.d)
                    g.op("dve", lambda e: e.tensor_copy(out=hcol[:, 16:17], in_=hh[:, T - 1:T]), reads=hh.d, writes=hcol.d)
                    if not flags.get('noh3'):
                        g.op("pe", lambda e: e.transpose(out=pss[:, 128:256], in_=hcol[:, :], identity=ident_f[:]),
                             reads=hcol.d + ident_f.d, writes=pss.d)
                    if not flags.get('noh4'):
                        cp("dve", hst[:], pss[0:32, 128:256], pss.d, hst.d)
                    if not flags.get('noh1'):
                        g.dma("sp", o_srh[:, n_ * 128:(n_ + 1) * 128], hst[0:16, :], reads=hst.d)
                    if not flags.get('noh2'):
                        g.dma("sp", o_prh[0:1, n_ * 128:(n_ + 1) * 128], hst[16:17, :], reads=hst.d)
                    chk(0.95)
                    def emit_g(p, gi, t0, n):
                        if n == NS:
                            tq = tmp64[t64c[0] % 2]
                            t64c[0] += 1
                            cp("dve", tq[:, 0:n], p[:, 0:n], p.d, tq.d)
                            g.op("act", lambda e: e.activation(out=gg[:, t0:t0 + n], in_=tq[:, 0:n], func=AF.Gelu_apprx_tanh), reads=tq.d, writes=gg.d)
                        else:
                            g.op("act", lambda e: e.activation(out=gg[:, t0:t0 + n], in_=p[:, 0:n], func=AF.Gelu_apprx_tanh), reads=p.d, writes=gg.d)
                    f_layout(hT, wgr, emit_g)
                    g.op("dve", lambda e: e.tensor_tensor(out=ysb[:], in0=hh[:], in1=gg[:], op=ALU.mult), reads=hh.d + gg.d, writes=ysb.d)
                    g.dma("sp", yrT_d[n_], ysb[:], reads=ysb.d)
                g.flush()

        chk(2)
        with contextlib.ExitStack() as s2:
            zt = k.sb(s2, [128, NT], BF16, name="zt")
            g.op("dve", lambda e: e.memset(zt[:], 0.0), writes=zt.d)
            if flags.get("noatt") or flags.get("nosatt"):
                for h in range(8):
                    g.dma("sp", oT_d[h, :, T:NT], zt[:, T:NT], reads=zt.d)
            if flags.get("noatt"):
                for h in range(8):
                    g.dma("sp", oT_d[h, :, 0:T], zt[:, 0:T], reads=zt.d)
            for sq in range(16):
                g.dma("sp", o_swin[sq, 0:508, :], cwin[sq, 4:512, :])
            g.dma("sp", o_swin[:, 508:512, :], o_skv[2].rearrange("(s t) c -> s t c", t=4))
            g.flush()
        if not flags.get("noatt"):
          with contextlib.ExitStack() as s2:
            NEGF = -1.0e30
            regcache = {}

            def negreg(e):
                key = g.start
                if key not in regcache:
                    regcache[key] = e.to_reg(NEGF)
                return regcache[key]
            SL = [float(2.0 ** (-(h + 1))) for h in range(8)]
            qT = k.sb(s2, [128, 8, T], BF16, name="qT")
            ksT = k.sb(s2, [128, 2, T], BF16, name="ksT")
            kwT = k.sb(s2, [128, 2, T], BF16, name="kwT")
            kcT = k.sb(s2, [128, 4, T], BF16, name="kcT")
            vs = k.sb(s2, [128, 16, 2, 128], BF16, name="vs")
            vw = k.sb(s2, [128, 16, 2, 128], BF16, name="vw")
            gn = k.sb(s2, [128, 16, 24], F32, name="gn")
            cw = k.sb(s2, [128, 2, 32, 128], BF16, name="cw")
            cpr = k.sb(s2, [32, 2, 128], F32, name="cpr")
            cpT = k.sb(s2, [128, 2, 32], BF16, name="cpT")
            m_sb = k.sb(s2, [128, 32], BF16, name="m_sb")
            m_f = k.sb(s2, [128, 32], F32, name="m_f")
            npos_i = k.sb(s2, [128, T], I32, name="nposi")
            npos = k.sb(s2, [128, T], F32, name="npos")
            cpos = k.sb(s2, [128, 128], F32, name="cpos")
            kcK = k.sb(s2, [128, 2, 128], BF16, name="kcK")
            kcV = k.sb(s2, [128, 2, 128], BF16, name="kcV")
            pbK = k.sb(s2, [128, 1], F32, name="pbK")
            pbV = k.sb(s2, [1, 128], BF16, name="pbV")
            ones_r = k.sb(s2, [1, 128], BF16, name="onesr")
            oT_sb = k.sb(s2, [128, 8, T], BF16, nd=8, name="oTsb")
            g.dma("sp", qT[:], qT_d[:, :, 0:T].rearrange("h p t -> p h t"), writes=qT.d)
            g.dma("sp", ksT[:], ksT_d[:, :, 0:T].rearrange("h p t -> p h t"), writes=ksT.d)
            g.dma("sp", kwT[:], kwT_d[:, :, 0:T].rearrange("h p t -> p h t"), writes=kwT.d)
            g.dma("sp", kcT[:], kcT_d[:, :, 0:T].rearrange("h p t -> p h t"), writes=kcT.d)
            for gg_ in range(2):
                g.dma("pool", vs[:, :, gg_, :], o_pkv[1, :, 256 + gg_ * 128:384 + gg_ * 128].rearrange("(kt p) d -> p kt d", p=128), writes=vs.d)
                g.dma("pool", vw[:, :, gg_, :], o_pkv[2, :, 256 + gg_ * 128:384 + gg_ * 128].rearrange("(kt p) d -> p kt d", p=128), writes=vw.d)
            g.dma("sp", gn[:], gn_d[0:T, :].rearrange("(qt p) c -> p qt c", p=128), writes=gn.d)
            for c_ in range(2):
                for lh in range(2):
                    g.dma("pool", cw[:, c_, lh * 16:(lh + 1) * 16, :], cmp_w[c_, lh * 16:(lh + 1) * 16].rearrange("l d e -> d l e"), writes=cw.d)
            g.dma("sp", cpr[:], cmp_pos.rearrange("c l d -> l c d"), writes=cpr.d)
            g.dma("sp", m_f[0:127, :], m_c2s, writes=m_f.d)
            g.op("dve", lambda e: e.tensor_copy(out=m_sb[0:127, :], in_=m_f[0:127, :]), reads=m_f.d, writes=m_sb.d)
            g.op("pool", lambda e: e.iota(npos_i[:], pattern=[[1, T]], base=0, channel_multiplier=0), writes=npos_i.d)
            g.op("dve", lambda e: e.tensor_copy(out=npos[:], in_=npos_i[:]), reads=npos_i.d, writes=npos.d)
            g.op("dve", lambda e: e.tensor_scalar(out=cpos[:, 0:127], in0=npos[:, 0:127], scalar1=16.0, scalar2=31.0, op0=ALU.mult, op1=ALU.add),
                 reads=npos.d, writes=cpos.d)
            g.op("dve", lambda e: e.memset(ones_r[:], 1.0), writes=ones_r.d)

            psA = k.ps(s2, [128, 512], F32)
            psI = k.ps(s2, [128, 512], F32)
            psO = k.ps(s2, [128, 512], F32)
            psO2 = k.ps(s2, [128, 512], F32)
            psS = [k.ps(s2, [128, 512], F32) for _ in range(2)]
            psT = k.ps(s2, [128, 8, 128], BF16, nd=8)
            tctr = [0]
            sctr2 = [0]

            for c_ in range(2):
                g.op("pe", lambda e, c_=c_: e.transpose(out=psA[:, c_ * 32:c_ * 32 + 32], in_=cpr[0:32, c_, :], identity=ident_f[0:32, 0:32]),
                     reads=cpr.d + ident_f.d, writes=psA.d)
            cp("dve", cpT[:], psA[:, 0:64].rearrange("p (c l) -> p c l", l=32), psA.d, cpT.d)
            for l in range(32):
                g.op("pe", lambda e, l=l: e.matmul(psI[:, 0:1], lhsT=cw[:, 0, l, :], rhs=cpT[:, 0, l:l + 1], start=(l == 0), stop=(l == 31)),
                     reads=cw.d + cpT.d, writes=psI.d)
            cp("dve", pbK[:], psI[:, 0:1], psI.d, pbK.d)
            for l in range(32):
                g.op("pe", lambda e, l=l: e.matmul(psO[0:1, 0:128], lhsT=cpT[:, 1, l:l + 1], rhs=cw[:, 1, l, :], start=(l == 0), stop=(l == 31)),
                     reads=cw.d + cpT.d, writes=psO.d)
            cp("dve", pbV[:], psO[0:1, 0:128], psO.d, pbV.d)
            for gg_ in range(2):
                for l in range(32):
                    off = l if l < 16 else l
                    g.op("pe", lambda e, l=l, gg_=gg_, off=off: e.matmul(psA[:, 0:127], lhsT=cw[:, 0, l, :], rhs=kcT[:, gg_, off:off + 16 * 126 + 1:16],
                                                                      start=(l == 0), stop=(l == 31)), reads=cw.d + kcT.d, writes=psA.d)
                g.op("dve", lambda e, gg_=gg_: e.tensor_scalar(out=kcK[:, gg_, 0:127], in0=psA[:, 0:127], scalar1=pbK[:, 0:1], scalar2=None, op0=ALU.add),
                     reads=psA.d + pbK.d, writes=kcK.d)
                for l in range(32):
                    g.op("pe", lambda e, l=l, gg_=gg_: e.matmul(psO[0:127, 0:128], lhsT=kcT[:, 2 + gg_, l:l + 16 * 126 + 1:16], rhs=cw[:, 1, l, :],
                                                              start=(l == 0), stop=False), reads=cw.d + kcT.d, writes=psO.d)
                g.op("pe", lambda e: e.matmul(psO[0:127, 0:128], lhsT=ones_r[0:1, 0:127], rhs=pbV[0:1, :], start=False, stop=True),
                     reads=ones_r.d + pbV.d, writes=psO.d)
                cp("dve", kcV[0:127, gg_, :], psO[0:127, 0:128], psO.d, kcV.d)

            lmask = k.sb(s2, [128, 128], F32, name="lmask")
            wmask = k.sb(s2, [128, 128], F32, name="wmask")
            g.op("dve", lambda e: e.memset(lmask[:], 0.0), writes=lmask.d)
            g.op("pool", lambda e: e.affine_select(out=lmask[:], in_=lmask[:], pattern=[[-1, 128]], compare_op=ALU.is_ge, fill=negreg(e), base=0, channel_multiplier=1),
                 reads=lmask.d, writes=lmask.d)
            g.op("pe", lambda e: e.transpose(out=psA[:, 0:128], in_=lmask[:, :], identity=ident_f[:, :]), reads=lmask.d + ident_f.d, writes=psA.d)
            cp("dve", wmask[:], psA[:, 0:128], psA.d, wmask.d)
            xs_ = [k.sb(s2, [128, T], F32, name="xs") for _ in range(2)]
            pb_ = [k.sb(s2, [128, T], BF16, name="pb") for _ in range(2)]
            pT_ = [k.sb(s2, [128, 128], BF16, name="pT") for _ in range(4)]
            sm_ = [k.sb(s2, [128, 8], F32, name="sm") for _ in range(4)]
            oc_sb = k.sb(s2, [128, 4, 128], F32, name="ocsb")
            o_bf = k.sb(s2, [128, 4, 128], BF16, name="obf")
            sc = k.sb(s2, [128, 32], F32, name="sc")
            cand = k.sb(s2, [128, 32], F32, name="cand")
            top8 = k.sb(s2, [128, 8], F32, name="top8")
            negb = k.sb(s2, [128, 32], F32, name="negb")
            g.op("dve", lambda e: e.memset(top8[:], 0.0), writes=top8.d)
            selb = k.sb(s2, [128, 32, 1], F32, name="selb")
            xctr = [0]
            ptc = [0]
            smc = [0]

            def softmax_rows(x, nk, gate_ap):
                sm = sm_[smc[0] % 4]
                smc[0] += 1
                pb = pb_[xctr[0] % 2]
                g.op("dve", lambda e: e.reduce_max(out=sm[:, 0:1], in_=x[:, 0:nk], axis=AX.X), reads=x.d, writes=sm.d)
                g.op("dve", lambda e: e.tensor_scalar(out=sm[:, 1:2], in0=sm[:, 0:1], scalar1=-1.0e20, scalar2=-1.0, op0=ALU.max, op1=ALU.mult),
                     reads=sm.d, writes=sm.d)
                g.op("act", lambda e: e.activation(out=pb[:, 0:nk], in_=x[:, 0:nk], func=AF.Exp, bias=sm[:, 1:2], accum_out=sm[:, 2:3]),
                     reads=x.d + sm.d, writes=pb.d + sm.d)
                g.op("dve", lambda e: e.tensor_scalar(out=sm[:, 3:4], in0=sm[:, 2:3], scalar1=1.0e-30, scalar2=None, op0=ALU.max), reads=sm.d, writes=sm.d)
                g.op("dve", lambda e: e.reciprocal(out=sm[:, 3:4], in_=sm[:, 3:4]), reads=sm.d, writes=sm.d)
                if gate_ap is not None:
                    g.op("dve", lambda e: e.tensor_tensor(out=sm[:, 3:4], in0=sm[:, 3:4], in1=gate_ap, op=ALU.mult), reads=sm.d + gn.d, writes=sm.d)
                g.op("dve", lambda e: e.tensor_scalar(out=pb[:, 0:nk], in0=pb[:, 0:nk], scalar1=sm[:, 3:4], scalar2=None, op0=ALU.mult),
                     reads=pb.d + sm.d, writes=pb.d)
                return pb

            def transpose_chunk(pb, c0, nkc):
                slot = tctr[0] % 8
                tctr[0] += 1
                pt = pT_[ptc[0] % 4]
                ptc[0] += 1
                g.op("pe", lambda e: e.transpose(out=psT[0:nkc, slot, :], in_=pb[:, c0:c0 + nkc], identity=ident_b[:, :]),
                     reads=pb.d + ident_b.d, writes=[psT.d[slot]])
                cp("act" if tctr[0] % 2 else "dve", pt[0:nkc, :], psT[0:nkc, slot, :], [psT.d[slot]], pt.d)
                return pt

            def scores(x, lhsT, kT_ap_fn, nk, posap, slope):
                for b0 in range(0, nk, 512):
                    nb = min(512, nk - b0)
                    ps_ = psS[sctr2[0] % 2]
                    sctr2[0] += 1
                    g.op("pe", lambda e, ps_=ps_, b0=b0, nb=nb: e.matmul(ps_[:, 0:nb], lhsT=lhsT, rhs=kT_ap_fn(b0, nb), start=True, stop=True),
                         reads=qT.d + ksT.d + kwT.d + kcK.d, writes=ps_.d)
                    g.op("dve", lambda e, ps_=ps_, b0=b0, nb=nb: e.scalar_tensor_tensor(
                        out=x[:, b0:b0 + nb], in0=posap(b0, nb), scalar=slope, in1=ps_[:, 0:nb], op0=ALU.mult, op1=ALU.add),
                        reads=ps_.d + npos.d + cpos.d, writes=x.d)

            for i in range(flags.get("nqt", 16)):
                q0 = i * 128
                for gg_ in range(2):
                    for p_ in range(4):
                        h = gg_ * 4 + p_
                        x = xs_[xctr[0] % 2]
                        lq = qT[:, h, q0:q0 + 128]
                        scores(x, lq, lambda b0, nb, gg_=gg_: kcK[:, gg_, b0:b0 + nb], 127, lambda b0, nb: cpos[:, b0:b0 + nb], SL[h])
                        g.op("pool", lambda e, x=x, i=i: e.affine_select(out=x[:, 0:127], in_=x[:, 0:127], pattern=[[-16, 127]], compare_op=ALU.is_ge,
                                                                       fill=negreg(e), base=128 * i - 31, channel_multiplier=1), reads=x.d, writes=x.d)
                        pb = softmax_rows(x, 127, None)
                        xctr[0] += 1
                        pt = transpose_chunk(pb, 0, 127)
                        g.op("pe", lambda e, pt=pt, p_=p_, gg_=gg_: e.matmul(psO[:, p_ * 128:(p_ + 1) * 128], lhsT=pt[0:127, :], rhs=kcV[0:127, gg_, :], start=True, stop=True),
                             reads=pt.d + kcV.d, writes=psO.d)
                        g.op("pe", lambda e, pt=pt, p_=p_: e.matmul(psI[:, 0:32], lhsT=pt[0:127, :], rhs=m_sb[0:127, :], start=(p_ == 0), stop=(p_ == 3)),
                             reads=pt.d + m_sb.d, writes=psI.d)
                        g.op("dve", lambda e, p_=p_, h=h, i=i: e.tensor_scalar(out=oc_sb[:, p_, :], in0=psO[:, p_ * 128:(p_ + 1) * 128], scalar1=gn[:, i, h:h + 1], scalar2=None, op0=ALU.mult),
                             reads=psO.d + gn.d, writes=oc_sb.d)
                    cp("dve", sc[:], psI[:, 0:32], psI.d, sc.d)
                    for half in range(2):
                        g.op("dve", lambda e, half=half, i=i: e.tensor_scalar(out=cand[half * 64:half * 64 + 64, :], in0=npos[half * 64:half * 64 + 64, 0:32],
                                                                            scalar1=float(2 * i + half), scalar2=None, op0=ALU.is_lt), reads=npos.d, writes=cand.d)
                    g.op("dve", lambda e: e.tensor_tensor(out=sc[:], in0=sc[:], in1=cand[:], op=ALU.mult), reads=sc.d + cand.d, writes=sc.d)
                    g.op("dve", lambda e: e.tensor_scalar(out=top8[:, 0:1], in0=top8[:, 0:1], scalar1=0.0, scalar2=None, op0=ALU.mult), reads=top8.d, writes=top8.d)
                    g.op("dve", lambda e: e.scalar_tensor_tensor(out=sc[:], in0=cand[:], scalar=-1.0, in1=sc[:], op0=ALU.add, op1=ALU.add) if False else
                         e.tensor_scalar(out=negb[:], in0=cand[:], scalar1=-1.0, scalar2=1.0e30, op0=ALU.add, op1=ALU.mult), reads=cand.d, writes=negb.d)
                    g.op("dve", lambda e: e.tensor_tensor(out=sc[:], in0=sc[:], in1=negb[:], op=ALU.add), reads=sc.d + negb.d, writes=sc.d)
                    if i >= 1:
                        g.op("dve", lambda e: e.memset(sc[0:64, 0:1], 1.0e9), reads=cand.d, writes=sc.d)
                    g.op("dve", lambda e: e.memset(sc[64:128, 0:1], 1.0e9), reads=cand.d, writes=sc.d)
                    g.op("dve", lambda e: e.max(out=top8[:], in_=sc[:]), reads=sc.d, writes=top8.d)
                    g.op("dve", lambda e: e.tensor_scalar(out=sc[:], in0=sc[:], scalar1=top8[:, 6:7], scalar2=None, op0=ALU.is_ge), reads=sc.d + top8.d, writes=sc.d)
                    g.op("dve", lambda e: e.tensor_tensor(out=sc[:], in0=sc[:], in1=cand[:], op=ALU.mult), reads=sc.d + cand.d, writes=sc.d)
                    g.op("dve", lambda e: e.tensor_scalar(out=selb[:, :, 0], in0=sc[:], scalar1=-1.0, scalar2=1.0e30, op0=ALU.add, op1=ALU.mult), reads=sc.d, writes=selb.d)
                    g.op("dve", lambda e, i=i: e.memset(selb[0:64, 2 * i:2 * i + 1, :], 0.0), writes=selb.d)
                    g.op("dve", lambda e, i=i: e.memset(selb[64:128, 2 * i + 1:2 * i + 2, :], 0.0), writes=selb.d)
                    for p_ in range(4):
                        h = gg_ * 4 + p_
                        lq = qT[:, h, q0:q0 + 128]
                        nk = (i + 1) * 128
                        nblk = 2 * (i + 1)
                        x = xs_[xctr[0] % 2]
                        scores(x, lq, lambda b0, nb, gg_=gg_: ksT[:, gg_, b0:b0 + nb], nk, lambda b0, nb: npos[:, b0:b0 + nb], SL[h])
                        g.op("dve", lambda e, x=x, nk=nk, nblk=nblk: e.tensor_tensor(
                            out=x[:, 0:nk].rearrange("p (b s) -> p b s", s=64), in0=x[:, 0:nk].rearrange("p (b s) -> p b s", s=64),
                            in1=selb[:, 0:nblk, :].to_broadcast([128, nblk, 64]), op=ALU.add), reads=x.d + selb.d, writes=x.d)
                        g.op("pool", lambda e, x=x, i=i, nk=nk: e.affine_select(out=x[:, 0:nk], in_=x[:, 0:nk], pattern=[[-1, nk]], compare_op=ALU.is_ge,
                                                                              fill=negreg(e), base=128 * i, channel_multiplier=1), reads=x.d, writes=x.d)
                        pb = softmax_rows(x, nk, gn[:, i, 8 + h:9 + h])
                        xctr[0] += 1
                        for kt in range(i + 1):
                            pt = transpose_chunk(pb, kt * 128, 128)
                            g.op("pe", lambda e, pt=pt, p_=p_, gg_=gg_, kt=kt: e.matmul(psO2[:, p_ * 128:(p_ + 1) * 128], lhsT=pt[:, :], rhs=vs[:, kt, gg_, :],
                                                                                start=(kt == 0), stop=False), reads=pt.d + vs.d, writes=psO2.d)
                        kt0 = max(0, i - 4)
                        k0 = kt0 * 128
                        nkw = (i + 1) * 128 - k0
                        x = xs_[xctr[0] % 2]
                        scores(x, lq, lambda b0, nb, gg_=gg_, k0=k0: kwT[:, gg_, k0 + b0:k0 + b0 + nb], nkw, lambda b0, nb, k0=k0: npos[:, k0 + b0:k0 + b0 + nb], SL[h])
                        g.op("pool", lambda e, x=x, i=i, nkw=nkw, k0=k0: e.affine_select(out=x[:, 0:nkw], in_=x[:, 0:nkw], pattern=[[-1, nkw]], compare_op=ALU.is_ge,
                                                                                       fill=negreg(e), base=128 * i - k0, channel_multiplier=1), reads=x.d, writes=x.d)
                        if i >= 4:
                            g.op("dve", lambda e, x=x: e.tensor_tensor(out=x[:, 0:128], in0=x[:, 0:128], in1=wmask[:, :], op=ALU.add), reads=x.d + wmask.d, writes=x.d)
                        pb = softmax_rows(x, nkw, gn[:, i, 16 + h:17 + h])
                        xctr[0] += 1
                        nkt = i + 1 - kt0
                        for kk in range(nkt):
                            pt = transpose_chunk(pb, kk * 128, 128)
                            g.op("pe", lambda e, pt=pt, p_=p_, gg_=gg_, kk=kk, kt0=kt0, nkt=nkt: e.matmul(
                                psO2[:, p_ * 128:(p_ + 1) * 128], lhsT=pt[:, :], rhs=vw[:, kt0 + kk, gg_, :], start=False, stop=(kk == nkt - 1)),
                                reads=pt.d + vw.d, writes=psO2.d)
                        g.op("dve", lambda e, p_=p_: e.tensor_tensor(out=o_bf[:, p_, :], in0=psO2[:, p_ * 128:(p_ + 1) * 128], in1=oc_sb[:, p_, :], op=ALU.add),
                             reads=psO2.d + oc_sb.d, writes=o_bf.d)
                        slot = tctr[0] % 8
                        tctr[0] += 1
                        g.op("pe", lambda e, p_=p_, slot=slot: e.transpose(out=psT[:, slot, :], in_=o_bf[:, p_, :], identity=ident_b[:, :]),
                             reads=o_bf.d + ident_b.d, writes=[psT.d[slot]])
                        cp("act", oT_sb[:, h, q0:q0 + 128], psT[:, slot, :], [psT.d[slot]], [oT_sb.d[h]])
            for h in range(8):
                g.dma("sp", oT_d[h, :, 0:T], oT_sb[:, h, :], reads=[oT_sb.d[h]])
            g.flush()

        if not flags.get("noatt") and not flags.get("nosatt"):
          with contextlib.ExitStack() as s2:
            NEGF = -1.0e30
            SL = [float(2.0 ** (-(h + 1))) for h in range(8)]
            regcache2 = {}

            def negreg2(e):
                key = g.start
                if key not in regcache2:
                    regcache2[key] = e.to_reg(NEGF)
                return regcache2[key]
            NK = T + 4
            cw = k.sb(s2, [128, 2, 32, 128], BF16, name="cw")
            cpr = k.sb(s2, [32, 2, 128], F32, name="cpr")
            cpT = k.sb(s2, [128, 2, 32], BF16, name="cpT")
            m_sb = k.sb(s2, [128, 32], BF16, name="m_sb")
            m_f = k.sb(s2, [128, 32], F32, name="m_f")
            npos_i = k.sb(s2, [128, NK], I32, name="nposi")
            npos = k.sb(s2, [128, NK], F32, name="npos")
            cpos = k.sb(s2, [128, 128], F32, name="cpos")
            pbK = k.sb(s2, [128, 1], F32, name="pbK")
            pbV = k.sb(s2, [1, 128], BF16, name="pbV")
            ones_r = k.sb(s2, [1, 128], BF16, name="onesr")
            lmask = k.sb(s2, [128, 128], F32, name="lmask")
            wmask = k.sb(s2, [128, 128], F32, name="wmask")
            ptab_i = k.sb(s2, [128, 256], I32, name="ptabi")
            ptab_f = k.sb(s2, [128, 256], F32, name="ptabf")
            pidx_i = k.sb(s2, [128, 1], I32, name="pidxi")
            pidx_f = k.sb(s2, [128, 1], F32, name="pidxf")
            idx_i = k.sb(s2, [128, 256], I32, name="idxi")
            qT_s = k.sb(s2, [128, 8, NS], BF16, name="qTs")
            oT_s = k.sb(s2, [128, 8, NS], BF16, name="oTs")
            cpg = k.sb(s2, [128, 16, 512], BF16, name="cpg")
            spg = k.sb(s2, [128, 16, 512], BF16, name="spg")
            wpg = k.sb(s2, [128, 4, 512], BF16, name="wpg")
            kcT_s = k.sb(s2, [128, 4, T], BF16, name="kcTs")
            ksT_s = k.sb(s2, [128, 2, NK], BF16, name="ksTs")
            kwT_s = k.sb(s2, [128, 2, 516], BF16, name="kwTs")
            vs_n = k.sb(s2, [4, 2, 128], BF16, name="vsn")
            vw_n = k.sb(s2, [4, 2, 128], BF16, name="vwn")
            gn_s = k.sb(s2, [4, 24], F32, name="gns")
            kcK = k.sb(s2, [128, 2, 128], BF16, name="kcK")
            kcV = k.sb(s2, [128, 2, 128], BF16, name="kcV")
            for c_ in range(2):
                for lh in range(2):
                    g.dma("pool", cw[:, c_, lh * 16:(lh + 1) * 16, :], cmp_w[c_, lh * 16:(lh + 1) * 16].rearrange("l d e -> d l e"), writes=cw.d)
            g.dma("sp", cpr[:], cmp_pos.rearrange("c l d -> l c d"), writes=cpr.d)
            g.dma("sp", m_f[0:127, :], m_c2s, writes=m_f.d)
            g.op("dve", lambda e: e.tensor_copy(out=m_sb[0:127, :], in_=m_f[0:127, :]), reads=m_f.d, writes=m_sb.d)
            g.op("pool", lambda e: e.iota(npos_i[:], pattern=[[1, NK]], base=0, channel_multiplier=0), writes=npos_i.d)
            g.op("dve", lambda e: e.tensor_copy(out=npos[:], in_=npos_i[:]), reads=npos_i.d, writes=npos.d)
            g.op("dve", lambda e: e.tensor_scalar(out=cpos[:, 0:127], in0=npos[:, 0:127], scalar1=16.0, scalar2=31.0, op0=ALU.mult, op1=ALU.add),
                 reads=npos.d, writes=cpos.d)
            g.op("dve", lambda e: e.memset(ones_r[:], 1.0), writes=ones_r.d)
            g.dma("sp", qT_s[:], qT_d[:, :, T:NT].rearrange("h p t -> p h t"), writes=qT_s.d)
            g.dma("sp", ptab_i[:], ptab.partition_broadcast(128), writes=ptab_i.d)
            g.op("pool", lambda e: e.iota(pidx_i[:], pattern=[[0, 1]], base=0, channel_multiplier=1), writes=pidx_i.d)
            g.op("dve", lambda e: e.tensor_copy(out=ptab_f[:], in_=ptab_i[:]), reads=ptab_i.d, writes=ptab_f.d)
            g.op("dve", lambda e: e.tensor_copy(out=pidx_f[:], in_=pidx_i[:]), reads=pidx_i.d, writes=pidx_f.d)
            g.op("dve", lambda e: e.tensor_scalar(out=ptab_f[:], in0=ptab_f[:], scalar1=128.0, scalar2=pidx_f[:, 0:1], op0=ALU.mult, op1=ALU.add),
                 reads=ptab_f.d + pidx_f.d, writes=ptab_f.d)
            g.op("dve", lambda e: e.tensor_copy(out=idx_i[:], in_=ptab_f[:]), reads=ptab_f.d, writes=idx_i.d)

            psA = k.ps(s2, [128, 512], F32)
            psI = k.ps(s2, [128, 512], F32)
            psO = k.ps(s2, [128, 512], F32)
            psO2 = k.ps(s2, [128, 512], F32)
            psS = [k.ps(s2, [128, 512], F32) for _ in range(1)]
            psT = k.ps(s2, [128, 8, 128], BF16, nd=8)
            psU = k.ps(s2, [128, 8, 128], BF16)
            tbank = [psT, psU]
            tctr = [0]
            sctr2 = [0]
            g.op("dve", lambda e: e.memset(lmask[:], 0.0), writes=lmask.d)
            g.op("pool", lambda e: e.affine_select(out=lmask[:], in_=lmask[:], pattern=[[-1, 128]], compare_op=ALU.is_ge, fill=negreg2(e), base=0, channel_multiplier=1),
                 reads=lmask.d, writes=lmask.d)
            g.op("pe", lambda e: e.transpose(out=psA[:, 0:128], in_=lmask[:, :], identity=ident_f[:, :]), reads=lmask.d + ident_f.d, writes=psA.d)
            cp("dve", wmask[:], psA[:, 0:128], psA.d, wmask.d)
            for c_ in range(2):
                g.op("pe", lambda e, c_=c_: e.transpose(out=psA[:, c_ * 32:c_ * 32 + 32], in_=cpr[0:32, c_, :], identity=ident_f[0:32, 0:32]),
                     reads=cpr.d + ident_f.d, writes=psA.d)
            cp("dve", cpT[:], psA[:, 0:64].rearrange("p (c l) -> p c l", l=32), psA.d, cpT.d)
            for l in range(32):
                g.op("pe", lambda e, l=l: e.matmul(psI[:, 0:1], lhsT=cw[:, 0, l, :], rhs=cpT[:, 0, l:l + 1], start=(l == 0), stop=(l == 31)),
                     reads=cw.d + cpT.d, writes=psI.d)
            cp("dve", pbK[:], psI[:, 0:1], psI.d, pbK.d)
            for l in range(32):
                g.op("pe", lambda e, l=l: e.matmul(psO[0:1, 0:128], lhsT=cpT[:, 1, l:l + 1], rhs=cw[:, 1, l, :], start=(l == 0), stop=(l == 31)),
                     reads=cw.d + cpT.d, writes=psO.d)
            cp("dve", pbV[:], psO[0:1, 0:128], psO.d, pbV.d)

            xs_ = [k.sb(s2, [4, NK], F32, name="xs") for _ in range(2)]
            pb_ = [k.sb(s2, [4, NK], BF16, name="pb") for _ in range(2)]
            pT_ = [k.sb(s2, [128, 4], BF16, name="pT") for _ in range(4)]
            sm_ = [k.sb(s2, [4, 8], F32, name="sm") for _ in range(4)]
            oc_sb = k.sb(s2, [4, 4, 128], F32, name="ocsb")
            o_bf = k.sb(s2, [4, 4, 128], BF16, name="obf")
            sc = k.sb(s2, [4, 32], F32, name="sc")
            top8 = k.sb(s2, [4, 8], F32, name="top8")
            selb = k.sb(s2, [4, 32, 1], F32, name="selb")
            xctr = [0]
            ptc = [0]
            smc = [0]
            NQ = 4

            def softmax_rows(x, nk, gate_ap):
                sm = sm_[smc[0] % 4]
                smc[0] += 1
                pb = pb_[xctr[0] % 2]
                g.op("dve", lambda e: e.reduce_max(out=sm[:, 0:1], in_=x[:, 0:nk], axis=AX.X), reads=x.d, writes=sm.d)
                g.op("dve", lambda e: e.tensor_scalar(out=sm[:, 1:2], in0=sm[:, 0:1], scalar1=-1.0e20, scalar2=-1.0, op0=ALU.max, op1=ALU.mult),
                     reads=sm.d, writes=sm.d)
                g.op("act", lambda e: e.activation(out=pb[:, 0:nk], in_=x[:, 0:nk], func=AF.Exp, bias=sm[:, 1:2], accum_out=sm[:, 2:3]),
                     reads=x.d + sm.d, writes=pb.d + sm.d)
                g.op("dve", lambda e: e.tensor_scalar(out=sm[:, 3:4], in0=sm[:, 2:3], scalar1=1.0e-30, scalar2=None, op0=ALU.max), reads=sm.d, writes=sm.d)
                g.op("dve", lambda e: e.reciprocal(out=sm[:, 3:4], in_=sm[:, 3:4]), reads=sm.d, writes=sm.d)
                if gate_ap is not None:
                    g.op("dve", lambda e: e.tensor_tensor(out=sm[:, 3:4], in0=sm[:, 3:4], in1=gate_ap, op=ALU.mult), reads=sm.d + gn_s.d, writes=sm.d)
                g.op("dve", lambda e: e.tensor_scalar(out=pb[:, 0:nk], in0=pb[:, 0:nk], scalar1=sm[:, 3:4], scalar2=None, op0=ALU.mult),
                     reads=pb.d + sm.d, writes=pb.d)
                return pb

            def transpose_chunk(pb, c0, nkc):
                slot = tctr[0] % 8
                tctr[0] += 1
                pt = pT_[ptc[0] % 4]
                ptc[0] += 1
                g.op("pe", lambda e: e.transpose(out=psT[0:nkc, slot, 0:NQ], in_=pb[0:NQ, c0:c0 + nkc], identity=ident_b[0:NQ, 0:NQ]),
                     reads=pb.d + ident_b.d, writes=[psT.d[slot]])
                cp("dve", pt[0:nkc, :], psT[0:nkc, slot, 0:NQ], [psT.d[slot]], pt.d)
                return pt

            def scores(x, lhsT, kT_ap_fn, nk, posap, slope, rd):
                for b0 in range(0, nk, 512):
                    nb = min(512, nk - b0)
                    ps_ = psS[0]
                    sctr2[0] += 1
                    g.op("pe", lambda e, ps_=ps_, b0=b0, nb=nb: e.matmul(ps_[0:NQ, 0:nb], lhsT=lhsT, rhs=kT_ap_fn(b0, nb), start=True, stop=True),
                         reads=qT_s.d + rd, writes=ps_.d)
                    g.op("dve", lambda e, ps_=ps_, b0=b0, nb=nb: e.scalar_tensor_tensor(
                        out=x[:, b0:b0 + nb], in0=posap(b0, nb), scalar=slope, in1=ps_[0:NQ, 0:nb], op0=ALU.mult, op1=ALU.add),
                        reads=ps_.d + npos.d + cpos.d, writes=x.d)

            sst('A')
            for sq in range(flags.get("nsq", 16)):
                for j in range(0 if flags.get("nogather") else 16):
                    col = sq * 16 + j
                    g.op("pool", lambda e, j=j, col=col: e.indirect_dma_start(
                        out=cpg[:, j, :], out_offset=None, in_=ccmp[:, :], in_offset=bass.IndirectOffsetOnAxis(ap=idx_i[:, col:col + 1], axis=0)),
                        reads=idx_i.d, writes=cpg.d, dma=True)
                    g.op("pool", lambda e, j=j, col=col: e.indirect_dma_start(
                        out=spg[:, j, :], out_offset=None, in_=cslc[:, :], in_offset=bass.IndirectOffsetOnAxis(ap=idx_i[:, col:col + 1], axis=0)),
                        reads=idx_i.d, writes=spg.d, dma=True)
                g.dma("pool", wpg[:], cwin[sq].rearrange("(kt p) c -> p kt c", p=128), writes=wpg.d)
                g.dma("sp", ksT_s[:, :, T:NK], ksT_d[:, :, T + sq * 4:T + sq * 4 + 4].rearrange("h p t -> p h t"), writes=ksT_s.d)
                g.dma("sp", kwT_s[:, :, 512:516], kwT_d[:, :, T + sq * 4:T + sq * 4 + 4].rearrange("h p t -> p h t"), writes=kwT_s.d)
                g.dma("pool", vs_n[:], o_skv[1, sq * 4:sq * 4 + 4, 256:512].rearrange("t (g d) -> t g d", g=2), writes=vs_n.d)
                g.dma("pool", vw_n[:], o_skv[2, sq * 4:sq * 4 + 4, 256:512].rearrange("t (g d) -> t g d", g=2), writes=vw_n.d)
                g.dma("sp", gn_s[:], gn_d[T + sq * 4:T + sq * 4 + 4, :], writes=gn_s.d)
                sst('B')
                for j in range(flags.get("ntc", 16)):
                    tb = tbank[j % 2]
                    for cgi in range(4):
                        g.op("pe", lambda e, j=j, cgi=cgi, tb=tb: e.transpose(out=tb[:, cgi, :], in_=cpg[:, j, cgi * 128:(cgi + 1) * 128], identity=ident_b[:, :]),
                             reads=cpg.d + ident_b.d, writes=tb.d)
                    cp("dve", kcT_s[:, :, j * 128:(j + 1) * 128], tb[:, 0:4, :], tb.d, kcT_s.d)
                for j in range(flags.get("nts", 16)):
                    tb = tbank[j % 2]
                    for gi2 in range(2):
                        g.op("pe", lambda e, j=j, gi2=gi2, tb=tb: e.transpose(out=tb[:, gi2, :], in_=spg[:, j, gi2 * 128:(gi2 + 1) * 128], identity=ident_b[:, :]),
                             reads=spg.d + ident_b.d, writes=tb.d)
                    cp("dve", ksT_s[:, :, j * 128:(j + 1) * 128], tb[:, 0:2, :], tb.d, ksT_s.d)
                for j in range(flags.get("ntw", 4)):
                    tb = tbank[j % 2]
                    for gi2 in range(2):
                        g.op("pe", lambda e, j=j, gi2=gi2, tb=tb: e.transpose(out=tb[:, gi2, :], in_=wpg[:, j, gi2 * 128:(gi2 + 1) * 128], identity=ident_b[:, :]),
                             reads=wpg.d + ident_b.d, writes=tb.d)
                    cp("dve", kwT_s[:, :, j * 128:(j + 1) * 128], tb[:, 0:2, :], tb.d, kwT_s.d)
                sst('C')
                for gg_ in range(2):
                    for l in range(32):
                        g.op("pe", lambda e, l=l, gg_=gg_: e.matmul(psA[:, 0:127], lhsT=cw[:, 0, l, :], rhs=kcT_s[:, gg_, l:l + 16 * 126 + 1:16],
                                                                  start=(l == 0), stop=(l == 31)), reads=cw.d + kcT_s.d, writes=psA.d)
                    g.op("dve", lambda e, gg_=gg_: e.tensor_scalar(out=kcK[:, gg_, 0:127], in0=psA[:, 0:127], scalar1=pbK[:, 0:1], scalar2=None, op0=ALU.add),
                         reads=psA.d + pbK.d, writes=kcK.d)
                    for l in range(32):
                        g.op("pe", lambda e, l=l, gg_=gg_: e.matmul(psO[0:127, 0:128], lhsT=kcT_s[:, 2 + gg_, l:l + 16 * 126 + 1:16], rhs=cw[:, 1, l, :],
                                                                  start=(l == 0), stop=False), reads=cw.d + kcT_s.d, writes=psO.d)
                    g.op("pe", lambda e: e.matmul(psO[0:127, 0:128], lhsT=ones_r[0:1, 0:127], rhs=pbV[0:1, :], start=False, stop=True),
                         reads=ones_r.d + pbV.d, writes=psO.d)
                    cp("dve", kcV[0:127, gg_, :], psO[0:127, 0:128], psO.d, kcV.d)
                sst('D')
                q0 = sq * 4
                for gg_ in range(2):
                    for p_ in range(4):
                        h = gg_ * 4 + p_
                        x = xs_[xctr[0] % 2]
                        lq = qT_s[:, h, q0:q0 + 4]
                        scores(x, lq, lambda b0, nb, gg_=gg_: kcK[:, gg_, b0:b0 + nb], 127, lambda b0, nb: cpos[0:NQ, b0:b0 + nb], SL[h], kcK.d)
                        pb = softmax_rows(x, 127, None)
                        xctr[0] += 1
                        pt = transpose_chunk(pb, 0, 127)
                        g.op("pe", lambda e, pt=pt, p_=p_, gg_=gg_: e.matmul(psO[0:NQ, p_ * 128:(p_ + 1) * 128], lhsT=pt[0:127, :], rhs=kcV[0:127, gg_, :], start=True, stop=True),
                             reads=pt.d + kcV.d, writes=psO.d)
                        g.op("pe", lambda e, pt=pt, p_=p_: e.matmul(psI[0:NQ, 0:32], lhsT=pt[0:127, :], rhs=m_sb[0:127, :], start=(p_ == 0), stop=(p_ == 3)),
                             reads=pt.d + m_sb.d, writes=psI.d)
                        g.op("dve", lambda e, p_=p_, h=h: e.tensor_scalar(out=oc_sb[:, p_, :], in0=psO[0:NQ, p_ * 128:(p_ + 1) * 128], scalar1=gn_s[:, h:h + 1], scalar2=None, op0=ALU.mult),
                             reads=psO.d + gn_s.d, writes=oc_sb.d)
                    sst('E')
                    cp("dve", sc[:], psI[0:NQ, 0:32], psI.d, sc.d)
                    g.op("dve", lambda e: e.memset(sc[:, 0:1], 1.0e9), writes=sc.d)
                    g.op("dve", lambda e: e.max(out=top8[:], in_=sc[:]), reads=sc.d, writes=top8.d)
                    g.op("dve", lambda e: e.tensor_scalar(out=sc[:], in0=sc[:], scalar1=top8[:, 6:7], scalar2=None, op0=ALU.is_ge), reads=sc.d + top8.d, writes=sc.d)
                    g.op("dve", lambda e: e.tensor_scalar(out=selb[:, :, 0], in0=sc[:], scalar1=-1.0, scalar2=1.0e30, op0=ALU.add, op1=ALU.mult), reads=sc.d, writes=selb.d)
                    for p_ in range(4):
                        h = gg_ * 4 + p_
                        lq = qT_s[:, h, q0:q0 + 4]
                        x = xs_[xctr[0] % 2]
                        scores(x, lq, lambda b0, nb, gg_=gg_: ksT_s[:, gg_, b0:b0 + nb], NK, lambda b0, nb: npos[0:NQ, b0:b0 + nb], SL[h], ksT_s.d)
                        g.op("dve", lambda e, x=x: e.tensor_tensor(
                            out=x[:, 0:T].rearrange("p (b s) -> p b s", s=64), in0=x[:, 0:T].rearrange("p (b s) -> p b s", s=64),
                            in1=selb[:, :, :].to_broadcast([NQ, 32, 64]), op=ALU.add), reads=x.d + selb.d, writes=x.d)
                        g.op("dve", lambda e, x=x: e.tensor_tensor(out=x[:, T:NK], in0=x[:, T:NK], in1=lmask[0:NQ, 0:4], op=ALU.add), reads=x.d + lmask.d, writes=x.d)
                        pb = softmax_rows(x, NK, gn_s[:, 8 + h:9 + h])
                        xctr[0] += 1
                        for kt in range(16):
                            pt = transpose_chunk(pb, kt * 128, 128)
                            g.op("pe", lambda e, pt=pt, p_=p_, gg_=gg_, kt=kt: e.matmul(psO2[0:NQ, p_ * 128:(p_ + 1) * 128], lhsT=pt[:, :], rhs=spg[:, kt, 256 + gg_ * 128:384 + gg_ * 128],
                                                                                start=(kt == 0), stop=False), reads=pt.d + spg.d, writes=psO2.d)
                        pt = transpose_chunk(pb, T, 4)
                        g.op("pe", lambda e, pt=pt, p_=p_, gg_=gg_: e.matmul(psO2[0:NQ, p_ * 128:(p_ + 1) * 128], lhsT=pt[0:4, :], rhs=vs_n[0:4, gg_, :], start=False, stop=False),
                             reads=pt.d + vs_n.d, writes=psO2.d)
                        x = xs_[xctr[0] % 2]
                        scores(x, lq, lambda b0, nb, gg_=gg_: kwT_s[:, gg_, b0:b0 + nb], 516, lambda b0, nb: npos[0:NQ, T - 512 + b0:T - 512 + b0 + nb], SL[h], kwT_s.d)
                        g.op("dve", lambda e, x=x: e.tensor_tensor(out=x[:, 0:4], in0=x[:, 0:4], in1=wmask[0:NQ, 0:4], op=ALU.add), reads=x.d + wmask.d, writes=x.d)
                        g.op("dve", lambda e, x=x: e.tensor_tensor(out=x[:, 512:516], in0=x[:, 512:516], in1=lmask[0:NQ, 0:4], op=ALU.add), reads=x.d + lmask.d, writes=x.d)
                        pb = softmax_rows(x, 516, gn_s[:, 16 + h:17 + h])
                        xctr[0] += 1
                        for kk in range(4):
                            pt = transpose_chunk(pb, kk * 128, 128)
                            g.op("pe", lambda e, pt=pt, p_=p_, gg_=gg_, kk=kk: e.matmul(psO2[0:NQ, p_ * 128:(p_ + 1) * 128], lhsT=pt[:, :], rhs=wpg[:, kk, 256 + gg_ * 128:384 + gg_ * 128],
                                                                                start=False, stop=False), reads=pt.d + wpg.d, writes=psO2.d)
                        pt = transpose_chunk(pb, 512, 4)
                        g.op("pe", lambda e, pt=pt, p_=p_, gg_=gg_: e.matmul(psO2[0:NQ, p_ * 128:(p_ + 1) * 128], lhsT=pt[0:4, :], rhs=vw_n[0:4, gg_, :], start=False, stop=True),
                             reads=pt.d + vw_n.d, writes=psO2.d)
                        g.op("dve", lambda e, p_=p_: e.tensor_tensor(out=o_bf[:, p_, :], in0=psO2[0:NQ, p_ * 128:(p_ + 1) * 128], in1=oc_sb[:, p_, :], op=ALU.add),
                             reads=psO2.d + oc_sb.d, writes=o_bf.d)
                        slot = tctr[0] % 8
                        tctr[0] += 1
                        g.op("pe", lambda e, p_=p_, slot=slot: e.transpose(out=psT[:, slot, 0:NQ], in_=o_bf[:, p_, :], identity=ident_b[0:NQ, 0:NQ]),
                             reads=o_bf.d + ident_b.d, writes=[psT.d[slot]])
                        cp("dve", oT_s[:, h, q0:q0 + 4], psT[:, slot, 0:NQ], [psT.d[slot]], oT_s.d)
            g.dma("sp", oT_d[:, :, T:NT].rearrange("h p t -> p h t"), oT_s[:], reads=oT_s.d)
            g.flush()

        chk(3)
        with contextlib.ExitStack() as s3:
            mT = k.sb(s3, [128, 16, NT], BF16, name="mT")
            with contextlib.ExitStack() as sa:
                yrT = k.sb(sa, [128, 8, NT], BF16, name="yrT")
                oT = k.sb(sa, [128, 8, NT], BF16, name="oT")
                g.dma("sp", yrT[:], yrT_d.rearrange("c p t -> p c t"), writes=yrT.d)
                g.dma("sp", oT[:], oT_d.rearrange("c p t -> p c t"), writes=oT.d)
                wps = [k.sb(sa, [128, 8, 128], BF16, name="wps") for _ in range(4)]
                gms = [k.sb(sa, [128, NT], BF16, name="gms") for _ in range(4)]
                tmp = [k.sb(sa, [128, 512], F32, name="tmp") for _ in range(4)]
                psf = [k.ps(sa, [128, 512], F32) for _ in range(6)]
                pc = 0
                for j in range(16):
                    wr = wps[(2 * j) % 4]
                    wa_ = wps[(2 * j + 1) % 4]
                    g.dma("pool", wr[:], w_proj_rnn[:, j * 128:(j + 1) * 128].rearrange("(kc p) n -> p kc n", p=128), writes=wr.d)
                    g.dma("pool", wa_[:], w_proj_att[:, j * 128:(j + 1) * 128].rearrange("(kc p) n -> p kc n", p=128), writes=wa_.d)
                    gr_ = gms[(2 * j) % 4]
                    ga_ = gms[(2 * j + 1) % 4]
                    g.dma("sp", gr_[:], gm_d[j], writes=gr_.d)
                    g.dma("sp", ga_[:], gm_d[16 + j], writes=ga_.d)
                    for (t0, n) in TG:
                        p1 = psf[pc % 6]
                        p2 = psf[(pc + 1) % 6]
                        tA = tmp[pc % 4]
                        tB = tmp[(pc + 1) % 4]
                        pc += 2
                        for kc in range(8):
                            g.op("pe", lambda e, p1=p1, kc=kc, t0=t0, n=n, wr=wr: e.matmul(
                                p1[:, 0:n], lhsT=wr[:, kc, :], rhs=yrT[:, kc, t0:t0 + n], start=(kc == 0), stop=(kc == 7)),
                                reads=wr.d + yrT.d, writes=p1.d)
                        for kc in range(8):
                            g.op("pe", lambda e, p2=p2, kc=kc, t0=t0, n=n, wa_=wa_: e.matmul(
                                p2[:, 0:n], lhsT=wa_[:, kc, :], rhs=oT[:, kc, t0:t0 + n], start=(kc == 0), stop=(kc == 7)),
                                reads=wa_.d + oT.d, writes=p2.d)
                        g.op("dve", lambda e, p1=p1, tA=tA, t0=t0, n=n, gr_=gr_: e.tensor_tensor(
                            out=tA[:, 0:n], in0=p1[:, 0:n], in1=gr_[:, t0:t0 + n], op=ALU.mult), reads=p1.d + gr_.d, writes=tA.d)
                        g.op("dve", lambda e, p2=p2, tB=tB, t0=t0, n=n, ga_=ga_: e.tensor_tensor(
                            out=tB[:, 0:n], in0=p2[:, 0:n], in1=ga_[:, t0:t0 + n], op=ALU.mult), reads=p2.d + ga_.d, writes=tB.d)
                        g.op("pool", lambda e, tA=tA, tB=tB, t0=t0, n=n, j=j: e.tensor_tensor(
                            out=mT[:, j, t0:t0 + n], in0=tA[:, 0:n], in1=tB[:, 0:n], op=ALU.add), reads=tA.d + tB.d, writes=mT.d)
                g.flush()
            with contextlib.ExitStack() as sb_:
                wos = [k.sb(sb_, [128, 16, 512], BF16, name="wos") for _ in range(2)]
                xr_ = [k.sb(sb_, [128, 512], F32, name="xres") for _ in range(3)]
                so = [k.sb(sb_, [128, 512], F32, name="so") for _ in range(3)]
                psf = [k.ps(sb_, [128, 512], F32) for _ in range(6)]
                pc = 0
                for cg in range(4):
                    wo = wos[cg % 2]
                    g.dma("pool", wo[:], w_out[:, cg * 512:(cg + 1) * 512].rearrange("(kc p) n -> p kc n", p=128), writes=wo.d)
                    for ti, (t0, n) in enumerate(TT):
                        xr = xr_[pc % 3]
                        so_ = so[pc % 3]
                        p = psf[pc % 6]
                        pc += 1
                        src = xp[t0:t0 + n, cg * 512:(cg + 1) * 512] if ti < 16 else xs[:, cg * 512:(cg + 1) * 512]
                        g.dma("sp", xr[0:n, :], src, writes=xr.d)
                        for kc in range(16):
                            g.op("pe", lambda e, p=p, kc=kc, t0=t0, n=n, wo=wo: e.matmul(
                                p[0:n, :], lhsT=mT[:, kc, t0:t0 + n], rhs=wo[:, kc, :], start=(kc == 0), stop=(kc == 15)),
                                reads=mT.d + wo.d, writes=p.d)
                        g.op("dve", lambda e, p=p, xr=xr, so_=so_, n=n: e.tensor_tensor(
                            out=so_[0:n, :], in0=p[0:n, :], in1=xr[0:n, :], op=ALU.add), reads=p.d + xr.d, writes=so_.d)
                        g.dma("sp", x2_d[t0:t0 + n, cg * 512:(cg + 1) * 512], so_[0:n, :], reads=so_.d)
                g.flush()

        chk(4)
        with contextlib.ExitStack() as s4:
            h2T = k.sb(s4, [128, 16, NT], BF16, name="h2T")
            norm_phase(h2T, lambda ti, t0, n: x2_d[t0:t0 + n, :], norm_ffn)
            with contextlib.ExitStack() as sb_:
                wsl = [k.sb(sb_, [128, 16, 128], BF16, name="wsl") for _ in range(4)]
                psf = [k.ps(sb_, [128, 512], F32) for _ in range(6)]
                pss = k.ps(sb_, [128, 512], F32)
                frow = k.sb(sb_, [4, DFF], F32, name="frow")
                g.dma("sp", frow[:], ffn_rows, writes=frow.d)
                fsr = k.sb(sb_, [32, DFF], F32, name="fsr")
                g.dma("sp", fsr[:], st_fconv, writes=fsr.d)
                fp_ = k.sb(sb_, [128, 48, 4], F32, name="fp")
                for c in range(48):
                    g.op("pe", lambda e, c=c: e.transpose(out=pss[:, c * 4:c * 4 + 4], in_=frow[0:4, c * 128:(c + 1) * 128],
                                                        identity=ident_f[0:4, 0:4]), reads=frow.d + ident_f.d, writes=pss.d)
                cp("dve", fp_[:], pss[:, 0:192].rearrange("p (c j) -> p c j", j=4), pss.d, fp_.d)
                gpad = k.sb(sb_, [128, 2 + T], F32, nd=5, name="gpad")
                gsm = k.sb(sb_, [128, 16, 6], F32, name="gsm")
                uc = k.sb(sb_, [128, NT], F32, name="uc")
                ug = k.sb(sb_, [128, NT], BF16, name="ug")
                at = [k.sb(sb_, [128, NT], BF16, name="at") for _ in range(2)]
                fst2 = k.sb(sb_, [2, 128], F32, name="fst2")
                fst = k.sb(sb_, [16, 2, 128], F32, name="fst")
                g.op("dve", lambda e: e.memset(gpad[:, 0:2], 0.0), writes=[gpad.d[4]])
                pc = 0
                ec = 0

                def v3(ap, t):
                    return ap.rearrange("p (s t) -> p s t", t=t)

                for c in range(48):
                    wg = wsl[(2 * c) % 4]
                    wu = wsl[(2 * c + 1) % 4]
                    g.dma("pool", wg[:], w_gate[:, c * 128:(c + 1) * 128].rearrange("(kc p) n -> p kc n", p=128), writes=wg.d)
                    g.dma("pool", wu[:], w_up[:, c * 128:(c + 1) * 128].rearrange("(kc p) n -> p kc n", p=128), writes=wu.d)
                    g.op("pe", lambda e, c=c: e.transpose(out=pss[:, 256:288], in_=fsr[0:32, c * 128:(c + 1) * 128],
                                                        identity=ident_f[0:32, 0:32]), reads=fsr.d + ident_f.d, writes=pss.d)
                    cp("dve", gsm[:, :, 0:2], v3(pss[:, 256:288], 2), pss.d, gsm.d)
                    for gi_, (t0, n) in enumerate(TG):
                        p = psf[pc % 6]
                        pc += 1
                        for kc in range(16):
                            g.op("pe", lambda e, p=p, kc=kc, t0=t0, n=n, wg=wg: e.matmul(
                                p[:, 0:n], lhsT=wg[:, kc, :], rhs=h2T[:, kc, t0:t0 + n], start=(kc == 0), stop=(kc == 15)),
                                reads=h2T.d + wg.d, writes=p.d)
                        if t0 < T:
                            ec += 1
                            cp("act" if ec % 2 else "dve", gpad[:, 2 + t0:2 + t0 + n], p[:, 0:n], p.d, [gpad.d[gi_]])
                        else:
                            cp("dve", gsm[:, :, 2:6], v3(p[:, 0:NS], 4), p.d, gsm.d)
                    p = psf[pc % 6]
                    pc += 1
                    for kc in range(16):
                        g.op("pe", lambda e, p=p, kc=kc, wg=wg: e.matmul(p[0:2, 0:128], lhsT=h2T[:, kc, T - 2:T], rhs=wg[:, kc, :],
                                                                       start=(kc == 0), stop=(kc == 15)), reads=h2T.d + wg.d, writes=p.d)
                    for j in range(2):
                        for kc in range(16):
                            g.op("pe", lambda e, p=p, kc=kc, j=j, wg=wg: e.matmul(
                                p[0:16, 128 + j * 128:256 + j * 128], lhsT=h2T[:, kc, T + 2 + j:NT:4], rhs=wg[:, kc, :],
                                start=(kc == 0), stop=(kc == 15)), reads=h2T.d + wg.d, writes=p.d)
                    cp("dve", fst2[:], p[0:2, 0:128], p.d, fst2.d)
                    cp("dve", fst[:], p[0:16, 128:384].rearrange("p (j c) -> p j c", c=128), p.d, fst.d)
                    g.dma("sp", o_pfconv[:, c * 128:(c + 1) * 128], fst2[:], reads=fst2.d)
                    g.dma("sp", o_sfconv[:, :, c * 128:(c + 1) * 128], fst[:], reads=fst.d)
                    g.op("dve", lambda e, c=c: e.tensor_scalar(out=uc[:, 0:T], in0=gpad[:, 2:2 + T], scalar1=fp_[:, c, 2:3], scalar2=fp_[:, c, 3:4],
                                                             op0=ALU.mult, op1=ALU.add), reads=gpad.d + fp_.d, writes=uc.d)
                    g.op("dve", lambda e, c=c: e.tensor_scalar(out=v3(uc[:, T:NT], 4), in0=gsm[:, :, 2:6], scalar1=fp_[:, c, 2:3], scalar2=fp_[:, c, 3:4],
                                                             op0=ALU.mult, op1=ALU.add), reads=gsm.d + fp_.d, writes=uc.d)
                    for j in range(2):
                        g.op("dve", lambda e, c=c, j=j: e.scalar_tensor_tensor(
                            out=uc[:, 0:T], in0=gpad[:, j:j + T], scalar=fp_[:, c, j:j + 1], in1=uc[:, 0:T], op0=ALU.mult, op1=ALU.add),
                            reads=gpad.d + fp_.d + uc.d, writes=uc.d)
                        g.op("dve", lambda e, c=c, j=j: e.scalar_tensor_tensor(
                            out=v3(uc[:, T:NT], 4), in0=gsm[:, :, j:j + 4], scalar=fp_[:, c, j:j + 1], in1=v3(uc[:, T:NT], 4), op0=ALU.mult, op1=ALU.add),
                            reads=gsm.d + fp_.d + uc.d, writes=uc.d)
                    g.op("act", lambda e: e.activation(out=ug[:], in_=uc[:], func=AF.Gelu_apprx_tanh), reads=uc.d, writes=ug.d)
                    a_t = at[c % 2]
                    for (t0, n) in TG:
                        p = psf[pc % 6]
                        pc += 1
                        for kc in range(16):
                            g.op("pe", lambda e, p=p, kc=kc, t0=t0, n=n, wu=wu: e.matmul(
                                p[:, 0:n], lhsT=wu[:, kc, :], rhs=h2T[:, kc, t0:t0 + n], start=(kc == 0), stop=(kc == 15)),
                                reads=h2T.d + wu.d, writes=p.d)
                        g.op("dve", lambda e, p=p, t0=t0, n=n, a_t=a_t: e.tensor_tensor(
                            out=a_t[:, t0:t0 + n], in0=p[:, 0:n], in1=ug[:, t0:t0 + n], op=ALU.mult), reads=p.d + ug.d, writes=a_t.d)
                    g.dma("sp", act_d[c], a_t[:], reads=a_t.d)
                g.flush()
        chk(5)
        with contextlib.ExitStack() as s5:
            wds = [k.sb(s5, [128, 48, 512], BF16, name="wds") for _ in range(2)]
            ats = [k.sb(s5, [128, 48, 128], BF16, name="ats") for _ in range(2)]
            xr_ = [k.sb(s5, [128, 512], F32, name="xres") for _ in range(3)]
            so = [k.sb(s5, [128, 512], F32, name="so") for _ in range(3)]
            psf = [k.ps(s5, [128, 512], F32) for _ in range(6)]
            pc = 0
            for cg in range(4):
                wd = wds[cg % 2]
                for q in range(4):
                    g.dma("pool", wd[:, q * 12:(q + 1) * 12, :],
                          w_down[q * 1536:(q + 1) * 1536, cg * 512:(cg + 1) * 512].rearrange("(kc p) n -> p kc n", p=128), writes=wd.d)
                for ti, (t0, n) in enumerate(TT):
                    xr = xr_[pc % 3]
                    so_ = so[pc % 3]
                    a_s = ats[pc % 2]
                    p = psf[pc % 6]
                    pc += 1
                    g.dma("sp", xr[0:n, :], x2_d[t0:t0 + n, cg * 512:(cg + 1) * 512], writes=xr.d)
                    g.dma("sp", a_s[:, :, 0:n], act_d[:, :, t0:t0 + n].rearrange("c p t -> p c t"), writes=a_s.d)
                    for kc in range(48):
                        g.op("pe", lambda e, p=p, kc=kc, n=n, wd=wd, a_s=a_s: e.matmul(
                            p[0:n, :], lhsT=a_s[:, kc, 0:n], rhs=wd[:, kc, :], start=(kc == 0), stop=(kc == 47)),
                            reads=a_s.d + wd.d, writes=p.d)
                    g.op("dve", lambda e, p=p, xr=xr, so_=so_, n=n: e.tensor_tensor(
                        out=so_[0:n, :], in0=p[0:n, :], in1=xr[0:n, :], op=ALU.add), reads=p.d + xr.d, writes=so_.d)
                    g.dma("sp", x3_d[t0:t0 + n, cg * 512:(cg + 1) * 512], so_[0:n, :], reads=so_.d)
            g.flush()
        chk(6)
        with contextlib.ExitStack() as s6:
            gbc = k.sb(s6, [128, D], F32, name="gbc")
            g.dma("sp", gbc[:], norm_final.partition_broadcast(128), writes=gbc.d)
            xt = [k.sb(s6, [128, D], F32, name="xt") for _ in range(2)]
            yo = [k.sb(s6, [128, D], F32, name="yo") for _ in range(2)]
            junk = k.sb(s6, [128, D], BF16, name="junk")
            ss = k.sb(s6, [128, 17], F32, name="ss")
            rstd = k.sb(s6, [128, 17], F32, name="rstd")
            for ti, (t0, n) in enumerate(TT):
                x_t, y_o = xt[ti % 2], yo[ti % 2]
                g.dma("sp", x_t[0:n, :], x3_d[t0:t0 + n, :], writes=x_t.d)
                row_stats(s6, x_t, n, ss, rstd, junk, ti)
                g.op("dve", lambda e, x_t=x_t, y_o=y_o, n=n, ti=ti: e.scalar_tensor_tensor(
                    out=y_o[0:n, :], in0=x_t[0:n, :], scalar=rstd[0:n, ti:ti + 1], in1=gbc[0:n, :],
                    op0=ALU.mult, op1=ALU.mult), reads=x_t.d + rstd.d + gbc.d, writes=y_o.d)
                g.dma("sp", (y_p[t0:t0 + n, :] if ti < 16 else y_s[:, :]), y_o[0:n, :], reads=y_o.d)
            g.flush()
    except Stop:
        pass
    return nc, ins, outs


_st = 16 * np.arange(127)[:, None]
_b0 = 64 * np.arange(32)[None, :]
M_C2S = ((_st < _b0 + 64) & (_st + 32 > _b0)).astype(np.float32)


def make_in_maps(inputs):
    f = lambda a: np.ascontiguousarray(a, dtype=np.float32)
    x_prompt = f(inputs["x_prompt"])
    x_sample = f(inputs["x_sample"])
    rnn_rows = f(np.concatenate([inputs["rnn_conv_w"][0], inputs["rnn_conv_b"], inputs["rnn_ba"], inputs["rnn_bx"],
                                 inputs["rnn_lambda"]], axis=0))
    ffn_rows = f(np.concatenate([inputs["ffn_conv_w"][0], inputs["ffn_conv_b"]], axis=0))
    shared = {
        "ident": np.eye(128, dtype=np.float32),
        "norm_mix": f(inputs["norm_mix"]).reshape(1, D),
        "w_in": f(inputs["w_in"][0]),
        "rnn_rows": rnn_rows,
        "rnn_wa": f(inputs["rnn_wa"][0]),
        "rnn_wx": f(inputs["rnn_wx"][0]),
        "w_proj_rnn": f(inputs["w_proj_rnn"][0]),
        "w_proj_att": f(inputs["w_proj_att"][0]),
        "w_out": f(inputs["w_out"][0]),
        "norm_ffn": f(inputs["norm_ffn"]).reshape(1, D),
        "ffn_w_gate": f(inputs["ffn_w_gate"][0]),
        "ffn_w_up": f(inputs["ffn_w_up"][0]),
        "ffn_rows": ffn_rows,
        "ffn_w_down": f(inputs["ffn_w_down"][0]),
        "norm_final": f(inputs["norm_final"]).reshape(1, D),
        "cmp_w": f(inputs["cmp_w"][0]),
        "cmp_pos": f(inputs["cmp_pos"][0]),
        "m_c2s": M_C2S,
    }
    in_maps = []
    for c in range(NCORES):
        m = dict(shared)
        m["xp"] = x_prompt[c]
        m["xs"] = x_sample[16 * c:16 * c + 16].reshape(NS, D)
        m["st_rconv"] = f(inputs["state_rnn_conv"][0, 16 * c:16 * c + 16]).reshape(48, DRNN)
        m["st_rh"] = f(inputs["state_rnn_h"][0, 16 * c:16 * c + 16])
        m["st_fconv"] = f(inputs["state_ffn_conv"][0, 16 * c:16 * c + 16]).reshape(32, DFF)
        if "page_table" in inputs:
            m["ccmp"] = f(inputs["cache_kv_cmp"][0]).reshape(2560 * 128, 512)
            m["cslc"] = f(inputs["cache_kv_slc"][0]).reshape(2560 * 128, 512)
            m["ptab"] = np.ascontiguousarray(inputs["page_table"][16 * c:16 * c + 16], dtype=np.int32).reshape(1, 256)
        if "cache_kv_win" in inputs:
            m["cwin"] = f(inputs["cache_kv_win"][0, 16 * c:16 * c + 16]).reshape(16, 512, 512)
        in_maps.append(m)
    return in_maps


def kernel(**inputs):
    flags = {}
    nc, ins, outs = build(flags)
    in_maps = [{k_: v for k_, v in m.items() if k_ in ins} for m in make_in_maps(inputs)]
    res = run_bass_kernel_spmd(nc, in_maps, core_ids=list(range(NCORES)))
    R = res.results

    def cat(name, shape):
        return np.stack([np.asarray(R[c][name], dtype=np.float32).reshape(shape) for c in range(NCORES)], axis=0)

    y_prompt = cat("y_p", (T, D))
    y_sample = cat("y_s", (16, 4, D)).reshape(128, 4, D)
    pkv = cat("o_pkv", (3, T, 2, 2, 128))
    skv = cat("o_skv", (3, 16, 4, 2, 2, 128))
    p_kv_cmp = pkv[:, 0][None]
    p_kv_slc = pkv[:, 1][None]
    p_kv_win = np.ascontiguousarray(pkv[:, 2, T - 512:])[None]
    s_kv_cmp = skv[:, 0].reshape(128, 4, 2, 2, 128)[None]
    s_kv_slc = skv[:, 1].reshape(128, 4, 2, 2, 128)[None]
    s_kv_win = cat("o_swin", (16, 512, 2, 2, 128)).reshape(128, 512, 2, 2, 128)[None]
    p_rnn_h = cat("o_prh", (DRNN,))[None]
    p_rnn_conv = cat("o_prconv", (3, DRNN))[None]
    p_ffn_conv = cat("o_pfconv", (2, DFF))[None]
    s_rnn_h = cat("o_srh", (16, DRNN)).reshape(128, DRNN)[None]
    s_rnn_conv = cat("o_srconv", (16, 3, DRNN)).reshape(128, 3, DRNN)[None]
    s_ffn_conv = cat("o_sfconv", (16, 2, DFF)).reshape(128, 2, DFF)[None]
    outs_ = (y_prompt, y_sample, p_kv_cmp, p_kv_slc, p_kv_win, p_rnn_h, p_rnn_conv, p_ffn_conv,
             s_kv_cmp, s_kv_slc, s_kv_win, s_rnn_h, s_rnn_conv, s_ffn_conv)
    return tuple(np.ascontiguousarray(o, dtype=np.float32) for o in outs_)
```

```python
import contextlib
import numpy as np
import concourse.bass as bass
import concourse.mybir as mybir
from concourse.bass_utils import run_bass_kernel_spmd

F32 = mybir.dt.float32
BF16 = mybir.dt.bfloat16
I32 = mybir.dt.int32
AF = mybir.ActivationFunctionType
ALU = mybir.AluOpType
AX = mybir.AxisListType

NCORES = 8
D = 2048
T = 2048
NS = 64
NT = T + NS
DIN = 8728
DRNN = 1024
DFF = 6144
TG = [(0, 512), (512, 512), (1024, 512), (1536, 512), (2048, 64)]
TT = [(i * 128, 128) for i in range(16)] + [(2048, 64)]
C_XR, C_GR, C_Q, C_KVC, C_KVS, C_KVW, C_GN, C_GM = 0, 1024, 2048, 3072, 3584, 4096, 4608, 4632

ENGS = ("pe", "act", "dve", "pool", "sp")
BLOCKFN = {"pe": "tensor", "act": "scalar", "dve": "vector", "pool": "gpsimd", "sp": "sync"}
DMA_SLOTS = {"sp": 30, "pool": 24, "act": 4}


class Dep:
    __slots__ = ("w", "r", "rd")

    def __init__(self):
        self.w = None
        self.r = {}
        self.rd = []


class Graph:
    def __init__(self, nc, stack):
        self.nc = nc
        self.ops = []
        self.start = 0
        self.esem = {}
        self.ecnt = {e: 0 for e in ENGS}
        self.last_eng_op = {e: None for e in ENGS}
        for e in ENGS:
            self.esem[e] = stack.enter_context(nc.semaphore("es_" + e))
        self.dsem, self.dtot, self.dlast, self.dnext = {}, {}, {}, {}
        for q, n in DMA_SLOTS.items():
            self.dsem[q] = [stack.enter_context(nc.semaphore("ds_%s%d" % (q, i))) for i in range(n)]
            self.dtot[q] = [0] * n
            self.dlast[q] = [None] * n
            self.dnext[q] = 0
        self.waited = {e: {} for e in ENGS}
        self.pe_fence = None

    def op(self, eng, fn, reads=(), writes=(), dma=False):
        deps = set()
        for d in reads:
            if d.w is not None:
                deps.add(d.w)
        for d in writes:
            if d.w is not None:
                deps.add(d.w)
            deps.update(d.r.values())
            deps.update(d.rd)
        i = len(self.ops)
        slot = None
        if dma:
            k = self.dnext[eng]
            self.dnext[eng] = (k + 1) % len(self.dsem[eng])
            if self.dlast[eng][k] is not None:
                deps.add(self.dlast[eng][k])
            self.dlast[eng][k] = i
            slot = k
        self.ops.append([eng, fn, deps, dma, slot, None])
        if not dma and fn is not None:
            self.last_eng_op[eng] = i
        for d in writes:
            d.w = i
            d.r = {}
            d.rd = []
        for d in reads:
            if d.w == i:
                continue
            if dma:
                d.rd.append(i)
            else:
                d.r[eng] = i
        return i

    def dma(self, q, out, in_, reads=(), writes=(), **kw):
        return self.op(q, lambda e: e.dma_start(out=out, in_=in_, **kw), reads, writes, dma=True)

    def barrier(self):
        a = []
        real = [x for x in self.last_eng_op.values() if x is not None]
        for e in ENGS:
            deps = set()
            if self.last_eng_op[e] is not None:
                deps.add(self.last_eng_op[e])
            if e in self.dlast:
                for x in self.dlast[e]:
                    if x is not None:
                        deps.add(x)
            i = len(self.ops)
            self.ops.append([e, (lambda en: en.nop()), deps, False, None, None])
            a.append(i)
        for k, e in enumerate(ENGS):
            self.last_eng_op[e] = a[k]
        for e in ENGS:
            self.ops.append([e, None, set(a) | set(real), False, None, None])

    def flush(self):
        self.barrier()
        nc, ops, start = self.nc, self.ops, self.start
        n = len(ops)
        needed = set()
        for i in range(start, n):
            for d in ops[i][2]:
                if d >= start and not ops[d][3]:
                    needed.add(d)
        by_eng = {e: [] for e in ENGS}
        for i in range(start, n):
            o = ops[i]
            eng = o[0]
            by_eng[eng].append(i)
            if o[3]:
                k = o[4]
                self.dtot[eng][k] += 16
                o[5] = (self.dsem[eng][k], self.dtot[eng][k])
            elif i in needed and o[1] is not None:
                self.ecnt[eng] += 1
                o[5] = (self.esem[eng], self.ecnt[eng])
        g = self
        with nc.Block() as block:
            for eng in ENGS:
                def body(e, eng=eng):
                    waited = g.waited[eng]
                    for i in by_eng[eng]:
                        o = ops[i]
                        waited_now = False
                        for d in sorted(o[2]):
                            if d < start:
                                continue
                            od = ops[d]
                            if od[0] == "pe" and eng == "pe" and not od[3]:
                                continue
                            sem, val = od[5]
                            key = id(sem)
                            if waited.get(key, 0) >= val:
                                continue
                            e.wait_ge(sem, val)
                            waited[key] = val
                            waited_now = True
                        if o[1] is None:
                            continue
                        ins = o[1](e)
                        if o[3]:
                            ins.then_inc(o[5][0], 16)
                        elif o[5] is not None:
                            if eng == "pe" and g.pe_fence is not None:
                                ins = g.pe_fence(e)
                            ins.then_inc(o[5][0], 1)
                getattr(block, BLOCKFN[eng])(body)
        self.start = n


class Stop(Exception):
    pass


class Tl:
    def __init__(self, t, nd=1):
        self.t = t
        self.d = [Dep() for _ in range(nd)]

    def __getitem__(self, k):
        return self.t[k]


class K:
    def __init__(self, nc, stack):
        self.nc = nc
        self.g = Graph(nc, stack)
        self.uid = 0

    def sb(self, st, shape, dt, nd=1, name=None):
        self.uid += 1
        t = st.enter_context(self.nc.sbuf_tensor("%s_%d" % (name or "sb", self.uid), list(shape), dt))
        return Tl(t, nd)

    def ps(self, st, shape, dt, nd=1):
        self.uid += 1
        t = st.enter_context(self.nc.psum_tensor("ps_%d" % self.uid, list(shape), dt))
        return Tl(t, nd)


def build(flags):
    nc = bass.Bass("TRN2", target_bir_lowering=False)
    ins = {}
    outs = {}

    def din(name, shape, dt=F32):
        if flags.get("lean") and name not in ("xp", "xs", "ident", "norm_mix", "w_in"):
            return None
        ins[name] = nc.dram_tensor(name, list(shape), dt, kind="ExternalInput").ap()
        return ins[name]

    def dout(name, shape, dt=F32):
        if flags.get("noout") and name not in ("o_pkv", "o_skv"):
            return None
        outs[name] = nc.dram_tensor(name, list(shape), dt, kind="ExternalOutput").ap()
        return outs[name]

    def dscr(name, shape, dt):
        if flags.get("noscr") or (flags.get("lean") and name not in ("qT_d", "kcT_d", "ksT_d", "kwT_d")):
            return None
        return nc.dram_tensor(name, list(shape), dt, kind="ExternalOutput").ap()

    xp = din("xp", [T, D])
    xs = din("xs", [NS, D])
    ident = din("ident", [128, 128])
    norm_mix = din("norm_mix", [1, D])
    w_in = din("w_in", [D, DIN])
    rnn_rows = din("rnn_rows", [8, DRNN])
    rnn_wa = din("rnn_wa", [8, 128, 128])
    rnn_wx = din("rnn_wx", [8, 128, 128])
    st_rconv = din("st_rconv", [48, DRNN])
    st_rh = din("st_rh", [16, DRNN])
    w_proj_rnn = din("w_proj_rnn", [DRNN, D])
    w_proj_att = din("w_proj_att", [DRNN, D])
    w_out = din("w_out", [D, D])
    norm_ffn = din("norm_ffn", [1, D])
    w_gate = din("ffn_w_gate", [D, DFF])
    w_up = din("ffn_w_up", [D, DFF])
    ffn_rows = din("ffn_rows", [4, DFF])
    st_fconv = din("st_fconv", [32, DFF])
    w_down = din("ffn_w_down", [DFF, D])
    norm_final = din("norm_final", [1, D])
    cwin = din("cwin", [16, 512, 512])
    cmp_w = din("cmp_w", [2, 32, 128, 128])
    cmp_pos = din("cmp_pos", [2, 32, 128])
    m_c2s = din("m_c2s", [127, 32])
    ccmp = din("ccmp", [(16 if flags.get("smallc") else 2560) * 128, 512])
    cslc = din("cslc", [(16 if flags.get("smallc") else 2560) * 128, 512])
    ptab = din("ptab", [1, 256], I32)

    o_pkv = dout("o_pkv", [3, T, 512])
    o_skv = dout("o_skv", [3, NS, 512])
    o_prh = dout("o_prh", [1, DRNN])
    o_prconv = dout("o_prconv", [3, DRNN])
    o_srh = dout("o_srh", [16, DRNN])
    o_srconv = dout("o_srconv", [16, 3, DRNN])
    o_pfconv = dout("o_pfconv", [2, DFF])
    o_sfconv = dout("o_sfconv", [16, 2, DFF])
    y_p = dout("y_p", [T, D])
    y_s = dout("y_s", [NS, D])
    o_swin = dout("o_swin", [16, 512, 512])

    gm_d = dscr("gm_d", [flags.get("gmn", 32), 128, NT], BF16)
    qT_d = dscr("qT_d", [8, 128, NT], BF16)
    kcT_d = dscr("kcT_d", [4, 128, NT], BF16)
    ksT_d = dscr("ksT_d", [2, 128, NT], BF16)
    kwT_d = dscr("kwT_d", [2, 128, NT], BF16)
    yrT_d = dscr("yrT_d", [8, 128, NT], BF16)
    oT_d = dscr("oT_d", [8, 128, NT], BF16)
    gn_d = dscr("gn_d", [NT, 24], F32)
    x2_d = dscr("x2_d", [NT, D], F32)
    x3_d = dscr("x3_d", [NT, D], F32)
    act_d = dscr("act_d", [48, 128, NT], BF16)

    kh = []

    def sst(tag):
        if flags.get('sstop') == tag:
            kh[0].g.flush()
            raise Stop()

    def chk(n):
        if flags.get('upto', 99) < n:
            kh[0].g.flush()
            raise Stop()

    try:
      with contextlib.ExitStack() as top:
        k = K(nc, top)
        kh.append(k)
        g = k.g
        ident_f = k.sb(top, [128, 128], F32, name="identf")
        ident_b = k.sb(top, [128, 128], BF16, name="identb")
        ones_c = k.sb(top, [128, 1], F32, name="onesc")
        epst = k.sb(top, [128, 1], F32, name="epst")
        g.dma("sp", ident_f[:], ident, writes=ident_f.d)
        g.op("dve", lambda e: e.tensor_copy(out=ident_b[:], in_=ident_f[:]), reads=ident_f.d, writes=ident_b.d)
        g.op("dve", lambda e: e.memset(ones_c[:], 1.0), writes=ones_c.d)
        g.op("dve", lambda e: e.memset(epst[:], 1e-6), writes=epst.d)
        fence_ps = k.ps(top, [128, 512], F32)
        g.pe_fence = lambda e: e.matmul(fence_ps[:, 0:128], lhsT=ident_b[:, :], rhs=ident_b[:, :], start=True, stop=True)

        def cp(eng, out, in_, reads, writes):
            if eng == "act":
                g.op("act", lambda e: e.copy(out=out, in_=in_), reads=reads, writes=writes)
            else:
                g.op(eng, lambda e: e.tensor_copy(out=out, in_=in_), reads=reads, writes=writes)

        def row_stats(sa, x_t, n, ss, rstd, junk, ti):
            g.op("act", lambda e: e.activation(
                out=junk[0:n, :], in_=x_t[0:n, :], func=AF.Square, accum_out=ss[0:n, ti:ti + 1]),
                reads=x_t.d, writes=junk.d + ss.d)
            g.op("act", lambda e: e.activation(
                out=rstd[0:n, ti:ti + 1], in_=ss[0:n, ti:ti + 1], func=AF.Sqrt, scale=1.0 / D, bias=epst[0:n, 0:1]),
                reads=ss.d + epst.d, writes=rstd.d)
            g.op("dve", lambda e: e.reciprocal(
                out=rstd[0:n, ti:ti + 1], in_=rstd[0:n, ti:ti + 1]), reads=rstd.d, writes=rstd.d)

        def norm_phase(hT, src_fn, gain):
            with contextlib.ExitStack() as sa:
                gbc = k.sb(sa, [128, D], F32, name="gbc")
                g.dma("sp", gbc[:], gain.partition_broadcast(128), writes=gbc.d)
                xt = [k.sb(sa, [128, D], F32, name="xt") for _ in range(2)]
                hb = [k.sb(sa, [128, D], BF16, name="hb") for _ in range(2)]
                junk = k.sb(sa, [128, D], BF16, name="junk")
                ss = k.sb(sa, [128, 17], F32, name="ss")
                rstd = k.sb(sa, [128, 17], F32, name="rstd")
                pst = [k.ps(sa, [128, 8, 128], BF16) for _ in range(2)]
                for ti, (t0, n) in enumerate(TT):
                    x_t, h_b = xt[ti % 2], hb[ti % 2]
                    g.dma("sp", x_t[0:n, :], src_fn(ti, t0, n), writes=x_t.d)
                    row_stats(sa, x_t, n, ss, rstd, junk, ti)
                    g.op("dve", lambda e, x_t=x_t, h_b=h_b, n=n, ti=ti: e.scalar_tensor_tensor(
                        out=h_b[0:n, :], in0=x_t[0:n, :], scalar=rstd[0:n, ti:ti + 1], in1=gbc[0:n, :],
                        op0=ALU.mult, op1=ALU.mult), reads=x_t.d + rstd.d + gbc.d, writes=h_b.d)
                    for half in range(2):
                        p = pst[half]
                        for j in range(8):
                            kc = half * 8 + j
                            g.op("pe", lambda e, p=p, j=j, kc=kc, h_b=h_b, n=n: e.transpose(
                                out=p[:, j, 0:n], in_=h_b[0:n, kc * 128:(kc + 1) * 128], identity=ident_b[0:n, 0:n]),
                                reads=h_b.d + ident_b.d, writes=p.d)
                        cp("act" if half == 0 else "dve", hT[:, half * 8:half * 8 + 8, t0:t0 + n], p[:, :, 0:n], p.d, hT.d)
                g.flush()

        with contextlib.ExitStack() as s1:
            hT = k.sb(s1, [128, 16, NT], BF16, name="hT")
            norm_phase(hT, lambda ti, t0, n: (xp[t0:t0 + n, :] if ti < 16 else xs[:, :]), norm_mix)

            with contextlib.ExitStack() as sb_:
                wsl = [k.sb(sb_, [128, 16, 128], BF16, name="wsl") for _ in range(4)]
                wctr = [0]
                psf = [k.ps(sb_, [128, 512], F32) for _ in range(6)]
                pss = k.ps(sb_, [128, 512], F32)
                pctr = [0]
                stg_t = [k.sb(sb_, [128, 4, 128], F32, name="stgt") for _ in range(3)]
                stg_f = [k.sb(sb_, [128, NT], BF16, nd=5, name="stgf") for _ in range(2)]
                tmp64 = [k.sb(sb_, [128, NS], F32, name="tmp64") for _ in range(2)]
                t64c = [0]
                sctr = [0]
                fctr = [0]
                ectr = [0]

                def load_w(wd, c0, ncols=128):
                    w = wsl[wctr[0] % len(wsl)]
                    wctr[0] += 1
                    src = wd[:, c0:c0 + ncols].rearrange("(kc p) n -> p kc n", p=128)
                    g.dma("pool", w[:, :, 0:ncols], src, writes=w.d)
                    return w

                def next_ps():
                    p = psf[pctr[0] % len(psf)]
                    pctr[0] += 1
                    return p

                def alt():
                    ectr[0] += 1
                    return "act" if ectr[0] % 2 else "dve"

                def t_layout(hT_, w, ncols, emit):
                    for q0 in range(0, 17, 4):
                        tiles = TT[q0:q0 + 4]
                        p = next_ps()
                        for qi, (t0, n) in enumerate(tiles):
                            for kc in range(16):
                                g.op("pe", lambda e, p=p, qi=qi, kc=kc, t0=t0, n=n: e.matmul(
                                    p[0:n, qi * 128:qi * 128 + ncols], lhsT=hT_[:, kc, t0:t0 + n], rhs=w[:, kc, 0:ncols],
                                    start=(kc == 0), stop=(kc == 15)), reads=hT_.d + w.d, writes=p.d)
                        emit(p, q0, tiles)

                def f_layout(hT_, w, emit):
                    for gi, (t0, n) in enumerate(TG):
                        p = next_ps()
                        for kc in range(16):
                            g.op("pe", lambda e, p=p, kc=kc, t0=t0, n=n: e.matmul(
                                p[:, 0:n], lhsT=w[:, kc, :], rhs=hT_[:, kc, t0:t0 + n],
                                start=(kc == 0), stop=(kc == 15)), reads=hT_.d + w.d, writes=p.d)
                        emit(p, gi, t0, n)

                def f_to_scratch(hT_, w, dst, func=None, scale=None):
                    sf = stg_f[fctr[0] % 2]
                    fctr[0] += 1

                    def emit(p, gi, t0, n):
                        sd = [sf.d[gi]]
                        if func is not None:
                            if n == NS:
                                tq = tmp64[t64c[0] % 2]
                                t64c[0] += 1
                                cp("dve", tq[:, 0:n], p[:, 0:n], p.d, tq.d)
                                g.op("act", lambda e: e.activation(out=sf[:, t0:t0 + n], in_=tq[:, 0:n], func=func),
                                     reads=tq.d, writes=sd)
                            else:
                                g.op("act", lambda e: e.activation(out=sf[:, t0:t0 + n], in_=p[:, 0:n], func=func),
                                     reads=p.d, writes=sd)
                        elif scale is not None:
                            g.op("dve", lambda e: e.tensor_scalar(out=sf[:, t0:t0 + n], in0=p[:, 0:n], scalar1=scale, scalar2=None, op0=ALU.mult),
                                 reads=p.d, writes=sd)
                        else:
                            cp("dve" if n == NS else alt(), sf[:, t0:t0 + n], p[:, 0:n], p.d, sd)
                    f_layout(hT_, w, emit)
                    g.dma("sp", dst, sf[:], reads=sf.d)

                for br in range(3):
                    for cc in range(4):
                        c0 = C_KVC + br * 512 + cc * 128
                        w = load_w(w_in, c0)

                        def emit(p, q0, tiles, br=br, cc=cc):
                            st_ = stg_t[sctr[0] % len(stg_t)]
                            sctr[0] += 1
                            nq = len(tiles)
                            nrow = tiles[0][1]
                            cp(alt(), st_[0:nrow, 0:nq, :], p[0:nrow, 0:nq * 128].rearrange("p (q c) -> p q c", c=128),
                               p.d, st_.d)
                            if tiles[0][0] < T:
                                t0 = tiles[0][0]
                                dst = o_pkv[br, t0:t0 + nq * 128, cc * 128:(cc + 1) * 128].rearrange("(q p) c -> p q c", p=128)
                                g.dma("sp", dst, st_[:, 0:nq, :], reads=st_.d)
                            else:
                                g.dma("sp", o_skv[br, :, cc * 128:(cc + 1) * 128], st_[0:NS, 0, :], reads=st_.d)
                        t_layout(hT, w, 128, emit)
                        if flags.get('nof2s'):
                            pass
                        elif br == 0:
                            f_to_scratch(hT, w, kcT_d[cc])
                        elif cc < 2:
                            f_to_scratch(hT, w, (ksT_d if br == 1 else kwT_d)[cc])
                chk(0.5)
                for h in range(flags.get('nq', 8)):
                    w = load_w(w_in, C_Q + h * 128)
                    f_to_scratch(hT, w, qT_d[h], scale=float(1.0 / np.sqrt(128.0)))
                chk(0.6)
                if not flags.get('nofl'):
                    g.flush()
                for j in range(flags.get('ngm', 32)):
                    if flags.get('gmz'):
                        w = load_w(w_in, C_Q + j * 128)
                        f_to_scratch(hT, w, qT_d[j], scale=float(1.0 / np.sqrt(128.0)))
                        continue
                    w = load_w(w_in, flags.get('gmoff', C_GM) + j * 128)
                    f_to_scratch(hT, w, (qT_d if flags.get('gmq') else gm_d)[j], func=(None if flags.get('gmcopy') else AF.Sigmoid), scale=(1.0 if flags.get('gms') else None))
                chk(0.7)
                w = load_w(w_in, C_GN, 24)
                gn_st = k.sb(sb_, [128, 4, 24], F32, name="gnst")

                def emit_gn(p, q0, tiles):
                    nq = len(tiles)
                    nrow = tiles[0][1]
                    g.op("act", lambda e: e.activation(
                        out=gn_st[0:nrow, 0:nq, :], in_=p[0:nrow, 0:nq * 128].rearrange("p (q c) -> p q c", c=128)[:, :, 0:24],
                        func=AF.Sigmoid), reads=p.d, writes=gn_st.d)
                    t0 = tiles[0][0]
                    if t0 < T:
                        g.dma("sp", gn_d[t0:t0 + nq * 128, :].rearrange("(q p) c -> p q c", p=128), gn_st[:, 0:nq, :], reads=gn_st.d)
                    else:
                        g.dma("sp", gn_d[T:NT, :], gn_st[0:NS, 0, :], reads=gn_st.d)
                t_layout(hT, w, 24, emit_gn)

                chk(0.8)
                prow = k.sb(sb_, [8, DRNN], F32, name="prow")
                g.dma("sp", prow[:], rnn_rows, writes=prow.d)
                csr = k.sb(sb_, [48, DRNN], F32, name="csr")
                g.dma("sp", csr[:], st_rconv, writes=csr.d)
                h0r = k.sb(sb_, [16, DRNN], F32, name="h0r")
                g.dma("sp", h0r[:], st_rh, writes=h0r.d)
                wa_sb = k.sb(sb_, [128, 8, 128], BF16, name="wa")
                wx_sb = k.sb(sb_, [128, 8, 128], BF16, name="wx")
                g.dma("pool", wa_sb[:], rnn_wa.rearrange("n c d -> c n d"), writes=wa_sb.d)
                g.dma("pool", wx_sb[:], rnn_wx.rearrange("n c d -> c n d"), writes=wx_sb.d)
                rp = k.sb(sb_, [128, 8, 8], F32, name="rp")
                for n_ in range(8):
                    g.op("pe", lambda e, n_=n_: e.transpose(out=pss[:, n_ * 8:n_ * 8 + 8], in_=prow[0:8, n_ * 128:(n_ + 1) * 128],
                                                          identity=ident_f[0:8, 0:8]), reads=prow.d + ident_f.d, writes=pss.d)
                cp("dve", rp[:], pss[:, 0:64].rearrange("p (n j) -> p n j", j=8), pss.d, rp.d)
                c1 = k.sb(sb_, [128, 8], F32, name="c1")
                g.op("act", lambda e: e.activation(out=c1[:], in_=rp[:, :, 7], func=AF.Exp, scale=-1.0), reads=rp.d, writes=c1.d)
                g.op("act", lambda e: e.activation(out=c1[:], in_=c1[:], func=AF.Ln, bias=ones_c[:, 0:1]), reads=c1.d + ones_c.d, writes=c1.d)
                g.op("dve", lambda e: e.tensor_scalar(out=c1[:], in0=c1[:], scalar1=-8.0, scalar2=None, op0=ALU.mult), reads=c1.d, writes=c1.d)

                chk(0.85)
                xpad = k.sb(sb_, [128, 3 + T], F32, nd=5, name="xpad")
                xsm = k.sb(sb_, [128, 16, 7], F32, name="xsm")
                h0T = k.sb(sb_, [128, 16], F32, name="h0T")
                xc = k.sb(sb_, [128, NT], F32, name="xc")
                xcb = k.sb(sb_, [128, NT], BF16, name="xcb")
                rr = k.sb(sb_, [128, NT], F32, name="rr")
                ii = k.sb(sb_, [128, NT], F32, name="ii")
                aa = k.sb(sb_, [128, NT], F32, name="aa")
                hh = k.sb(sb_, [128, NT], F32, name="hh")
                gg = k.sb(sb_, [128, NT], BF16, name="gg")
                ysb = k.sb(sb_, [128, NT], BF16, name="ysb")
                cst = k.sb(sb_, [16, 3, 128], F32, name="cst")
                cst3 = k.sb(sb_, [3, 128], F32, name="cst3")
                hst = k.sb(sb_, [32, 128], F32, name="hst")
                hcol = k.sb(sb_, [128, 128], F32, name="hcol")
                g.op("dve", lambda e: e.memset(hcol[:], 0.0), writes=hcol.d)
                g.op("dve", lambda e: e.memset(xpad[:, 0:3], 0.0), writes=[xpad.d[4]])

                def v3(ap, t):
                    return ap.rearrange("p (s t) -> p s t", t=t)

                for n_ in range(8):
                    wxr = load_w(w_in, C_XR + n_ * 128)
                    wgr = load_w(w_in, C_GR + n_ * 128)
                    g.op("pe", lambda e, n_=n_: e.transpose(out=pss[:, 64:112], in_=csr[0:48, n_ * 128:(n_ + 1) * 128],
                                                          identity=ident_f[0:48, 0:48]), reads=csr.d + ident_f.d, writes=pss.d)
                    g.op("pe", lambda e, n_=n_: e.transpose(out=pss[:, 112:128], in_=h0r[0:16, n_ * 128:(n_ + 1) * 128],
                                                          identity=ident_f[0:16, 0:16]), reads=h0r.d + ident_f.d, writes=pss.d)
                    cp("dve", xsm[:, :, 0:3], v3(pss[:, 64:112], 3), pss.d, xsm.d)
                    cp("dve", h0T[:], pss[:, 112:128], pss.d, h0T.d)

                    def emit_x(p, gi, t0, n):
                        if t0 < T:
                            cp(alt(), xpad[:, 3 + t0:3 + t0 + n], p[:, 0:n], p.d, [xpad.d[gi]])
                        else:
                            cp("dve", xsm[:, :, 3:7], v3(p[:, 0:NS], 4), p.d, xsm.d)
                    f_layout(hT, wxr, emit_x)
                    chk(0.86)
                    p = next_ps()
                    for kc in range(16):
                        g.op("pe", lambda e, p=p, kc=kc, wxr=wxr: e.matmul(p[0:3, 0:128], lhsT=hT[:, kc, T - 3:T], rhs=wxr[:, kc, :],
                                                                  start=(kc == 0), stop=(kc == 15)), reads=hT.d + wxr.d, writes=p.d)
                    for j in range(3):
                        for kc in range(16):
                            g.op("pe", lambda e, p=p, kc=kc, j=j, wxr=wxr: e.matmul(
                                p[0:16, 128 + j * 128:256 + j * 128], lhsT=hT[:, kc, T + 1 + j:NT:4], rhs=wxr[:, kc, :],
                                start=(kc == 0), stop=(kc == 15)), reads=hT.d + wxr.d, writes=p.d)
                    cp("dve", cst3[:], p[0:3, 0:128], p.d, cst3.d)
                    cp("dve", cst[:], p[0:16, 128:512].rearrange("p (j c) -> p j c", c=128), p.d, cst.d)
                    g.dma("sp", o_prconv[:, n_ * 128:(n_ + 1) * 128], cst3[:], reads=cst3.d)
                    g.dma("sp", o_srconv[:, :, n_ * 128:(n_ + 1) * 128], cst[:], reads=cst.d)
                    chk(0.87)
                    def rpc(j):
                        return rp[:, n_, j:j + 1]
                    g.op("dve", lambda e, n_=n_: e.tensor_scalar(out=xc[:, 0:T], in0=xpad[:, 3:3 + T], scalar1=rp[:, n_, 3:4], scalar2=rp[:, n_, 4:5],
                                                               op0=ALU.mult, op1=ALU.add), reads=xpad.d + rp.d, writes=xc.d)
                    g.op("dve", lambda e, n_=n_: e.tensor_scalar(out=v3(xc[:, T:NT], 4), in0=xsm[:, :, 3:7], scalar1=rp[:, n_, 3:4], scalar2=rp[:, n_, 4:5],
                                                               op0=ALU.mult, op1=ALU.add), reads=xsm.d + rp.d, writes=xc.d)
                    for j in range(3):
                        g.op("dve", lambda e, n_=n_, j=j: e.scalar_tensor_tensor(
                            out=xc[:, 0:T], in0=xpad[:, j:j + T], scalar=rp[:, n_, j:j + 1], in1=xc[:, 0:T], op0=ALU.mult, op1=ALU.add),
                            reads=xpad.d + rp.d + xc.d, writes=xc.d)
                        g.op("dve", lambda e, n_=n_, j=j: e.scalar_tensor_tensor(
                            out=v3(xc[:, T:NT], 4), in0=xsm[:, :, j:j + 4], scalar=rp[:, n_, j:j + 1], in1=v3(xc[:, T:NT], 4), op0=ALU.mult, op1=ALU.add),
                            reads=xsm.d + rp.d + xc.d, writes=xc.d)
                    chk(0.88)
                    cp("act", xcb[:], xc[:], xc.d, xcb.d)
                    for (t0, n) in TG:
                        p1 = next_ps()
                        g.op("pe", lambda e, p1=p1, t0=t0, n=n, n_=n_: e.matmul(p1[:, 0:n], lhsT=wa_sb[:, n_, :], rhs=xcb[:, t0:t0 + n], start=True, stop=True),
                             reads=wa_sb.d + xcb.d, writes=p1.d)
                        src1 = p1
                        if n == NS:
                            src1 = tmp64[t64c[0] % 2]
                            t64c[0] += 1
                            cp("dve", src1[:, 0:n], p1[:, 0:n], p1.d, src1.d)
                        g.op("act", lambda e, src1=src1, t0=t0, n=n, n_=n_: e.activation(out=rr[:, t0:t0 + n], in_=src1[:, 0:n], func=AF.Sigmoid, bias=rp[:, n_, 5:6]),
                             reads=src1.d + rp.d, writes=rr.d)
                        p2 = next_ps()
                        g.op("pe", lambda e, p2=p2, t0=t0, n=n, n_=n_: e.matmul(p2[:, 0:n], lhsT=wx_sb[:, n_, :], rhs=xcb[:, t0:t0 + n], start=True, stop=True),
                             reads=wx_sb.d + xcb.d, writes=p2.d)
                        src2 = p2
                        if n == NS:
                            src2 = tmp64[t64c[0] % 2]
                            t64c[0] += 1
                            cp("dve", src2[:, 0:n], p2[:, 0:n], p2.d, src2.d)
                        g.op("act", lambda e, src2=src2, t0=t0, n=n, n_=n_: e.activation(out=ii[:, t0:t0 + n], in_=src2[:, 0:n], func=AF.Sigmoid, bias=rp[:, n_, 6:7]),
                             reads=src2.d + rp.d, writes=ii.d)
                    chk(0.89)
                    g.op("act", lambda e, n_=n_: e.activation(out=aa[:], in_=rr[:], func=AF.Exp, scale=c1[:, n_:n_ + 1]), reads=rr.d + c1.d, writes=aa.d)
                    g.op("dve", lambda e: e.tensor_tensor(out=rr[:], in0=aa[:], in1=aa[:], op=ALU.mult), reads=aa.d, writes=rr.d)
                    g.op("act", lambda e: e.activation(out=rr[:], in_=rr[:], func=AF.Sqrt, scale=-1.0, bias=ones_c[:, 0:1]), reads=rr.d + ones_c.d, writes=rr.d)
                    g.op("dve", lambda e: e.tensor_tensor(out=ii[:], in0=ii[:], in1=xc[:], op=ALU.mult), reads=ii.d + xc.d, writes=ii.d)
                    g.op("dve", lambda e: e.tensor_tensor(out=rr[:], in0=rr[:], in1=ii[:], op=ALU.mult), reads=ii.d + rr.d, writes=rr.d)
                    chk(0.9)
                    a0 = v3(aa[:, T:NT], 4)[:, :, 0]
                    u0 = v3(rr[:, T:NT], 4)[:, :, 0]
                    g.op("dve", lambda e, a0=a0: e.tensor_tensor(out=h0T[:], in0=h0T[:], in1=a0, op=ALU.mult), reads=aa.d + h0T.d, writes=h0T.d)
                    g.op("dve", lambda e, u0=u0: e.tensor_tensor(out=u0, in0=u0, in1=h0T[:], op=ALU.add), reads=rr.d + h0T.d, writes=rr.d)
                    g.op("dve", lambda e, a0=a0: e.memset(a0, 0.0), reads=aa.d, writes=aa.d)
                    g.op("dve", lambda e: e.tensor_tensor_scan(out=hh[:, 0:T], data0=aa[:, 0:T], data1=rr[:, 0:T], initial=0.0, op0=ALU.mult, op1=ALU.add),
                         reads=aa.d + rr.d, writes=hh.d)
                    g.op("dve", lambda e: e.tensor_tensor_scan(out=hh[:, T:NT], data0=aa[:, T:NT], data1=rr[:, T:NT], initial=0.0, op0=ALU.mult, op1=ALU.add),
                         reads=aa.d + rr.d, writes=hh.d)
                    chk(0.92)
                    g.op("dve", lambda e: e.tensor_copy(out=hcol[:, 0:16], in_=hh[:, T + 3:NT:4]), reads=hh.d, writes=hcol.d)
                    g.op("dve", lambda e: e.tensor_copy(out=hcol[:, 16:17], in_=hh[:, T - 1:T]), reads=hh.d, writes=hcol.d)
                    if not flags.get('noh3'):
                        g.op("pe", lambda e: e.transpose(out=pss[:, 128:256], in_=hcol[:, :], identity=ident_f[:]),
                             reads=hcol.d + ident_f.d, writes=pss.d)
                    if not flags.get('noh4'):
                        cp("dve", hst[:], pss[0:32, 128:256], pss.d, hst.d)
                    if not flags.get('noh1'):
                        g.dma("sp", o_srh[:, n_ * 128:(n_ + 1) * 128], hst[0:16, :], reads=hst.d)
                    if not flags.get('noh2'):
                        g.dma("sp", o_prh[0:1, n_ * 128:(n_ + 1) * 128], hst[16:17, :], reads=hst.d)
                    chk(0.95)
                    def emit_g(p, gi, t0, n):
                        if n == NS:
                            tq = tmp64[t64c[0] % 2]
                            t64c[0] += 1
                            cp("dve", tq[:, 0:n], p[:, 0:n], p.d, tq.d)
                            g.op("act", lambda e: e.activation(out=gg[:, t0:t0 + n], in_=tq[:, 0:n], func=AF.Gelu_apprx_tanh), reads=tq.d, writes=gg.d)
                        else:
                            g.op("act", lambda e: e.activation(out=gg[:, t0:t0 + n], in_=p[:, 0:n], func=AF.Gelu_apprx_tanh), reads=p.d, writes=gg.d)
                    f_layout(hT, wgr, emit_g)
                    g.op("dve", lambda e: e.tensor_tensor(out=ysb[:], in0=hh[:], in1=gg[:], op=ALU.mult), reads=hh.d + gg.d, writes=ysb.d)
                    g.dma("sp", yrT_d[n_], ysb[:], reads=ysb.d)
                g.flush()

        chk(2)
        with contextlib.ExitStack() as s2:
            zt = k.sb(s2, [128, NT], BF16, name="zt")
            g.op("dve", lambda e: e.memset(zt[:], 0.0), writes=zt.d)
            if flags.get("noatt") or flags.get("nosatt"):
                for h in range(8):
                    g.dma("sp", oT_d[h, :, T:NT], zt[:, T:NT], reads=zt.d)
            if flags.get("noatt"):
                for h in range(8):
                    g.dma("sp", oT_d[h, :, 0:T], zt[:, 0:T], reads=zt.d)
            for sq in range(16):
                g.dma("sp", o_swin[sq, 0:508, :], cwin[sq, 4:512, :])
            g.dma("sp", o_swin[:, 508:512, :], o_skv[2].rearrange("(s t) c -> s t c", t=4))
            g.flush()
        if not flags.get("noatt"):
          with contextlib.ExitStack() as s2:
            NEGF = -1.0e30
            regcache = {}

            def negreg(e):
                key = g.start
                if key not in regcache:
                    regcache[key] = e.to_reg(NEGF)
                return regcache[key]
            SL = [float(2.0 ** (-(h + 1))) for h in range(8)]
            qT = k.sb(s2, [128, 8, T], BF16, name="qT")
            ksT = k.sb(s2, [128, 2, T], BF16, name="ksT")
            kwT = k.sb(s2, [128, 2, T], BF16, name="kwT")
            kcT = k.sb(s2, [128, 4, T], BF16, name="kcT")
            vs = k.sb(s2, [128, 16, 2, 128], BF16, name="vs")
            vw = k.sb(s2, [128, 16, 2, 128], BF16, name="vw")
            gn = k.sb(s2, [128, 16, 24], F32, name="gn")
            cw = k.sb(s2, [128, 2, 32, 128], BF16, name="cw")
            cpr = k.sb(s2, [32, 2, 128], F32, name="cpr")
            cpT = k.sb(s2, [128, 2, 32], BF16, name="cpT")
            m_sb = k.sb(s2, [128, 32], BF16, name="m_sb")
            m_f = k.sb(s2, [128, 32], F32, name="m_f")
            npos_i = k.sb(s2, [128, T], I32, name="nposi")
            npos = k.sb(s2, [128, T], F32, name="npos")
            cpos = k.sb(s2, [128, 128], F32, name="cpos")
            kcK = k.sb(s2, [128, 2, 128], BF16, name="kcK")
            kcV = k.sb(s2, [128, 2, 128], BF16, name="kcV")
            pbK = k.sb(s2, [128, 1], F32, name="pbK")
            pbV = k.sb(s2, [1, 128], BF16, name="pbV")
            ones_r = k.sb(s2, [1, 128], BF16, name="onesr")
            oT_sb = k.sb(s2, [128, 8, T], BF16, nd=8, name="oTsb")
            g.dma("sp", qT[:], qT_d[:, :, 0:T].rearrange("h p t -> p h t"), writes=qT.d)
            g.dma("sp", ksT[:], ksT_d[:, :, 0:T].rearrange("h p t -> p h t"), writes=ksT.d)
            g.dma("sp", kwT[:], kwT_d[:, :, 0:T].rearrange("h p t -> p h t"), writes=kwT.d)
            g.dma("sp", kcT[:], kcT_d[:, :, 0:T].rearrange("h p t -> p h t"), writes=kcT.d)
            for gg_ in range(2):
                g.dma("pool", vs[:, :, gg_, :], o_pkv[1, :, 256 + gg_ * 128:384 + gg_ * 128].rearrange("(kt p) d -> p kt d", p=128), writes=vs.d)
                g.dma("pool", vw[:, :, gg_, :], o_pkv[2, :, 256 + gg_ * 128:384 + gg_ * 128].rearrange("(kt p) d -> p kt d", p=128), writes=vw.d)
            g.dma("sp", gn[:], gn_d[0:T, :].rearrange("(qt p) c -> p qt c", p=128), writes=gn.d)
            for c_ in range(2):
                for lh in range(2):
                    g.dma("pool", cw[:, c_, lh * 16:(lh + 1) * 16, :], cmp_w[c_, lh * 16:(lh + 1) * 16].rearrange("l d e -> d l e"), writes=cw.d)
            g.dma("sp", cpr[:], cmp_pos.rearrange("c l d -> l c d"), writes=cpr.d)
            g.dma("sp", m_f[0:127, :], m_c2s, writes=m_f.d)
            g.op("dve", lambda e: e.tensor_copy(out=m_sb[0:127, :], in_=m_f[0:127, :]), reads=m_f.d, writes=m_sb.d)
            g.op("pool", lambda e: e.iota(npos_i[:], pattern=[[1, T]], base=0, channel_multiplier=0), writes=npos_i.d)
            g.op("dve", lambda e: e.tensor_copy(out=npos[:], in_=npos_i[:]), reads=npos_i.d, writes=npos.d)
            g.op("dve", lambda e: e.tensor_scalar(out=cpos[:, 0:127], in0=npos[:, 0:127], scalar1=16.0, scalar2=31.0, op0=ALU.mult, op1=ALU.add),
                 reads=npos.d, writes=cpos.d)
            g.op("dve", lambda e: e.memset(ones_r[:], 1.0), writes=ones_r.d)

            psA = k.ps(s2, [128, 512], F32)
            psI = k.ps(s2, [128, 512], F32)
            psO = k.ps(s2, [128, 512], F32)
            psO2 = k.ps(s2, [128, 512], F32)
            psS = [k.ps(s2, [128, 512], F32) for _ in range(2)]
            psT = k.ps(s2, [128, 8, 128], BF16, nd=8)
            tctr = [0]
            sctr2 = [0]

            for c_ in range(2):
                g.op("pe", lambda e, c_=c_: e.transpose(out=psA[:, c_ * 32:c_ * 32 + 32], in_=cpr[0:32, c_, :], identity=ident_f[0:32, 0:32]),
                     reads=cpr.d + ident_f.d, writes=psA.d)
            cp("dve", cpT[:], psA[:, 0:64].rearrange("p (c l) -> p c l", l=32), psA.d, cpT.d)
            for l in range(32):
                g.op("pe", lambda e, l=l: e.matmul(psI[:, 0:1], lhsT=cw[:, 0, l, :], rhs=cpT[:, 0, l:l + 1], start=(l == 0), stop=(l == 31)),
                     reads=cw.d + cpT.d, writes=psI.d)
            cp("dve", pbK[:], psI[:, 0:1], psI.d, pbK.d)
            for l in range(32):
                g.op("pe", lambda e, l=l: e.matmul(psO[0:1, 0:128], lhsT=cpT[:, 1, l:l + 1], rhs=cw[:, 1, l, :], start=(l == 0), stop=(l == 31)),
                     reads=cw.d + cpT.d, writes=psO.d)
            cp("dve", pbV[:], psO[0:1, 0:128], psO.d, pbV.d)
            for gg_ in range(2):
                for l in range(32):
                    off = l if l < 16 else l
                    g.op("pe", lambda e, l=l, gg_=gg_, off=off: e.matmul(psA[:, 0:127], lhsT=cw[:, 0, l, :], rhs=kcT[:, gg_, off:off + 16 * 126 + 1:16],
                                                                      start=(l == 0), stop=(l == 31)), reads=cw.d + kcT.d, writes=psA.d)
                g.op("dve", lambda e, gg_=gg_: e.tensor_scalar(out=kcK[:, gg_, 0:127], in0=psA[:, 0:127], scalar1=pbK[:, 0:1], scalar2=None, op0=ALU.add),
                     reads=psA.d + pbK.d, writes=kcK.d)
                for l in range(32):
                    g.op("pe", lambda e, l=l, gg_=gg_: e.matmul(psO[0:127, 0:128], lhsT=kcT[:, 2 + gg_, l:l + 16 * 126 + 1:16], rhs=cw[:, 1, l, :],
                                                              start=(l == 0), stop=False), reads=cw.d + kcT.d, writes=psO.d)
                g.op("pe", lambda e: e.matmul(psO[0:127, 0:128], lhsT=ones_r[0:1, 0:127], rhs=pbV[0:1, :], start=False, stop=True),
                     reads=ones_r.d + pbV.d, writes=psO.d)
                cp("dve", kcV[0:127, gg_, :], psO[0:127, 0:128], psO.d, kcV.d)

            lmask = k.sb(s2, [128, 128], F32, name="lmask")
            wmask = k.sb(s2, [128, 128], F32, name="wmask")
            g.op("dve", lambda e: e.memset(lmask[:], 0.0), writes=lmask.d)
            g.op("pool", lambda e: e.affine_select(out=lmask[:], in_=lmask[:], pattern=[[-1, 128]], compare_op=ALU.is_ge, fill=negreg(e), base=0, channel_multiplier=1),
                 reads=lmask.d, writes=lmask.d)
            g.op("pe", lambda e: e.transpose(out=psA[:, 0:128], in_=lmask[:, :], identity=ident_f[:, :]), reads=lmask.d + ident_f.d, writes=psA.d)
            cp("dve", wmask[:], psA[:, 0:128], psA.d, wmask.d)
            xs_ = [k.sb(s2, [128, T], F32, name="xs") for _ in range(2)]
            pb_ = [k.sb(s2, [128, T], BF16, name="pb") for _ in range(2)]
            pT_ = [k.sb(s2, [128, 128], BF16, name="pT") for _ in range(4)]
            sm_ = [k.sb(s2, [128, 8], F32, name="sm") for _ in range(4)]
            oc_sb = k.sb(s2, [128, 4, 128], F32, name="ocsb")
            o_bf = k.sb(s2, [128, 4, 128], BF16, name="obf")
            sc = k.sb(s2, [128, 32], F32, name="sc")
            cand = k.sb(s2, [128, 32], F32, name="cand")
            top8 = k.sb(s2, [128, 8], F32, name="top8")
            negb = k.sb(s2, [128, 32], F32, name="negb")
            g.op("dve", lambda e: e.memset(top8[:], 0.0), writes=top8.d)
            selb = k.sb(s2, [128, 32, 1], F32, name="selb")
            xctr = [0]
            ptc = [0]
            smc = [0]

            def softmax_rows(x, nk, gate_ap):
                sm = sm_[smc[0] % 4]
                smc[0] += 1
                pb = pb_[xctr[0] % 2]
                g.op("dve", lambda e: e.reduce_max(out=sm[:, 0:1], in_=x[:, 0:nk], axis=AX.X), reads=x.d, writes=sm.d)
                g.op("dve", lambda e: e.tensor_scalar(out=sm[:, 1:2], in0=sm[:, 0:1], scalar1=-1.0e20, scalar2=-1.0, op0=ALU.max, op1=ALU.mult),
                     reads=sm.d, writes=sm.d)
                g.op("act", lambda e: e.activation(out=pb[:, 0:nk], in_=x[:, 0:nk], func=AF.Exp, bias=sm[:, 1:2], accum_out=sm[:, 2:3]),
                     reads=x.d + sm.d, writes=pb.d + sm.d)
                g.op("dve", lambda e: e.tensor_scalar(out=sm[:, 3:4], in0=sm[:, 2:3], scalar1=1.0e-30, scalar2=None, op0=ALU.max), reads=sm.d, writes=sm.d)
                g.op("dve", lambda e: e.reciprocal(out=sm[:, 3:4], in_=sm[:, 3:4]), reads=sm.d, writes=sm.d)
                if gate_ap is not None:
                    g.op("dve", lambda e: e.tensor_tensor(out=sm[:, 3:4], in0=sm[:, 3:4], in1=gate_ap, op=ALU.mult), reads=sm.d + gn.d, writes=sm.d)
                g.op("dve", lambda e: e.tensor_scalar(out=pb[:, 0:nk], in0=pb[:, 0:nk], scalar1=sm[:, 3:4], scalar2=None, op0=ALU.mult),
                     reads=pb.d + sm.d, writes=pb.d)
                return pb

            def transpose_chunk(pb, c0, nkc):
                slot = tctr[0] % 8
                tctr[0] += 1
                pt = pT_[ptc[0] % 4]
                ptc[0] += 1
                g.op("pe", lambda e: e.transpose(out=psT[0:nkc, slot, :], in_=pb[:, c0:c0 + nkc], identity=ident_b[:, :]),
                     reads=pb.d + ident_b.d, writes=[psT.d[slot]])
                cp("act" if tctr[0] % 2 else "dve", pt[0:nkc, :], psT[0:nkc, slot, :], [psT.d[slot]], pt.d)
                return pt

            def scores(x, lhsT, kT_ap_fn, nk, posap, slope):
                for b0 in range(0, nk, 512):
                    nb = min(512, nk - b0)
                    ps_ = psS[sctr2[0] % 2]
                    sctr2[0] += 1
                    g.op("pe", lambda e, ps_=ps_, b0=b0, nb=nb: e.matmul(ps_[:, 0:nb], lhsT=lhsT, rhs=kT_ap_fn(b0, nb), start=True, stop=True),
                         reads=qT.d + ksT.d + kwT.d + kcK.d, writes=ps_.d)
                    g.op("dve", lambda e, ps_=ps_, b0=b0, nb=nb: e.scalar_tensor_tensor(
                        out=x[:, b0:b0 + nb], in0=posap(b0, nb), scalar=slope, in1=ps_[:, 0:nb], op0=ALU.mult, op1=ALU.add),
                        reads=ps_.d + npos.d + cpos.d, writes=x.d)

            for i in range(flags.get("nqt", 16)):
                q0 = i * 128
                for gg_ in range(2):
                    for p_ in range(4):
                        h = gg_ * 4 + p_
                        x = xs_[xctr[0] % 2]
                        lq = qT[:, h, q0:q0 + 128]
                        scores(x, lq, lambda b0, nb, gg_=gg_: kcK[:, gg_, b0:b0 + nb], 127, lambda b0, nb: cpos[:, b0:b0 + nb], SL[h])
                        g.op("pool", lambda e, x=x, i=i: e.affine_select(out=x[:, 0:127], in_=x[:, 0:127], pattern=[[-16, 127]], compare_op=ALU.is_ge,
                                                                       fill=negreg(e), base=128 * i - 31, channel_multiplier=1), reads=x.d, writes=x.d)
                        pb = softmax_rows(x, 127, None)
                        xctr[0] += 1
                        pt = transpose_chunk(pb, 0, 127)
                        g.op("pe", lambda e, pt=pt, p_=p_, gg_=gg_: e.matmul(psO[:, p_ * 128:(p_ + 1) * 128], lhsT=pt[0:127, :], rhs=kcV[0:127, gg_, :], start=True, stop=True),
                             reads=pt.d + kcV.d, writes=psO.d)
                        g.op("pe", lambda e, pt=pt, p_=p_: e.matmul(psI[:, 0:32], lhsT=pt[0:127, :], rhs=m_sb[0:127, :], start=(p_ == 0), stop=(p_ == 3)),
                             reads=pt.d + m_sb.d, writes=psI.d)
                        g.op("dve", lambda e, p_=p_, h=h, i=i: e.tensor_scalar(out=oc_sb[:, p_, :], in0=psO[:, p_ * 128:(p_ + 1) * 128], scalar1=gn[:, i, h:h + 1], scalar2=None, op0=ALU.mult),
                             reads=psO.d + gn.d, writes=oc_sb.d)
                    cp("dve", sc[:], psI[:, 0:32], psI.d, sc.d)
                    for half in range(2):
                        g.op("dve", lambda e, half=half, i=i: e.tensor_scalar(out=cand[half * 64:half * 64 + 64, :], in0=npos[half * 64:half * 64 + 64, 0:32],
                                                                            scalar1=float(2 * i + half), scalar2=None, op0=ALU.is_lt), reads=npos.d, writes=cand.d)
                    g.op("dve", lambda e: e.tensor_tensor(out=sc[:], in0=sc[:], in1=cand[:], op=ALU.mult), reads=sc.d + cand.d, writes=sc.d)
                    g.op("dve", lambda e: e.tensor_scalar(out=top8[:, 0:1], in0=top8[:, 0:1], scalar1=0.0, scalar2=None, op0=ALU.mult), reads=top8.d, writes=top8.d)
                    g.op("dve", lambda e: e.scalar_tensor_tensor(out=sc[:], in0=cand[:], scalar=-1.0, in1=sc[:], op0=ALU.add, op1=ALU.add) if False else
                         e.tensor_scalar(out=negb[:], in0=cand[:], scalar1=-1.0, scalar2=1.0e30, op0=ALU.add, op1=ALU.mult), reads=cand.d, writes=negb.d)
                    g.op("dve", lambda e: e.tensor_tensor(out=sc[:], in0=sc[:], in1=negb[:], op=ALU.add), reads=sc.d + negb.d, writes=sc.d)
                    if i >= 1:
                        g.op("dve", lambda e: e.memset(sc[0:64, 0:1], 1.0e9), reads=cand.d, writes=sc.d)
                    g.op("dve", lambda e: e.memset(sc[64:128, 0:1], 1.0e9), reads=cand.d, writes=sc.d)
                    g.op("dve", lambda e: e.max(out=top8[:], in_=sc[:]), reads=sc.d, writes=top8.d)
                    g.op("dve", lambda e: e.tensor_scalar(out=sc[:], in0=sc[:], scalar1=top8[:, 6:7], scalar2=None, op0=ALU.is_ge), reads=sc.d + top8.d, writes=sc.d)
                    g.op("dve", lambda e: e.tensor_tensor(out=sc[:], in0=sc[:], in1=cand[:], op=ALU.mult), reads=sc.d + cand.d, writes=sc.d)
                    g.op("dve", lambda e: e.tensor_scalar(out=selb[:, :, 0], in0=sc[:], scalar1=-1.0, scalar2=1.0e30, op0=ALU.add, op1=ALU.mult), reads=sc.d, writes=selb.d)
                    g.op("dve", lambda e, i=i: e.memset(selb[0:64, 2 * i:2 * i + 1, :], 0.0), writes=selb.d)
                    g.op("dve", lambda e, i=i: e.memset(selb[64:128, 2 * i + 1:2 * i + 2, :], 0.0), writes=selb.d)
                    for p_ in range(4):
                        h = gg_ * 4 + p_
                        lq = qT[:, h, q0:q0 + 128]
                        nk = (i + 1) * 128
                        nblk = 2 * (i + 1)
                        x = xs_[xctr[0] % 2]
                        scores(x, lq, lambda b0, nb, gg_=gg_: ksT[:, gg_, b0:b0 + nb], nk, lambda b0, nb: npos[:, b0:b0 + nb], SL[h])
                        g.op("dve", lambda e, x=x, nk=nk, nblk=nblk: e.tensor_tensor(
                            out=x[:, 0:nk].rearrange("p (b s) -> p b s", s=64), in0=x[:, 0:nk].rearrange("p (b s) -> p b s", s=64),
                            in1=selb[:, 0:nblk, :].to_broadcast([128, nblk, 64]), op=ALU.add), reads=x.d + selb.d, writes=x.d)
                        g.op("pool", lambda e, x=x, i=i, nk=nk: e.affine_select(out=x[:, 0:nk], in_=x[:, 0:nk], pattern=[[-1, nk]], compare_op=ALU.is_ge,
                                                                              fill=negreg(e), base=128 * i, channel_multiplier=1), reads=x.d, writes=x.d)
                        pb = softmax_rows(x, nk, gn[:, i, 8 + h:9 + h])
                        xctr[0] += 1
                        for kt in range(i + 1):
                            pt = transpose_chunk(pb, kt * 128, 128)
                            g.op("pe", lambda e, pt=pt, p_=p_, gg_=gg_, kt=kt: e.matmul(psO2[:, p_ * 128:(p_ + 1) * 128], lhsT=pt[:, :], rhs=vs[:, kt, gg_, :],
                                                                                start=(kt == 0), stop=False), reads=pt.d + vs.d, writes=psO2.d)
                        kt0 = max(0, i - 4)
                        k0 = kt0 * 128
                        nkw = (i + 1) * 128 - k0
                        x = xs_[xctr[0] % 2]
                        scores(x, lq, lambda b0, nb, gg_=gg_, k0=k0: kwT[:, gg_, k0 + b0:k0 + b0 + nb], nkw, lambda b0, nb, k0=k0: npos[:, k0 + b0:k0 + b0 + nb], SL[h])
                        g.op("pool", lambda e, x=x, i=i, nkw=nkw, k0=k0: e.affine_select(out=x[:, 0:nkw], in_=x[:, 0:nkw], pattern=[[-1, nkw]], compare_op=ALU.is_ge,
                                                                                       fill=negreg(e), base=128 * i - k0, channel_multiplier=1), reads=x.d, writes=x.d)
                        if i >= 4:
                            g.op("dve", lambda e, x=x: e.tensor_tensor(out=x[:, 0:128], in0=x[:, 0:128], in1=wmask[:, :], op=ALU.add), reads=x.d + wmask.d, writes=x.d)
                        pb = softmax_rows(x, nkw, gn[:, i, 16 + h:17 + h])
                        xctr[0] += 1
                        nkt = i + 1 - kt0
                        for kk in range(nkt):
                            pt = transpose_chunk(pb, kk * 128, 128)
                            g.op("pe", lambda e, pt=pt, p_=p_, gg_=gg_, kk=kk, kt0=kt0, nkt=nkt: e.matmul(
                                psO2[:, p_ * 128:(p_ + 1) * 128], lhsT=pt[:, :], rhs=vw[:, kt0 + kk, gg_, :], start=False, stop=(kk == nkt - 1)),
                                reads=pt.d + vw.d, writes=psO2.d)
                        g.op("dve", lambda e, p_=p_: e.tensor_tensor(out=o_bf[:, p_, :], in0=psO2[:, p_ * 128:(p_ + 1) * 128], in1=oc_sb[:, p_, :], op=ALU.add),
                             reads=psO2.d + oc_sb.d, writes=o_bf.d)
                        slot = tctr[0] % 8
                        tctr[0] += 1
                        g.op("pe", lambda e, p_=p_, slot=slot: e.transpose(out=psT[:, slot, :], in_=o_bf[:, p_, :], identity=ident_b[:, :]),
                             reads=o_bf.d + ident_b.d, writes=[psT.d[slot]])
                        cp("act", oT_sb[:, h, q0:q0 + 128], psT[:, slot, :], [psT.d[slot]], [oT_sb.d[h]])
            for h in range(8):
                g.dma("sp", oT_d[h, :, 0:T], oT_sb[:, h, :], reads=[oT_sb.d[h]])
            g.flush()

        if not flags.get("noatt") and not flags.get("nosatt"):
          with contextlib.ExitStack() as s2:
            NEGF = -1.0e30
            SL = [float(2.0 ** (-(h + 1))) for h in range(8)]
            regcache2 = {}

            def negreg2(e):
                key = g.start
                if key not in regcache2:
                    regcache2[key] = e.to_reg(NEGF)
                return regcache2[key]
            NK = T + 4
            cw = k.sb(s2, [128, 2, 32, 128], BF16, name="cw")
            cpr = k.sb(s2, [32, 2, 128], F32, name="cpr")
            cpT = k.sb(s2, [128, 2, 32], BF16, name="cpT")
            m_sb = k.sb(s2, [128, 32], BF16, name="m_sb")
            m_f = k.sb(s2, [128, 32], F32, name="m_f")
            npos_i = k.sb(s2, [128, NK], I32, name="nposi")
            npos = k.sb(s2, [128, NK], F32, name="npos")
            cpos = k.sb(s2, [128, 128], F32, name="cpos")
            pbK = k.sb(s2, [128, 1], F32, name="pbK")
            pbV = k.sb(s2, [1, 128], BF16, name="pbV")
            ones_r = k.sb(s2, [1, 128], BF16, name="onesr")
            lmask = k.sb(s2, [128, 128], F32, name="lmask")
            wmask = k.sb(s2, [128, 128], F32, name="wmask")
            ptab_i = k.sb(s2, [128, 256], I32, name="ptabi")
            ptab_f = k.sb(s2, [128, 256], F32, name="ptabf")
            pidx_i = k.sb(s2, [128, 1], I32, name="pidxi")
            pidx_f = k.sb(s2, [128, 1], F32, name="pidxf")
            idx_i = k.sb(s2, [128, 256], I32, name="idxi")
            qT_s = k.sb(s2, [128, 8, NS], BF16, name="qTs")
            oT_s = k.sb(s2, [128, 8, NS], BF16, name="oTs")
            cpg = k.sb(s2, [128, 16, 512], BF16, name="cpg")
            spg = k.sb(s2, [128, 16, 512], BF16, name="spg")
            wpg = k.sb(s2, [128, 4, 512], BF16, name="wpg")
            kcT_s = k.sb(s2, [128, 4, T], BF16, name="kcTs")
            ksT_s = k.sb(s2, [128, 2, NK], BF16, name="ksTs")
            kwT_s = k.sb(s2, [128, 2, 516], BF16, name="kwTs")
            vs_n = k.sb(s2, [4, 2, 128], BF16, name="vsn")
            vw_n = k.sb(s2, [4, 2, 128], BF16, name="vwn")
            gn_s = k.sb(s2, [4, 24], F32, name="gns")
            kcK = k.sb(s2, [128, 2, 128], BF16, name="kcK")
            kcV = k.sb(s2, [128, 2, 128], BF16, name="kcV")
            for c_ in range(2):
                for lh in range(2):
                    g.dma("pool", cw[:, c_, lh * 16:(lh + 1) * 16, :], cmp_w[c_, lh * 16:(lh + 1) * 16].rearrange("l d e -> d l e"), writes=cw.d)
            g.dma("sp", cpr[:], cmp_pos.rearrange("c l d -> l c d"), writes=cpr.d)
            g.dma("sp", m_f[0:127, :], m_c2s, writes=m_f.d)
            g.op("dve", lambda e: e.tensor_copy(out=m_sb[0:127, :], in_=m_f[0:127, :]), reads=m_f.d, writes=m_sb.d)
            g.op("pool", lambda e: e.iota(npos_i[:], pattern=[[1, NK]], base=0, channel_multiplier=0), writes=npos_i.d)
            g.op("dve", lambda e: e.tensor_copy(out=npos[:], in_=npos_i[:]), reads=npos_i.d, writes=npos.d)
            g.op("dve", lambda e: e.tensor_scalar(out=cpos[:, 0:127], in0=npos[:, 0:127], scalar1=16.0, scalar2=31.0, op0=ALU.mult, op1=ALU.add),
                 reads=npos.d, writes=cpos.d)
            g.op("dve", lambda e: e.memset(ones_r[:], 1.0), writes=ones_r.d)
            g.dma("sp", qT_s[:], qT_d[:, :, T:NT].rearrange("h p t -> p h t"), writes=qT_s.d)
            g.dma("sp", ptab_i[:], ptab.partition_broadcast(128), writes=ptab_i.d)
            g.op("pool", lambda e: e.iota(pidx_i[:], pattern=[[0, 1]], base=0, channel_multiplier=1), writes=pidx_i.d)
            g.op("dve", lambda e: e.tensor_copy(out=ptab_f[:], in_=ptab_i[:]), reads=ptab_i.d, writes=ptab_f.d)
            g.op("dve", lambda e: e.tensor_copy(out=pidx_f[:], in_=pidx_i[:]), reads=pidx_i.d, writes=pidx_f.d)
            g.op("dve", lambda e: e.tensor_scalar(out=ptab_f[:], in0=ptab_f[:], scalar1=128.0, scalar2=pidx_f[:, 0:1], op0=ALU.mult, op1=ALU.add),
                 reads=ptab_f.d + pidx_f.d, writes=ptab_f.d)
            g.op("dve", lambda e: e.tensor_copy(out=idx_i[:], in_=ptab_f[:]), reads=ptab_f.d, writes=idx_i.d)

            psA = k.ps(s2, [128, 512], F32)
            psI = k.ps(s2, [128, 512], F32)
            psO = k.ps(s2, [128, 512], F32)
            psO2 = k.ps(s2, [128, 512], F32)
            psS = [k.ps(s2, [128, 512], F32) for _ in range(1)]
            psT = k.ps(s2, [128, 8, 128], BF16, nd=8)
            psU = k.ps(s2, [128, 8, 128], BF16)
            tbank = [psT, psU]
            tctr = [0]
            sctr2 = [0]
            g.op("dve", lambda e: e.memset(lmask[:], 0.0), writes=lmask.d)
            g.op("pool", lambda e: e.affine_select(out=lmask[:], in_=lmask[:], pattern=[[-1, 128]], compare_op=ALU.is_ge, fill=negreg2(e), base=0, channel_multiplier=1),
                 reads=lmask.d, writes=lmask.d)
            g.op("pe", lambda e: e.transpose(out=psA[:, 0:128], in_=lmask[:, :], identity=ident_f[:, :]), reads=lmask.d + ident_f.d, writes=psA.d)
            cp("dve", wmask[:], psA[:, 0:128], psA.d, wmask.d)
            for c_ in range(2):
                g.op("pe", lambda e, c_=c_: e.transpose(out=psA[:, c_ * 32:c_ * 32 + 32], in_=cpr[0:32, c_, :], identity=ident_f[0:32, 0:32]),
                     reads=cpr.d + ident_f.d, writes=psA.d)
            cp("dve", cpT[:], psA[:, 0:64].rearrange("p (c l) -> p c l", l=32), psA.d, cpT.d)
            for l in range(32):
                g.op("pe", lambda e, l=l: e.matmul(psI[:, 0:1], lhsT=cw[:, 0, l, :], rhs=cpT[:, 0, l:l + 1], start=(l == 0), stop=(l == 31)),
                     reads=cw.d + cpT.d, writes=psI.d)
            cp("dve", pbK[:], psI[:, 0:1], psI.d, pbK.d)
            for l in range(32):
                g.op("pe", lambda e, l=l: e.matmul(psO[0:1, 0:128], lhsT=cpT[:, 1, l:l + 1], rhs=cw[:, 1, l, :], start=(l == 0), stop=(l == 31)),
                     reads=cw.d + cpT.d, writes=psO.d)
            cp("dve", pbV[:], psO[0:1, 0:128], psO.d, pbV.d)

            xs_ = [k.sb(s2, [4, NK], F32, name="xs") for _ in range(2)]
            pb_ = [k.sb(s2, [4, NK], BF16, name="pb") for _ in range(2)]
            pT_ = [k.sb(s2, [128, 4], BF16, name="pT") for _ in range(4)]
            sm_ = [k.sb(s2, [4, 8], F32, name="sm") for _ in range(4)]
            oc_sb = k.sb(s2, [4, 4, 128], F32, name="ocsb")
            o_bf = k.sb(s2, [4, 4, 128], BF16, name="obf")
            sc = k.sb(s2, [4, 32], F32, name="sc")
            top8 = k.sb(s2, [4, 8], F32, name="top8")
            selb = k.sb(s2, [4, 32, 1], F32, name="selb")
            xctr = [0]
            ptc = [0]
            smc = [0]
            NQ = 4

            def softmax_rows(x, nk, gate_ap):
                sm = sm_[smc[0] % 4]
                smc[0] += 1
                pb = pb_[xctr[0] % 2]
                g.op("dve", lambda e: e.reduce_max(out=sm[:, 0:1], in_=x[:, 0:nk], axis=AX.X), reads=x.d, writes=sm.d)
                g.op("dve", lambda e: e.tensor_scalar(out=sm[:, 1:2], in0=sm[:, 0:1], scalar1=-1.0e20, scalar2=-1.0, op0=ALU.max, op1=ALU.mult),
                     reads=sm.d, writes=sm.d)
                g.op("act", lambda e: e.activation(out=pb[:, 0:nk], in_=x[:, 0:nk], func=AF.Exp, bias=sm[:, 1:2], accum_out=sm[:, 2:3]),
                     reads=x.d + sm.d, writes=pb.d + sm.d)
                g.op("dve", lambda e: e.tensor_scalar(out=sm[:, 3:4], in0=sm[:, 2:3], scalar1=1.0e-30, scalar2=None, op0=ALU.max), reads=sm.d, writes=sm.d)
                g.op("dve", lambda e: e.reciprocal(out=sm[:, 3:4], in_=sm[:, 3:4]), reads=sm.d, writes=sm.d)
                if gate_ap is not None:
                    g.op("dve", lambda e: e.tensor_tensor(out=sm[:, 3:4], in0=sm[:, 3:4], in1=gate_ap, op=ALU.mult), reads=sm.d + gn_s.d, writes=sm.d)
                g.op("dve", lambda e: e.tensor_scalar(out=pb[:, 0:nk], in0=pb[:, 0:nk], scalar1=sm[:, 3:4], scalar2=None, op0=ALU.mult),
                     reads=pb.d + sm.d, writes=pb.d)
                return pb

            def transpose_chunk(pb, c0, nkc):
                slot = tctr[0] % 8
                tctr[0] += 1
                pt = pT_[ptc[0] % 4]
                ptc[0] += 1
                g.op("pe", lambda e: e.transpose(out=psT[0:nkc, slot, 0:NQ], in_=pb[0:NQ, c0:c0 + nkc], identity=ident_b[0:NQ, 0:NQ]),
                     reads=pb.d + ident_b.d, writes=[psT.d[slot]])
                cp("dve", pt[0:nkc, :], psT[0:nkc, slot, 0:NQ], [psT.d[slot]], pt.d)
                return pt

            def scores(x, lhsT, kT_ap_fn, nk, posap, slope, rd):
                for b0 in range(0, nk, 512):
                    nb = min(512, nk - b0)
                    ps_ = psS[0]
                    sctr2[0] += 1
                    g.op("pe", lambda e, ps_=ps_, b0=b0, nb=nb: e.matmul(ps_[0:NQ, 0:nb], lhsT=lhsT, rhs=kT_ap_fn(b0, nb), start=True, stop=True),
                         reads=qT_s.d + rd, writes=ps_.d)
                    g.op("dve", lambda e, ps_=ps_, b0=b0, nb=nb: e.scalar_tensor_tensor(
                        out=x[:, b0:b0 + nb], in0=posap(b0, nb), scalar=slope, in1=ps_[0:NQ, 0:nb], op0=ALU.mult, op1=ALU.add),
                        reads=ps_.d + npos.d + cpos.d, writes=x.d)

            sst('A')
            for sq in range(flags.get("nsq", 16)):
                for j in range(0 if flags.get("nogather") else 16):
                    col = sq * 16 + j
                    g.op("pool", lambda e, j=j, col=col: e.indirect_dma_start(
                        out=cpg[:, j, :], out_offset=None, in_=ccmp[:, :], in_offset=bass.IndirectOffsetOnAxis(ap=idx_i[:, col:col + 1], axis=0)),
                        reads=idx_i.d, writes=cpg.d, dma=True)
                    g.op("pool", lambda e, j=j, col=col: e.indirect_dma_start(
                        out=spg[:, j, :], out_offset=None, in_=cslc[:, :], in_offset=bass.IndirectOffsetOnAxis(ap=idx_i[:, col:col + 1], axis=0)),
                        reads=idx_i.d, writes=spg.d, dma=True)
                g.dma("pool", wpg[:], cwin[sq].rearrange("(kt p) c -> p kt c", p=128), writes=wpg.d)
                g.dma("sp", ksT_s[:, :, T:NK], ksT_d[:, :, T + sq * 4:T + sq * 4 + 4].rearrange("h p t -> p h t"), writes=ksT_s.d)
                g.dma("sp", kwT_s[:, :, 512:516], kwT_d[:, :, T + sq * 4:T + sq * 4 + 4].rearrange("h p t -> p h t"), writes=kwT_s.d)
                g.dma("pool", vs_n[:], o_skv[1, sq * 4:sq * 4 + 4, 256:512].rearrange("t (g d) -> t g d", g=2), writes=vs_n.d)
                g.dma("pool", vw_n[:], o_skv[2, sq * 4:sq * 4 + 4, 256:512].rearrange("t (g d) -> t g d", g=2), writes=vw_n.d)
                g.dma("sp", gn_s[:], gn_d[T + sq * 4:T + sq * 4 + 4, :], writes=gn_s.d)
                sst('B')
                for j in range(flags.get("ntc", 16)):
                    tb = tbank[j % 2]
                    for cgi in range(4):
                        g.op("pe", lambda e, j=j, cgi=cgi, tb=tb: e.transpose(out=tb[:, cgi, :], in_=cpg[:, j, cgi * 128:(cgi + 1) * 128], identity=ident_b[:, :]),
                             reads=cpg.d + ident_b.d, writes=tb.d)
                    cp("dve", kcT_s[:, :, j * 128:(j + 1) * 128], tb[:, 0:4, :], tb.d, kcT_s.d)
                for j in range(flags.get("nts", 16)):
                    tb = tbank[j % 2]
                    for gi2 in range(2):
                        g.op("pe", lambda e, j=j, gi2=gi2, tb=tb: e.transpose(out=tb[:, gi2, :], in_=spg[:, j, gi2 * 128:(gi2 + 1) * 128], identity=ident_b[:, :]),
                             reads=spg.d + ident_b.d, writes=tb.d)
                    cp("dve", ksT_s[:, :, j * 128:(j + 1) * 128], tb[:, 0:2, :], tb.d, ksT_s.d)
                for j in range(flags.get("ntw", 4)):
                    tb = tbank[j % 2]
                    for gi2 in range(2):
                        g.op("pe", lambda e, j=j, gi2=gi2, tb=tb: e.transpose(out=tb[:, gi2, :], in_=wpg[:, j, gi2 * 128:(gi2 + 1) * 128], identity=ident_b[:, :]),
                             reads=wpg.d + ident_b.d, writes=tb.d)
                    cp("dve", kwT_s[:, :, j * 128:(j + 1) * 128], tb[:, 0:2, :], tb.d, kwT_s.d)
                sst('C')
                for gg_ in range(2):
                    for l in range(32):
                        g.op("pe", lambda e, l=l, gg_=gg_: e.matmul(psA[:, 0:127], lhsT=cw[:, 0, l, :], rhs=kcT_s[:, gg_, l:l + 16 * 126 + 1:16],
                                                                  start=(l == 0), stop=(l == 31)), reads=cw.d + kcT_s.d, writes=psA.d)
                    g.op("dve", lambda e, gg_=gg_: e.tensor_scalar(out=kcK[:, gg_, 0:127], in0=psA[:, 0:127], scalar1=pbK[:, 0:1], scalar2=None, op0=ALU.add),
                         reads=psA.d + pbK.d, writes=kcK.d)
                    for l in range(32):
                        g.op("pe", lambda e, l=l, gg_=gg_: e.matmul(psO[0:127, 0:128], lhsT=kcT_s[:, 2 + gg_, l:l + 16 * 126 + 1:16], rhs=cw[:, 1, l, :],
                                                                  start=(l == 0), stop=False), reads=cw.d + kcT_s.d, writes=psO.d)
                    g.op("pe", lambda e: e.matmul(psO[0:127, 0:128], lhsT=ones_r[0:1, 0:127], rhs=pbV[0:1, :], start=False, stop=True),
                         reads=ones_r.d + pbV.d, writes=psO.d)
                    cp("dve", kcV[0:127, gg_, :], psO[0:127, 0:128], psO.d, kcV.d)
                sst('D')
                q0 = sq * 4
                for gg_ in range(2):
                    for p_ in range(4):
                        h = gg_ * 4 + p_
                        x = xs_[xctr[0] % 2]
                        lq = qT_s[:, h, q0:q0 + 4]
                        scores(x, lq, lambda b0, nb, gg_=gg_: kcK[:, gg_, b0:b0 + nb], 127, lambda b0, nb: cpos[0:NQ, b0:b0 + nb], SL[h], kcK.d)
                        pb = softmax_rows(x, 127, None)
                        xctr[0] += 1
                        pt = transpose_chunk(pb, 0, 127)
                        g.op("pe", lambda e, pt=pt, p_=p_, gg_=gg_: e.matmul(psO[0:NQ, p_ * 128:(p_ + 1) * 128], lhsT=pt[0:127, :], rhs=kcV[0:127, gg_, :], start=True, stop=True),
                             reads=pt.d + kcV.d, writes=psO.d)
                        g.op("pe", lambda e, pt=pt, p_=p_: e.matmul(psI[0:NQ, 0:32], lhsT=pt[0:127, :], rhs=m_sb[0:127, :], start=(p_ == 0), stop=(p_ == 3)),
                             reads=pt.d + m_sb.d, writes=psI.d)
                        g.op("dve", lambda e, p_=p_, h=h: e.tensor_scalar(out=oc_sb[:, p_, :], in0=psO[0:NQ, p_ * 128:(p_ + 1) * 128], scalar1=gn_s[:, h:h + 1], scalar2=None, op0=ALU.mult),
                             reads=psO.d + gn_s.d, writes=oc_sb.d)
                    sst('E')
                    cp("dve", sc[:], psI[0:NQ, 0:32], psI.d, sc.d)
                    g.op("dve", lambda e: e.memset(sc[:, 0:1], 1.0e9), writes=sc.d)
                    g.op("dve", lambda e: e.max(out=top8[:], in_=sc[:]), reads=sc.d, writes=top8.d)
                    g.op("dve", lambda e: e.tensor_scalar(out=sc[:], in0=sc[:], scalar1=top8[:, 6:7], scalar2=None, op0=ALU.is_ge), reads=sc.d + top8.d, writes=sc.d)
                    g.op("dve", lambda e: e.tensor_scalar(out=selb[:, :, 0], in0=sc[:], scalar1=-1.0, scalar2=1.0e30, op0=ALU.add, op1=ALU.mult), reads=sc.d, writes=selb.d)
                    for p_ in range(4):
                        h = gg_ * 4 + p_
                        lq = qT_s[:, h, q0:q0 + 4]
                        x = xs_[xctr[0] % 2]
                        scores(x, lq, lambda b0, nb, gg_=gg_: ksT_s[:, gg_, b0:b0 + nb], NK, lambda b0, nb: npos[0:NQ, b0:b0 + nb], SL[h], ksT_s.d)
                        g.op("dve", lambda e, x=x: e.tensor_tensor(
                            out=x[:, 0:T].rearrange("p (b s) -> p b s", s=64), in0=x[:, 0:T].rearrange("p (b s) -> p b s", s=64),
                            in1=selb[:, :, :].to_broadcast([NQ, 32, 64]), op=ALU.add), reads=x.d + selb.d, writes=x.d)
                        g.op("dve", lambda e, x=x: e.tensor_tensor(out=x[:, T:NK], in0=x[:, T:NK], in1=lmask[0:NQ, 0:4], op=ALU.add), reads=x.d + lmask.d, writes=x.d)
                        pb = softmax_rows(x, NK, gn_s[:, 8 + h:9 + h])
                        xctr[0] += 1
                        for kt in range(16):
                            pt = transpose_chunk(pb, kt * 128, 128)
                            g.op("pe", lambda e, pt=pt, p_=p_, gg_=gg_, kt=kt: e.matmul(psO2[0:NQ, p_ * 128:(p_ + 1) * 128], lhsT=pt[:, :], rhs=spg[:, kt, 256 + gg_ * 128:384 + gg_ * 128],
                                                                                start=(kt == 0), stop=False), reads=pt.d + spg.d, writes=psO2.d)
                        pt = transpose_chunk(pb, T, 4)
                        g.op("pe", lambda e, pt=pt, p_=p_, gg_=gg_: e.matmul(psO2[0:NQ, p_ * 128:(p_ + 1) * 128], lhsT=pt[0:4, :], rhs=vs_n[0:4, gg_, :], start=False, stop=False),
                             reads=pt.d + vs_n.d, writes=psO2.d)
                        x = xs_[xctr[0] % 2]
                        scores(x, lq, lambda b0, nb, gg_=gg_: kwT_s[:, gg_, b0:b0 + nb], 516, lambda b0, nb: npos[0:NQ, T - 512 + b0:T - 512 + b0 + nb], SL[h], kwT_s.d)
                        g.op("dve", lambda e, x=x: e.tensor_tensor(out=x[:, 0:4], in0=x[:, 0:4], in1=wmask[0:NQ, 0:4], op=ALU.add), reads=x.d + wmask.d, writes=x.d)
                        g.op("dve", lambda e, x=x: e.tensor_tensor(out=x[:, 512:516], in0=x[:, 512:516], in1=lmask[0:NQ, 0:4], op=ALU.add), reads=x.d + lmask.d, writes=x.d)
                        pb = softmax_rows(x, 516, gn_s[:, 16 + h:17 + h])
                        xctr[0] += 1
                        for kk in range(4):
                            pt = transpose_chunk(pb, kk * 128, 128)
                            g.op("pe", lambda e, pt=pt, p_=p_, gg_=gg_, kk=kk: e.matmul(psO2[0:NQ, p_ * 128:(p_ + 1) * 128], lhsT=pt[:, :], rhs=wpg[:, kk, 256 + gg_ * 128:384 + gg_ * 128],
                                                                                start=False, stop=False), reads=pt.d + wpg.d, writes=psO2.d)
                        pt = transpose_chunk(pb, 512, 4)
                        g.op("pe", lambda e, pt=pt, p_=p_, gg_=gg_: e.matmul(psO2[0:NQ, p_ * 128:(p_ + 1) * 128], lhsT=pt[0:4, :], rhs=vw_n[0:4, gg_, :], start=False, stop=True),
                             reads=pt.d + vw_n.d, writes=psO2.d)
                        g.op("dve", lambda e, p_=p_: e.tensor_tensor(out=o_bf[:, p_, :], in0=psO2[0:NQ, p_ * 128:(p_ + 1) * 128], in1=oc_sb[:, p_, :], op=ALU.add),
                             reads=psO2.d + oc_sb.d, writes=o_bf.d)
                        slot = tctr[0] % 8
                        tctr[0] += 1
                        g.op("pe", lambda e, p_=p_, slot=slot: e.transpose(out=psT[:, slot, 0:NQ], in_=o_bf[:, p_, :], identity=ident_b[0:NQ, 0:NQ]),
                             reads=o_bf.d + ident_b.d, writes=[psT.d[slot]])
                        cp("dve", oT_s[:, h, q0:q0 + 4], psT[:, slot, 0:NQ], [psT.d[slot]], oT_s.d)
            g.dma("sp", oT_d[:, :, T:NT].rearrange("h p t -> p h t"), oT_s[:], reads=oT_s.d)
            g.flush()

        chk(3)
        with contextlib.ExitStack() as s3:
            mT = k.sb(s3, [128, 16, NT], BF16, name="mT")
            with contextlib.ExitStack() as sa:
                yrT = k.sb(sa, [128, 8, NT], BF16, name="yrT")
                oT = k.sb(sa, [128, 8, NT], BF16, name="oT")
                g.dma("sp", yrT[:], yrT_d.rearrange("c p t -> p c t"), writes=yrT.d)
                g.dma("sp", oT[:], oT_d.rearrange("c p t -> p c t"), writes=oT.d)
                wps = [k.sb(sa, [128, 8, 128], BF16, name="wps") for _ in range(4)]
                gms = [k.sb(sa, [128, NT], BF16, name="gms") for _ in range(4)]
                tmp = [k.sb(sa, [128, 512], F32, name="tmp") for _ in range(4)]
                psf = [k.ps(sa, [128, 512], F32) for _ in range(6)]
                pc = 0
                for j in range(16):
                    wr = wps[(2 * j) % 4]
                    wa_ = wps[(2 * j + 1) % 4]
                    g.dma("pool", wr[:], w_proj_rnn[:, j * 128:(j + 1) * 128].rearrange("(kc p) n -> p kc n", p=128), writes=wr.d)
                    g.dma("pool", wa_[:], w_proj_att[:, j * 128:(j + 1) * 128].rearrange("(kc p) n -> p kc n", p=128), writes=wa_.d)
                    gr_ = gms[(2 * j) % 4]
                    ga_ = gms[(2 * j + 1) % 4]
                    g.dma("sp", gr_[:], gm_d[j], writes=gr_.d)
                    g.dma("sp", ga_[:], gm_d[16 + j], writes=ga_.d)
                    for (t0, n) in TG:
                        p1 = psf[pc % 6]
                        p2 = psf[(pc + 1) % 6]
                        tA = tmp[pc % 4]
                        tB = tmp[(pc + 1) % 4]
                        pc += 2
                        for kc in range(8):
                            g.op("pe", lambda e, p1=p1, kc=kc, t0=t0, n=n, wr=wr: e.matmul(
                                p1[:, 0:n], lhsT=wr[:, kc, :], rhs=yrT[:, kc, t0:t0 + n], start=(kc == 0), stop=(kc == 7)),
                                reads=wr.d + yrT.d, writes=p1.d)
                        for kc in range(8):
                            g.op("pe", lambda e, p2=p2, kc=kc, t0=t0, n=n, wa_=wa_: e.matmul(
                                p2[:, 0:n], lhsT=wa_[:, kc, :], rhs=oT[:, kc, t0:t0 + n], start=(kc == 0), stop=(kc == 7)),
                                reads=wa_.d + oT.d, writes=p2.d)
                        g.op("dve", lambda e, p1=p1, tA=tA, t0=t0, n=n, gr_=gr_: e.tensor_tensor(
                            out=tA[:, 0:n], in0=p1[:, 0:n], in1=gr_[:, t0:t0 + n], op=ALU.mult), reads=p1.d + gr_.d, writes=tA.d)
                        g.op("dve", lambda e, p2=p2, tB=tB, t0=t0, n=n, ga_=ga_: e.tensor_tensor(
                            out=tB[:, 0:n], in0=p2[:, 0:n], in1=ga_[:, t0:t0 + n], op=ALU.mult), reads=p2.d + ga_.d, writes=tB.d)
                        g.op("pool", lambda e, tA=tA, tB=tB, t0=t0, n=n, j=j: e.tensor_tensor(
                            out=mT[:, j, t0:t0 + n], in0=tA[:, 0:n], in1=tB[:, 0:n], op=ALU.add), reads=tA.d + tB.d, writes=mT.d)
                g.flush()
            with contextlib.ExitStack() as sb_:
                wos = [k.sb(sb_, [128, 16, 512], BF16, name="wos") for _ in range(2)]
                xr_ = [k.sb(sb_, [128, 512], F32, name="xres") for _ in range(3)]
                so = [k.sb(sb_, [128, 512], F32, name="so") for _ in range(3)]
                psf = [k.ps(sb_, [128, 512], F32) for _ in range(6)]
                pc = 0
                for cg in range(4):
                    wo = wos[cg % 2]
                    g.dma("pool", wo[:], w_out[:, cg * 512:(cg + 1) * 512].rearrange("(kc p) n -> p kc n", p=128), writes=wo.d)
                    for ti, (t0, n) in enumerate(TT):
                        xr = xr_[pc % 3]
                        so_ = so[pc % 3]
                        p = psf[pc % 6]
                        pc += 1
                        src = xp[t0:t0 + n, cg * 512:(cg + 1) * 512] if ti < 16 else xs[:, cg * 512:(cg + 1) * 512]
                        g.dma("sp", xr[0:n, :], src, writes=xr.d)
                        for kc in range(16):
                            g.op("pe", lambda e, p=p, kc=kc, t0=t0, n=n, wo=wo: e.matmul(
                                p[0:n, :], lhsT=mT[:, kc, t0:t0 + n], rhs=wo[:, kc, :], start=(kc == 0), stop=(kc == 15)),
                                reads=mT.d + wo.d, writes=p.d)
                        g.op("dve", lambda e, p=p, xr=xr, so_=so_, n=n: e.tensor_tensor(
                            out=so_[0:n, :], in0=p[0:n, :], in1=xr[0:n, :], op=ALU.add), reads=p.d + xr.d, writes=so_.d)
                        g.dma("sp", x2_d[t0:t0 + n, cg * 512:(cg + 1) * 512], so_[0:n, :], reads=so_.d)
                g.flush()

        chk(4)
        with contextlib.ExitStack() as s4:
            h2T = k.sb(s4, [128, 16, NT], BF16, name="h2T")
            norm_phase(h2T, lambda ti, t0, n: x2_d[t0:t0 + n, :], norm_ffn)
            with contextlib.ExitStack() as sb_:
                wsl = [k.sb(sb_, [128, 16, 128], BF16, name="wsl") for _ in range(4)]
                psf = [k.ps(sb_, [128, 512], F32) for _ in range(6)]
                pss = k.ps(sb_, [128, 512], F32)
                frow = k.sb(sb_, [4, DFF], F32, name="frow")
                g.dma("sp", frow[:], ffn_rows, writes=frow.d)
                fsr = k.sb(sb_, [32, DFF], F32, name="fsr")
                g.dma("sp", fsr[:], st_fconv, writes=fsr.d)
                fp_ = k.sb(sb_, [128, 48, 4], F32, name="fp")
                for c in range(48):
                    g.op("pe", lambda e, c=c: e.transpose(out=pss[:, c * 4:c * 4 + 4], in_=frow[0:4, c * 128:(c + 1) * 128],
                                                        identity=ident_f[0:4, 0:4]), reads=frow.d + ident_f.d, writes=pss.d)
                cp("dve", fp_[:], pss[:, 0:192].rearrange("p (c j) -> p c j", j=4), pss.d, fp_.d)
                gpad = k.sb(sb_, [128, 2 + T], F32, nd=5, name="gpad")
                gsm = k.sb(sb_, [128, 16, 6], F32, name="gsm")
                uc = k.sb(sb_, [128, NT], F32, name="uc")
                ug = k.sb(sb_, [128, NT], BF16, name="ug")
                at = [k.sb(sb_, [128, NT], BF16, name="at") for _ in range(2)]
                fst2 = k.sb(sb_, [2, 128], F32, name="fst2")
                fst = k.sb(sb_, [16, 2, 128], F32, name="fst")
                g.op("dve", lambda e: e.memset(gpad[:, 0:2], 0.0), writes=[gpad.d[4]])
                pc = 0
                ec = 0

                def v3(ap, t):
                    return ap.rearrange("p (s t) -> p s t", t=t)

                for c in range(48):
                    wg = wsl[(2 * c) % 4]
                    wu = wsl[(2 * c + 1) % 4]
                    g.dma("pool", wg[:], w_gate[:, c * 128:(c + 1) * 128].rearrange("(kc p) n -> p kc n", p=128), writes=wg.d)
                    g.dma("pool", wu[:], w_up[:, c * 128:(c + 1) * 128].rearrange("(kc p) n -> p kc n", p=128), writes=wu.d)
                    g.op("pe", lambda e, c=c: e.transpose(out=pss[:, 256:288], in_=fsr[0:32, c * 128:(c + 1) * 128],
                                                        identity=ident_f[0:32, 0:32]), reads=fsr.d + ident_f.d, writes=pss.d)
                    cp("dve", gsm[:, :, 0:2], v3(pss[:, 256:288], 2), pss.d, gsm.d)
                    for gi_, (t0, n) in enumerate(TG):
                        p = psf[pc % 6]
                        pc += 1
                        for kc in range(16):
                            g.op("pe", lambda e, p=p, kc=kc, t0=t0, n=n, wg=wg: e.matmul(
                                p[:, 0:n], lhsT=wg[:, kc, :], rhs=h2T[:, kc, t0:t0 + n], start=(kc == 0), stop=(kc == 15)),
                                reads=h2T.d + wg.d, writes=p.d)
                        if t0 < T:
                            ec += 1
                            cp("act" if ec % 2 else "dve", gpad[:, 2 + t0:2 + t0 + n], p[:, 0:n], p.d, [gpad.d[gi_]])
                        else:
                            cp("dve", gsm[:, :, 2:6], v3(p[:, 0:NS], 4), p.d, gsm.d)
                    p = psf[pc % 6]
                    pc += 1
                    for kc in range(16):
                        g.op("pe", lambda e, p=p, kc=kc, wg=wg: e.matmul(p[0:2, 0:128], lhsT=h2T[:, kc, T - 2:T], rhs=wg[:, kc, :],
                                                                       start=(kc == 0), stop=(kc == 15)), reads=h2T.d + wg.d, writes=p.d)
                    for j in range(2):
                        for kc in range(16):
                            g.op("pe", lambda e, p=p, kc=kc, j=j, wg=wg: e.matmul(
                                p[0:16, 128 + j * 128:256 + j * 128], lhsT=h2T[:, kc, T + 2 + j:NT:4], rhs=wg[:, kc, :],
                                start=(kc == 0), stop=(kc == 15)), reads=h2T.d + wg.d, writes=p.d)
                    cp("dve", fst2[:], p[0:2, 0:128], p.d, fst2.d)
                    cp("dve", fst[:], p[0:16, 128:384].rearrange("p (j c) -> p j c", c=128), p.d, fst.d)
                    g.dma("sp", o_pfconv[:, c * 128:(c + 1) * 128], fst2[:], reads=fst2.d)
                    g.dma("sp", o_sfconv[:, :, c * 128:(c + 1) * 128], fst[:], reads=fst.d)
                    g.op("dve", lambda e, c=c: e.tensor_scalar(out=uc[:, 0:T], in0=gpad[:, 2:2 + T], scalar1=fp_[:, c, 2:3], scalar2=fp_[:, c, 3:4],
                                                             op0=ALU.mult, op1=ALU.add), reads=gpad.d + fp_.d, writes=uc.d)
                    g.op("dve", lambda e, c=c: e.tensor_scalar(out=v3(uc[:, T:NT], 4), in0=gsm[:, :, 2:6], scalar1=fp_[:, c, 2:3], scalar2=fp_[:, c, 3:4],
                                                             op0=ALU.mult, op1=ALU.add), reads=gsm.d + fp_.d, writes=uc.d)
                    for j in range(2):
                        g.op("dve", lambda e, c=c, j=j: e.scalar_tensor_tensor(
                            out=uc[:, 0:T], in0=gpad[:, j:j + T], scalar=fp_[:, c, j:j + 1], in1=uc[:, 0:T], op0=ALU.mult, op1=ALU.add),
                            reads=gpad.d + fp_.d + uc.d, writes=uc.d)
                        g.op("dve", lambda e, c=c, j=j: e.scalar_tensor_tensor(
                            out=v3(uc[:, T:NT], 4), in0=gsm[:, :, j:j + 4], scalar=fp_[:, c, j:j + 1], in1=v3(uc[:, T:NT], 4), op0=ALU.mult, op1=ALU.add),
                            reads=gsm.d + fp_.d + uc.d, writes=uc.d)
                    g.op("act", lambda e: e.activation(out=ug[:], in_=uc[:], func=AF.Gelu_apprx_tanh), reads=uc.d, writes=ug.d)
                    a_t = at[c % 2]
                    for (t0, n) in TG:
                        p = psf[pc % 6]
                        pc += 1
                        for kc in range(16):
                            g.op("pe", lambda e, p=p, kc=kc, t0=t0, n=n, wu=wu: e.matmul(
                                p[:, 0:n], lhsT=wu[:, kc, :], rhs=h2T[:, kc, t0:t0 + n], start=(kc == 0), stop=(kc == 15)),
                                reads=h2T.d + wu.d, writes=p.d)
                        g.op("dve", lambda e, p=p, t0=t0, n=n, a_t=a_t: e.tensor_tensor(
                            out=a_t[:, t0:t0 + n], in0=p[:, 0:n], in1=ug[:, t0:t0 + n], op=ALU.mult), reads=p.d + ug.d, writes=a_t.d)
                    g.dma("sp", act_d[c], a_t[:], reads=a_t.d)
                g.flush()
        chk(5)
        with contextlib.ExitStack() as s5:
            wds = [k.sb(s5, [128, 48, 512], BF16, name="wds") for _ in range(2)]
            ats = [k.sb(s5, [128, 48, 128], BF16, name="ats") for _ in range(2)]
            xr_ = [k.sb(s5, [128, 512], F32, name="xres") for _ in range(3)]
            so = [k.sb(s5, [128, 512], F32, name="so") for _ in range(3)]
            psf = [k.ps(s5, [128, 512], F32) for _ in range(6)]
            pc = 0
            for cg in range(4):
                wd = wds[cg % 2]
                for q in range(4):
                    g.dma("pool", wd[:, q * 12:(q + 1) * 12, :],
                          w_down[q * 1536:(q + 1) * 1536, cg * 512:(cg + 1) * 512].rearrange("(kc p) n -> p kc n", p=128), writes=wd.d)
                for ti, (t0, n) in enumerate(TT):
                    xr = xr_[pc % 3]
                    so_ = so[pc % 3]
                    a_s = ats[pc % 2]
                    p = psf[pc % 6]
                    pc += 1
                    g.dma("sp", xr[0:n, :], x2_d[t0:t0 + n, cg * 512:(cg + 1) * 512], writes=xr.d)
                    g.dma("sp", a_s[:, :, 0:n], act_d[:, :, t0:t0 + n].rearrange("c p t -> p c t"), writes=a_s.d)
                    for kc in range(48):
                        g.op("pe", lambda e, p=p, kc=kc, n=n, wd=wd, a_s=a_s: e.matmul(
                            p[0:n, :], lhsT=a_s[:, kc, 0:n], rhs=wd[:, kc, :], start=(kc == 0), stop=(kc == 47)),
                            reads=a_s.d + wd.d, writes=p.d)
                    g.op("dve", lambda e, p=p, xr=xr, so_=so_, n=n: e.tensor_tensor(
                        out=so_[0:n, :], in0=p[0:n, :], in1=xr[0:n, :], op=ALU.add), reads=p.d + xr.d, writes=so_.d)
                    g.dma("sp", x3_d[t0:t0 + n, cg * 512:(cg + 1) * 512], so_[0:n, :], reads=so_.d)
            g.flush()
        chk(6)
        with contextlib.ExitStack() as s6:
            gbc = k.sb(s6, [128, D], F32, name="gbc")
            g.dma("sp", gbc[:], norm_final.partition_broadcast(128), writes=gbc.d)
            xt = [k.sb(s6, [128, D], F32, name="xt") for _ in range(2)]
            yo = [k.sb(s6, [128, D], F32, name="yo") for _ in range(2)]
            junk = k.sb(s6, [128, D], BF16, name="junk")
            ss = k.sb(s6, [128, 17], F32, name="ss")
            rstd = k.sb(s6, [128, 17], F32, name="rstd")
            for ti, (t0, n) in enumerate(TT):
                x_t, y_o = xt[ti % 2], yo[ti % 2]
                g.dma("sp", x_t[0:n, :], x3_d[t0:t0 + n, :], writes=x_t.d)
                row_stats(s6, x_t, n, ss, rstd, junk, ti)
                g.op("dve", lambda e, x_t=x_t, y_o=y_o, n=n, ti=ti: e.scalar_tensor_tensor(
                    out=y_o[0:n, :], in0=x_t[0:n, :], scalar=rstd[0:n, ti:ti + 1], in1=gbc[0:n, :],
                    op0=ALU.mult, op1=ALU.mult), reads=x_t.d + rstd.d + gbc.d, writes=y_o.d)
                g.dma("sp", (y_p[t0:t0 + n, :] if ti < 16 else y_s[:, :]), y_o[0:n, :], reads=y_o.d)
            g.flush()
    except Stop:
        pass
    return nc, ins, outs


_st = 16 * np.arange(127)[:, None]
_b0 = 64 * np.arange(32)[None, :]
M_C2S = ((_st < _b0 + 64) & (_st + 32 > _b0)).astype(np.float32)


def make_in_maps(inputs):
    f = lambda a: np.ascontiguousarray(a, dtype=np.float32)
    x_prompt = f(inputs["x_prompt"])
    x_sample = f(inputs["x_sample"])
    rnn_rows = f(np.concatenate([inputs["rnn_conv_w"][0], inputs["rnn_conv_b"], inputs["rnn_ba"], inputs["rnn_bx"],
                                 inputs["rnn_lambda"]], axis=0))
    ffn_rows = f(np.concatenate([inputs["ffn_conv_w"][0], inputs["ffn_conv_b"]], axis=0))
    shared = {
        "ident": np.eye(128, dtype=np.float32),
        "norm_mix": f(inputs["norm_mix"]).reshape(1, D),
        "w_in": f(inputs["w_in"][0]),
        "rnn_rows": rnn_rows,
        "rnn_wa": f(inputs["rnn_wa"][0]),
        "rnn_wx": f(inputs["rnn_wx"][0]),
        "w_proj_rnn": f(inputs["w_proj_rnn"][0]),
        "w_proj_att": f(inputs["w_proj_att"][0]),
        "w_out": f(inputs["w_out"][0]),
        "norm_ffn": f(inputs["norm_ffn"]).reshape(1, D),
        "ffn_w_gate": f(inputs["ffn_w_gate"][0]),
        "ffn_w_up": f(inputs["ffn_w_up"][0]),
        "ffn_rows": ffn_rows,
        "ffn_w_down": f(inputs["ffn_w_down"][0]),
        "norm_final": f(inputs["norm_final"]).reshape(1, D),
        "cmp_w": f(inputs["cmp_w"][0]),
        "cmp_pos": f(inputs["cmp_pos"][0]),
        "m_c2s": M_C2S,
    }
    in_maps = []
    for c in range(NCORES):
        m = dict(shared)
        m["xp"] = x_prompt[c]
        m["xs"] = x_sample[16 * c:16 * c + 16].reshape(NS, D)
        m["st_rconv"] = f(inputs["state_rnn_conv"][0, 16 * c:16 * c + 16]).reshape(48, DRNN)
        m["st_rh"] = f(inputs["state_rnn_h"][0, 16 * c:16 * c + 16])
        m["st_fconv"] = f(inputs["state_ffn_conv"][0, 16 * c:16 * c + 16]).reshape(32, DFF)
        if "page_table" in inputs:
            m["ccmp"] = f(inputs["cache_kv_cmp"][0]).reshape(2560 * 128, 512)
            m["cslc"] = f(inputs["cache_kv_slc"][0]).reshape(2560 * 128, 512)
            m["ptab"] = np.ascontiguousarray(inputs["page_table"][16 * c:16 * c + 16], dtype=np.int32).reshape(1, 256)
        if "cache_kv_win" in inputs:
            m["cwin"] = f(inputs["cache_kv_win"][0, 16 * c:16 * c + 16]).reshape(16, 512, 512)
        in_maps.append(m)
    return in_maps


def kernel(**inputs):
    flags = {}
    nc, ins, outs = build(flags)
    in_maps = [{k_: v for k_, v in m.items() if k_ in ins} for m in make_in_maps(inputs)]
    res = run_bass_kernel_spmd(nc, in_maps, core_ids=list(range(NCORES)))
    R = res.results

    def cat(name, shape):
        return np.stack([np.asarray(R[c][name], dtype=np.float32).reshape(shape) for c in range(NCORES)], axis=0)

    y_prompt = cat("y_p", (T, D))
    y_sample = cat("y_s", (16, 4, D)).reshape(128, 4, D)
    pkv = cat("o_pkv", (3, T, 2, 2, 128))
    skv = cat("o_skv", (3, 16, 4, 2, 2, 128))
    p_kv_cmp = pkv[:, 0][None]
    p_kv_slc = pkv[:, 1][None]
    p_kv_win = np.ascontiguousarray(pkv[:, 2, T - 512:])[None]
    s_kv_cmp = skv[:, 0].reshape(128, 4, 2, 2, 128)[None]
    s_kv_slc = skv[:, 1].reshape(128, 4, 2, 2, 128)[None]
    s_kv_win = cat("o_swin", (16, 512, 2, 2, 128)).reshape(128, 512, 2, 2, 128)[None]
    p_rnn_h = cat("o_prh", (DRNN,))[None]
    p_rnn_conv = cat("o_prconv", (3, DRNN))[None]
    p_ffn_conv = cat("o_pfconv", (2, DFF))[None]
    s_rnn_h = cat("o_srh", (16, DRNN)).reshape(128, DRNN)[None]
    s_rnn_conv = cat("o_srconv", (16, 3, DRNN)).reshape(128, 3, DRNN)[None]
    s_ffn_conv = cat("o_sfconv", (16, 2, DFF)).reshape(128, 2, DFF)[None]
    outs_ = (y_prompt, y_sample, p_kv_cmp, p_kv_slc, p_kv_win, p_rnn_h, p_rnn_conv, p_ffn_conv,
             s_kv_cmp, s_kv_slc, s_kv_win, s_rnn_h, s_rnn_conv, s_ffn_conv)
    return tuple(np.ascontiguousarray(o, dtype=np.float32) for o in outs_)
```
